# Optimizing a Trainium2 kernel written in Bass

```python
import math
import jax, jax.numpy as jnp
from jax import lax
import numpy as np

D_MODEL = 2048
BATCH = 2
SEQ = 16384
DEPTH = 1
DEC_BATCH = 4
DEC_SEQ = 4096
PAST_LEN = 128

GRID_W = 64
NA_HEADS = 8
NA_HD = 128
NA_WIN_ROWS = 8
NA_WIN_COLS = 16
DIFF_HEADS = 8
DIFF_DK = 64
DIFF_DV = 2 * DIFF_DK
Q_BLOCK = 128
ROPE_THETA = 10000.0
MEM_TOKENS = 256
MEM_HEADS = 4
MEM_HD = D_MODEL // MEM_HEADS
D_FF = ((8 * D_MODEL // 3 + 255) // 256) * 256
NA_W = NA_HEADS * NA_HD
DIFF_QK_W = DIFF_HEADS * 2 * DIFF_DK
DIFF_V_W = DIFF_HEADS * DIFF_DV
MIX_W = NA_W + DIFF_V_W
IN_W = 3 * NA_W + 2 * DIFF_QK_W + DIFF_V_W
RMS_EPS = 1e-6
SUBLN_EPS = 1e-5

kernel_name = "hymba_na_diffattn_encoder"


def rms_norm(x, g, eps=RMS_EPS):
    xf = x.astype(jnp.float32)
    y = xf * lax.rsqrt(jnp.mean(xf * xf, axis=-1, keepdims=True) + eps)
    return (y * g.astype(jnp.float32)).astype(x.dtype)


def rotary_tables(T):
    inv = 1.0 / (ROPE_THETA ** (jnp.arange(0, DIFF_DK, 2, dtype=jnp.float32) / DIFF_DK))
    ang = jnp.arange(T, dtype=jnp.float32)[:, None] * inv[None, :]
    ang = jnp.concatenate([ang, ang], axis=-1)
    return jnp.cos(ang), jnp.sin(ang)


def apply_rope(x, cos, sin):
    c = cos[None, :, None, None, :]
    s = sin[None, :, None, None, :]
    xf = x.astype(jnp.float32)
    x1, x2 = jnp.split(xf, 2, axis=-1)
    rot = jnp.concatenate([-x2, x1], axis=-1)
    return (xf * c + rot * s).astype(x.dtype)


def neighbourhood_attention(q, k, v, rpb):
    B, T = q.shape[0], q.shape[1]
    rows = T // GRID_W
    wr = min(NA_WIN_ROWS, rows)
    q4 = q.reshape(B, rows, GRID_W, NA_HEADS, NA_HD)
    k4 = k.reshape(B, rows, GRID_W, NA_HEADS, NA_HD)
    v4 = v.reshape(B, rows, GRID_W, NA_HEADS, NA_HD)
    col_start = np.clip(np.arange(GRID_W) - NA_WIN_COLS // 2, 0, GRID_W - NA_WIN_COLS)
    col_idx = col_start[:, None] + np.arange(NA_WIN_COLS)[None, :]
    dc_idx = col_idx - np.arange(GRID_W)[:, None] + (NA_WIN_COLS - 1)
    scale = NA_HD ** -0.5

    def row_block(r):
        rs = jnp.clip(r - wr // 2, 0, rows - wr)
        q_r = lax.dynamic_index_in_dim(q4, r, axis=1, keepdims=False)
        k_rows = lax.dynamic_slice_in_dim(k4, rs, wr, axis=1)
        v_rows = lax.dynamic_slice_in_dim(v4, rs, wr, axis=1)
        k_win = k_rows[:, :, col_idx]
        v_win = v_rows[:, :, col_idx]
        dr_idx = rs + jnp.arange(wr) - r + (NA_WIN_ROWS - 1)
        bias = rpb[:, dr_idx[None, :, None], dc_idx[:, None, :]]
        s = jnp.einsum('bchd,bicjhd->bhcij', q_r, k_win).astype(jnp.float32) * scale
        s = s + bias.astype(jnp.float32)[None]
        p = jax.nn.softmax(s.reshape(B, NA_HEADS, GRID_W, wr * NA_WIN_COLS), axis=-1)
        p = p.reshape(B, NA_HEADS, GRID_W, wr, NA_WIN_COLS).astype(v.dtype)
        return jnp.einsum('bhcij,bicjhd->bchd', p, v_win)

    out = lax.map(row_block, jnp.arange(rows, dtype=jnp.int32))
    return out.transpose(1, 0, 2, 3, 4).reshape(B, T, NA_W)


def differential_attention(q, k, v, lam, g_subln, lam_init):
    B, T = q.shape[0], q.shape[1]
    nblk = T // Q_BLOCK
    scale = DIFF_DK ** -0.5
    qb = q.reshape(B, nblk, Q_BLOCK, DIFF_HEADS, 2, DIFF_DK).transpose(1, 0, 2, 3, 4, 5)

    def q_block(qi):
        s = jnp.einsum('bqhsd,bkhsd->bhsqk', qi, k).astype(jnp.float32) * scale
        p = jax.nn.softmax(s, axis=-1)
        a = (p[:, :, 0] - lam * p[:, :, 1]).astype(v.dtype)
        return jnp.einsum('bhqk,bkhd->bqhd', a, v)

    o = lax.map(q_block, qb).transpose(1, 0, 2, 3, 4).reshape(B, T, DIFF_HEADS, DIFF_DV)
    o = rms_norm(o, g_subln, eps=SUBLN_EPS) * (1.0 - lam_init)
    return o.reshape(B, T, DIFF_V_W)


def memory_cross_attention(h, mem, g_x, g_mem, w_mq, w_mkv, w_mo):
    B, T = h.shape[0], h.shape[1]
    M = mem.shape[1]
    hn = rms_norm(h, g_x)
    mn = rms_norm(mem, g_mem)
    q = (hn @ w_mq).reshape(B, T, MEM_HEADS, MEM_HD)
    kv = mn @ w_mkv
    k = kv[..., :D_MODEL].reshape(B, M, MEM_HEADS, MEM_HD)
    v = kv[..., D_MODEL:].reshape(B, M, MEM_HEADS, MEM_HD)
    s = jnp.einsum('bqhd,bkhd->bhqk', q, k).astype(jnp.float32) * (MEM_HD ** -0.5)
    p = jax.nn.softmax(s, axis=-1).astype(v.dtype)
    o = jnp.einsum('bhqk,bkhd->bqhd', p, v).reshape(B, T, D_MODEL)
    return o @ w_mo


def swiglu_ffn(h, g_ffn, w_gate_up, w_down):
    hn = rms_norm(h, g_ffn)
    gu = hn @ w_gate_up
    gate, up = gu[..., :D_FF], gu[..., D_FF:]
    return (jax.nn.silu(gate) * up) @ w_down


def encoder_trunk(x, mem, g_mix, w_in, rpb, lam_q1, lam_k1, lam_q2, lam_k2, g_subln, w_out,
                  g_xattn, g_mem, w_mq, w_mkv, w_mo, g_ffn, w_gate_up, w_down, g_final):
    B, T = x.shape[0], x.shape[1]
    cos, sin = rotary_tables(T)
    splits = [NA_W, 2 * NA_W, 3 * NA_W, 3 * NA_W + DIFF_QK_W, 3 * NA_W + 2 * DIFF_QK_W]
    for l in range(DEPTH):
        lam_init = 0.8 - 0.6 * math.exp(-0.3 * l)
        xn = rms_norm(x, g_mix[l])
        proj = xn @ w_in[l]
        na_q, na_k, na_v, df_q, df_k, df_v = jnp.split(proj, splits, axis=-1)
        na_o = neighbourhood_attention(
            na_q.reshape(B, T, NA_HEADS, NA_HD),
            na_k.reshape(B, T, NA_HEADS, NA_HD),
            na_v.reshape(B, T, NA_HEADS, NA_HD), rpb[l])
        df_q = apply_rope(df_q.reshape(B, T, DIFF_HEADS, 2, DIFF_DK), cos, sin)
        df_k = apply_rope(df_k.reshape(B, T, DIFF_HEADS, 2, DIFF_DK), cos, sin)
        lam = (jnp.exp(jnp.sum(lam_q1[l].astype(jnp.float32) * lam_k1[l].astype(jnp.float32)))
               - jnp.exp(jnp.sum(lam_q2[l].astype(jnp.float32) * lam_k2[l].astype(jnp.float32)))
               + lam_init)
        df_o = differential_attention(df_q, df_k, df_v.reshape(B, T, DIFF_HEADS, DIFF_DV),
                                      lam, g_subln[l], lam_init)
        x = x + jnp.concatenate([na_o, df_o], axis=-1) @ w_out[l]
        x = x + memory_cross_attention(x, mem, g_xattn[l], g_mem[l], w_mq[l], w_mkv[l], w_mo[l])
        x = x + swiglu_ffn(x, g_ffn[l], w_gate_up[l], w_down[l])
    return rms_norm(x, g_final)


def setup_inputs(seed: int = 0) -> dict:
    key = jax.random.key(seed)
    ks = jax.random.split(key, 24)
    f32 = jnp.float32

    def w(k, shape, fan_in):
        return jax.random.normal(k, shape, f32) * (fan_in ** -0.5)

    def gain(k, shape):
        return 1.0 + 0.01 * jax.random.normal(k, shape, f32)

    return {
        "x_prompt": jax.random.normal(ks[0], (BATCH, SEQ, D_MODEL), f32),
        "x_sample": jax.random.normal(ks[1], (DEC_BATCH, DEC_SEQ, D_MODEL), f32),
        "mem_prompt": jax.random.normal(ks[2], (BATCH, MEM_TOKENS, D_MODEL), f32),
        "mem_sample": jax.random.normal(ks[3], (DEC_BATCH, MEM_TOKENS, D_MODEL), f32),
        "g_mix": gain(ks[4], (DEPTH, D_MODEL)),
        "w_in": w(ks[5], (DEPTH, D_MODEL, IN_W), D_MODEL),
        "rpb": 0.1 * jax.random.normal(ks[6], (DEPTH, NA_HEADS, 2 * NA_WIN_ROWS - 1, 2 * NA_WIN_COLS - 1), f32),
        "lam_q1": 0.1 * jax.random.normal(ks[7], (DEPTH, DIFF_DK), f32),
        "lam_k1": 0.1 * jax.random.normal(ks[8], (DEPTH, DIFF_DK), f32),
        "lam_q2": 0.1 * jax.random.normal(ks[9], (DEPTH, DIFF_DK), f32),
        "lam_k2": 0.1 * jax.random.normal(ks[10], (DEPTH, DIFF_DK), f32),
        "g_subln": gain(ks[11], (DEPTH, DIFF_DV)),
        "w_out": w(ks[12], (DEPTH, MIX_W, D_MODEL), MIX_W),
        "g_xattn": gain(ks[13], (DEPTH, D_MODEL)),
        "g_mem": gain(ks[14], (DEPTH, D_MODEL)),
        "w_mq": w(ks[15], (DEPTH, D_MODEL, D_MODEL), D_MODEL),
        "w_mkv": w(ks[16], (DEPTH, D_MODEL, 2 * D_MODEL), D_MODEL),
        "w_mo": w(ks[17], (DEPTH, D_MODEL, D_MODEL), D_MODEL),
        "g_ffn": gain(ks[18], (DEPTH, D_MODEL)),
        "w_gate_up": w(ks[19], (DEPTH, D_MODEL, 2 * D_FF), D_MODEL),
        "w_down": w(ks[20], (DEPTH, D_FF, D_MODEL), D_FF),
        "g_final": gain(ks[21], (D_MODEL,)),
    }


def reference(x_prompt, x_sample, mem_prompt, mem_sample, g_mix, w_in, rpb, lam_q1, lam_k1,
              lam_q2, lam_k2, g_subln, w_out, g_xattn, g_mem, w_mq, w_mkv, w_mo, g_ffn,
              w_gate_up, w_down, g_final):
    y_prompt = encoder_trunk(x_prompt, mem_prompt, g_mix, w_in, rpb, lam_q1, lam_k1, lam_q2, lam_k2,
                             g_subln, w_out, g_xattn, g_mem, w_mq, w_mkv, w_mo, g_ffn,
                             w_gate_up, w_down, g_final)
    y_sample = encoder_trunk(x_sample, mem_sample, g_mix, w_in, rpb, lam_q1, lam_k1, lam_q2, lam_k2,
                             g_subln, w_out, g_xattn, g_mem, w_mq, w_mkv, w_mo, g_ffn,
                             w_gate_up, w_down, g_final)
    return (y_prompt, y_sample)
```

```python
import contextlib
import math

import numpy as np
import ml_dtypes

import concourse.bass as bass
import concourse.mybir as mybir
from concourse.bass_utils import run_bass_kernel_spmd

F32 = mybir.dt.float32
BF16 = mybir.dt.bfloat16
AF = mybir.ActivationFunctionType
ALU = mybir.AluOpType
AX = mybir.AxisListType

NEG = -30000.0
RMS_EPS = 1e-6
SUBLN_EPS = 1e-5
LAM_INIT = 0.8 - 0.6 * math.exp(-0.3 * 0)


class Cfg:
    def __init__(self, D=2048, DFF=5632, jobs=((16384, 4096), (4096, 2048)), debug=False,
                 phases=(0, 1, 2, 3)):
        self.D = D
        self.DFF = DFF
        self.jobs = tuple(jobs)
        self.debug = debug
        self.KC = D // 128
        self.MEMHD = D // 4
        self.MC = self.MEMHD // 128
        self.NFB = DFF // 512
        self.phases = tuple(phases)


class Eng:
    def __init__(self, nc, e, name, es):
        self.e = e
        self.name = name
        self.sem = es.enter_context(nc.semaphore("es_" + name))
        self.n = 0
        self.seen = {}

    def wait(self, *toks):
        best = {}

        def walk(ts):
            for t in ts:
                if t is None:
                    continue
                if isinstance(t, list):
                    walk(t)
                    continue
                sem, v = t
                if best.get(sem, 0) < v:
                    best[sem] = v
        walk(toks)
        for sem, v in best.items():
            if self.seen.get(sem, 0) >= v:
                continue
            self.e.wait_ge(sem, v)
            self.seen[sem] = v

    def mark(self, ins):
        self.n += 1
        ins.then_inc(self.sem, 1)
        return (self.sem, self.n)


class SemC:
    pool = {"ld": [], "st": []}
    nc = None
    es = None
    nalloc = 0

    @classmethod
    def get(cls, kind):
        if cls.pool[kind]:
            return cls.pool[kind].pop()
        s = SemC()
        s.sem = cls.es.enter_context(cls.nc.semaphore(f"dsem{cls.nalloc}"))
        cls.nalloc += 1
        s.cnt = 0
        return s


class Slot:
    def __init__(self, nc, es, name, shape, dt, kind="ld"):
        self.t = es.enter_context(nc.sbuf_tensor(name, shape, dt))
        self.sc = SemC.get(kind)
        es.callback(SemC.pool[kind].append, self.sc)
        self.busy = []

    @property
    def sem(self):
        return self.sc.sem

    @property
    def cnt(self):
        return self.sc.cnt

    @cnt.setter
    def cnt(self, v):
        self.sc.cnt = v

    def tok(self):
        return (self.sc.sem, self.sc.cnt)


class Bank:
    def __init__(self, ap):
        self.ap = ap
        self.free = None


class K:
    def __init__(self, nc, cfg, es):
        self.nc = nc
        self.cfg = cfg
        self.es = es
        SemC.pool = {"ld": [], "st": []}
        SemC.nc = nc
        SemC.es = es
        SemC.nalloc = 0
        self.PE = Eng(nc, nc.tensor, "pe", es)
        self.ACT = Eng(nc, nc.scalar, "act", es)
        self.DVE = Eng(nc, nc.vector, "dve", es)
        self.POOL = Eng(nc, nc.gpsimd, "pool", es)
        self.SP = Eng(nc, nc.sync, "sp", es)
        self.store_toks = []
        self.load_last = {}
        self.rr = 0

    def load(self, slot, dst, src, first=True, eng=None):
        q = eng or self.SP
        if first:
            q.wait(slot.busy)
            slot.busy = []
        q.e.dma_start(out=dst, in_=src).then_inc(slot.sem, 16)
        slot.cnt += 16
        self.load_last[slot.sem] = slot.cnt
        return slot.tok()

    def store(self, slot, dst, src, wait):
        q = self.POOL
        q.wait(wait)
        q.e.dma_start(out=dst, in_=src).then_inc(slot.sem, 16)
        slot.cnt += 16
        t = slot.tok()
        slot.busy = [t]
        self.last_store = t
        self.store_toks.append(t)
        return t

    def drain_stores(self, engines):
        last = {}
        for sem, v in self.store_toks:
            last[sem] = max(last.get(sem, 0), v)
        for e in engines:
            for sem, v in last.items():
                e.wait((sem, v))

    def barrier(self):
        toks = []
        A, V, P = self.ACT, self.DVE, self.POOL
        A.wait((A.sem, A.n))
        toks.append(A.mark(A.e.activation(out=self.dummy[:, 0:1], in_=self.c["eps_rms"][:], func=AF.Copy)))
        V.wait((V.sem, V.n))
        toks.append(V.mark(V.e.tensor_copy(out=self.dummy[:, 1:2], in_=self.c["eps_rms"][:])))
        P.wait((P.sem, P.n))
        toks.append(P.mark(P.e.tensor_copy(out=self.dummy[:, 2:3], in_=self.c["eps_rms"][:])))
        toks.append((self.PE.sem, self.PE.n))
        engines = (self.PE, A, V, P, self.SP)
        self.drain_stores(engines)
        for e in engines:
            e.wait(toks)
            for sem, v in self.load_last.items():
                e.wait((sem, v))

    def alt(self):
        self.rr ^= 1
        return self.ACT if self.rr else self.DVE


_UID = [0]


def sbuf(kx, name, shape, dt, es=None):
    _UID[0] += 1
    return (es or kx.es).enter_context(kx.nc.sbuf_tensor(f"{name}_{_UID[0]}", shape, dt))


W_SPECS = None


def weight_specs(cfg):
    D, DFF = cfg.D, cfg.DFF
    return [("w_in", D, 6144), ("w_out", 2048, D), ("w_mq", D, D), ("w_mkv", D, 2 * D),
            ("w_mo", D, D), ("w_gu", D, 2 * DFF), ("w_down", DFF, D)]


def build_program(cfg):
    nc = bass.Bass("TRN2", target_bir_lowering=False)
    D, DFF, KC = cfg.D, cfg.DFF, cfg.KC
    dbg = cfg.debug
    SK = "ExternalOutput" if dbg else "Internal"
    dram = {}

    def din(name, shape, dt=F32):
        dram[name] = nc.dram_tensor(name, list(shape), dt, kind="ExternalInput").ap()
        return dram[name]

    def dscr(name, shape, dt=BF16):
        dram[name] = nc.dram_tensor(name, list(shape), dt, kind=SK).ap()
        return dram[name]

    def dout(name, shape, dt=F32):
        dram[name] = nc.dram_tensor(name, list(shape), dt, kind="ExternalOutput").ap()
        return dram[name]

    NJ = len(cfg.jobs)
    for j, (T, OWN) in enumerate(cfg.jobs):
        din(f"x{j}", [T, D])
        din(f"xh{j}", [512, D])
        din(f"mem{j}", [256, D])
        din(f"cos{j}", [128, T])
        din(f"sin{j}", [128, T])
        din(f"nab{j}", [4, 128, 8, 640])
        dout(f"y{j}", [OWN, D])
        dscr(f"KT{j}", [8, 128, T])
        dscr(f"VH{j}", [8, 128, T // 128, 128])
        dscr(f"QT{j}", [8, 128, OWN])
        dscr(f"NAQT{j}", [8, 128, OWN])
        dscr(f"NAKT{j}", [8, 128, OWN + 512])
        dscr(f"NAV{j}", [OWN + 512, 1024])
        dscr(f"MIXT{j}", [16, 128, OWN])
    din("nabi", [128, 8, 640])
    for name, R, C in weight_specs(cfg):
        din(name, [R, C])
    for g in ("g_mix", "g_xattn", "g_mem", "g_ffn"):
        din(g, [128, KC])
    din("g_final", [128, D])
    din("g_subln", [128, 1])
    din("lamv", [128, 4, 64])
    din("ident", [128, 128], BF16)
    din("rotm", [128, 128], BF16)
    dscr("Wb_in", [12, 128, KC, 512])
    dscr("Wb_out", [D // 512, 128, 16, 512])
    dscr("Wb_mq", [D // 512, 128, KC, 512])
    dscr("Wb_mkv", [2 * D // 512, 128, KC, 512])
    dscr("Wb_mo", [D // 512, 128, KC, 512])
    dscr("Wb_gu", [2 * DFF // 512, 128, KC, 512])
    dscr("Wb_down", [DFF // 512, 128, 4, D])

    es = contextlib.ExitStack()
    with es:
        kx = K(nc, cfg, es)
        kx.dram = dram
        setup_consts(kx)
        if 0 in cfg.phases:
            phase0_weights(kx, ("w_in",))
        if 1 in cfg.phases:
            phase1(kx)
        if 2 in cfg.phases:
            phase2_na(kx)
            phase2_diff(kx)
        if 3 in cfg.phases:
            phase3(kx)
        finish(kx)
    return nc


def setup_consts(kx):
    nc, es, cfg = kx.nc, kx.es, kx.cfg
    c = kx.c = {}
    d = kx.dram
    KC = cfg.KC
    cs = Slot(nc, es, "cst", [128, 1], F32)
    kx.cslot = cs
    kx.dummy = sbuf(kx, "bar_dummy", [128, 4], F32)
    kx.ysem = SemC.get("st")
    kx.ysem2 = SemC.get("st")

    def ld(name, shape, dt, src):
        t = sbuf(kx, "c_" + name, shape, dt)
        kx.load(cs, t[:], src, first=False)
        c[name] = t
        return t

    ld("ident", [128, 128], BF16, d["ident"])
    ld("rotm", [128, 128], BF16, d["rotm"])
    for g in ("g_mix", "g_xattn", "g_mem", "g_ffn"):
        ld(g, [128, KC], F32, d[g])
    ld("g_subln", [128, 1], F32, d["g_subln"])
    ld("lamv", [128, 4, 64], F32, d["lamv"])
    ctok = cs.tok()
    kx.ctok = ctok
    P = kx.POOL
    ones_bf = sbuf(kx, "ones_bf", [128, 128], BF16)
    ones_f = sbuf(kx, "ones_f", [128, 128], F32)
    sel1 = sbuf(kx, "sel1", [64, 128], F32)
    sel2 = sbuf(kx, "sel2", [64, 128], F32)
    P.e.memset(ones_bf[:], 1.0)
    P.e.memset(ones_f[:], 1.0)
    P.e.memset(sel1[:], 0.0)
    t0 = P.mark(P.e.memset(sel2[:], 0.0))
    P.wait(t0)
    P.e.memset(sel1[0:1, :], 1.0)
    P.e.memset(sel2[32:33, :], 1.0)
    eps_rms = sbuf(kx, "eps_rms", [128, 1], F32)
    eps_sub = sbuf(kx, "eps_sub", [128, 1], F32)
    P.e.memset(eps_rms[:], RMS_EPS)
    t2 = P.mark(P.e.memset(eps_sub[:], SUBLN_EPS))
    c.update(ones_bf=ones_bf, ones_f=ones_f, sel1=sel1, sel2=sel2, eps_rms=eps_rms, eps_sub=eps_sub)
    V, A = kx.DVE, kx.ACT
    lt = sbuf(kx, "lam_t", [128, 2, 64], F32)
    ls = sbuf(kx, "lam_s", [128, 2], F32)
    le = sbuf(kx, "lam_e", [128, 2], F32)
    nlam = sbuf(kx, "nlam", [128, 1], F32)
    gsub = sbuf(kx, "gsub", [128, 1], F32)
    V.wait(ctok)
    lv = c["lamv"]
    V.e.tensor_tensor(out=lt[:, 0, :], in0=lv[:, 0, :], in1=lv[:, 1, :], op=ALU.mult)
    ta = V.mark(V.e.tensor_tensor(out=lt[:, 1, :], in0=lv[:, 2, :], in1=lv[:, 3, :], op=ALU.mult))
    V.e.wait_ge(V.sem, ta[1])
    tb = V.mark(V.e.reduce_sum(out=ls[:], in_=lt[:], axis=AX.X))
    A.wait(tb)
    tc = A.mark(A.e.activation(out=le[:], in_=ls[:], func=AF.Exp))
    V.wait(tc)
    td = V.mark(V.e.tensor_tensor(out=nlam[:], in0=le[:, 1:2], in1=le[:, 0:1], op=ALU.subtract))
    V.e.wait_ge(V.sem, td[1])
    te = V.mark(V.e.tensor_scalar_add(out=nlam[:], in0=nlam[:], scalar1=-LAM_INIT))
    tf = V.mark(V.e.tensor_scalar_mul(out=gsub[:], in0=c["g_subln"][:], scalar1=(1.0 - LAM_INIT)))
    c.update(nlam=nlam, gsub=gsub)
    kx.const_toks = [ctok, t2, tf, te]
    for e in (kx.PE, kx.ACT, kx.DVE, kx.POOL):
        e.wait(kx.const_toks)


def finish(kx):
    kx.drain_stores([kx.POOL])
    t = kx.POOL.mark(kx.POOL.e.memset(kx.cslot.t[:], 0.0))
    for e in (kx.SP, kx.ACT, kx.DVE, kx.PE):
        e.wait(t)


def phase0_weights(kx, names):
    nc, cfg, d = kx.nc, kx.cfg, kx.dram
    es = contextlib.ExitStack()
    with es:
        NS = 3
        fin = [Slot(nc, es, f"p0f{i}", [128, 4096], F32) for i in range(NS)]
        fout = [Slot(nc, es, f"p0b{i}", [128, 4096], BF16, "st") for i in range(NS)]
        it = 0
        for name, R, C in weight_specs(cfg):
            if name not in names:
                continue
            src = d[name]
            dstn = "Wb_" + name[2:]
            dst = d[dstn]
            for kc in range(R // 128):
                for c0 in range(0, C, 4096):
                    cw = min(4096, C - c0)
                    si, so = fin[it % NS], fout[it % NS]
                    lt = kx.load(si, si.t[:, 0:cw], src[kc * 128:(kc + 1) * 128, c0:c0 + cw])
                    E = kx.ACT if (it % 2) else kx.DVE
                    E.wait(lt, so.busy)
                    so.busy = []
                    if E is kx.ACT:
                        ins = E.e.activation(out=so.t[:, 0:cw], in_=si.t[:, 0:cw], func=AF.Copy)
                    else:
                        ins = E.e.tensor_copy(out=so.t[:, 0:cw], in_=si.t[:, 0:cw])
                    ct = E.mark(ins)
                    si.busy = [ct]
                    if name == "w_down":
                        fb, kcl = kc // 4, kc % 4
                        dap = dst[fb, :, kcl, :]
                        sap = so.t[:, 0:cw]
                    else:
                        nb0 = c0 // 512
                        nbn = cw // 512
                        dap = dst[nb0:nb0 + nbn, :, kc, :].rearrange("nb p c -> p nb c")
                        sap = so.t[:, 0:cw].rearrange("p (nb c) -> p nb c", c=512)
                    kx.store(so, dap, sap, ct)
                    it += 1
        kx.barrier()


class NormCtx:
    def __init__(self, kx, es, pb, nx=2, nhn=1):
        nc, cfg = kx.nc, kx.cfg
        D = cfg.D
        _UID[0] += 1
        self.xs = [Slot(nc, es, f"nx{i}_{_UID[0]}", [128, D], F32) for i in range(nx)]
        self.junk = sbuf(kx, "njunk", [128, D], BF16, es)
        self.hns = [sbuf(kx, f"nhn{i}", [128, 4, D], BF16, es) for i in range(nhn)]
        self.ss = sbuf(kx, "nss", [128, 8], F32, es)
        self.rstd = sbuf(kx, "nrstd", [128, 8], F32, es)
        self.pb = pb
        self.hn_free = [None] * nhn
        self.i = 0
        self.junk_tok = None


def rms_rstd(kx, src_ap, ss_ap, rstd_ap, junk_ap, width, in_toks, eps_ap=None):
    A, V = kx.ACT, kx.DVE
    A.wait(in_toks)
    ta = A.mark(A.e.activation(out=junk_ap, in_=src_ap, func=AF.Square, accum_out=ss_ap))
    A.wait(ta)
    tb = A.mark(A.e.activation(out=ss_ap, in_=ss_ap, func=AF.Sqrt, scale=1.0 / width,
                               bias=(eps_ap if eps_ap is not None else kx.c["eps_rms"][:])))
    V.wait(tb)
    tc = V.mark(V.e.reciprocal(out=rstd_ap, in_=ss_ap))
    return ta, tc


def norm_part(kx, ncx, tiles, ntt=4, hp=0):
    cfg = kx.cfg
    D = cfg.D
    A, V = kx.ACT, kx.DVE
    hn = ncx.hns[hp]
    hn_toks = []
    for tt in range(ntt):
        ap, in_toks, rel = tiles[tt]()
        col = (ncx.i % 8)
        ncx.i += 1
        A.wait(ncx.junk_tok)
        ta, tc = rms_rstd(kx, ap, ncx.ss[:, col:col + 1], ncx.rstd[:, col:col + 1], ncx.junk[:],
                          D, in_toks)
        ncx.junk_tok = ta
        V.wait(ncx.hn_free[hp], tc)
        th = V.mark(V.e.tensor_scalar(out=hn[:, tt, :], in0=ap, scalar1=ncx.rstd[:, col:col + 1],
                                      scalar2=None, op0=ALU.mult))
        hn_toks.append(th)
        rel([ta, th])
    ncx.hn_free[hp] = None
    return hn_toks


def trans_part(kx, ncx, hn_toks, gT, dstT, dst_free, ntt=4, hp=0):
    cfg = kx.cfg
    KC = cfg.KC
    A, PE = kx.ACT, kx.PE
    hn = ncx.hns[hp]
    out_toks = []
    last_pe = None
    for kc in range(KC):
        bk = ncx.pb[kc % len(ncx.pb)]
        PE.wait(hn_toks, bk.free)
        for tt in range(ntt):
            ins = PE.e.transpose(bk.ap[:, tt * 128:(tt + 1) * 128], hn[:, tt, kc * 128:(kc + 1) * 128],
                                 kx.c["ident"][:])
        tp = PE.mark(ins)
        last_pe = tp
        E = kx.alt()
        E.wait(tp, dst_free)
        if E is A:
            ins = E.e.activation(out=dstT[:, kc, 0:ntt * 128], in_=bk.ap[:, 0:ntt * 128], func=AF.Copy,
                                 scale=gT[:, kc:kc + 1])
        else:
            ins = E.e.tensor_scalar(out=dstT[:, kc, 0:ntt * 128], in0=bk.ap[:, 0:ntt * 128],
                                    scalar1=gT[:, kc:kc + 1], scalar2=None, op0=ALU.mult)
        te = E.mark(ins)
        bk.free = te
        out_toks.append(te)
    ncx.hn_free[hp] = last_pe
    return out_toks[-2:]


def make_xnT(kx, ncx, tiles, gT, dstT, dst_free, ntt=4):
    hn_toks = norm_part(kx, ncx, tiles, ntt, 0)
    return trans_part(kx, ncx, hn_toks, gT, dstT, dst_free, ntt, 0)


def phase1(kx):
    nc, cfg, d = kx.nc, kx.cfg, kx.dram
    D, KC = cfg.D, cfg.KC
    PE, A, V = kx.PE, kx.ACT, kx.DVE
    es = contextlib.ExitStack()
    with es:
        NB1 = 5
        ps = [es.enter_context(nc.psum_tensor(f"p1s{i}", [128, 512], F32)) for i in range(NB1)]
        pbt = [es.enter_context(nc.psum_tensor(f"p1b{i}", [128, 1024], BF16)) for i in range(3)]
        banks = [Bank(p[:]) for p in ps]
        pb = [Bank(pbt[i][:, 0:512]) for i in range(3)]
        ncx = NormCtx(kx, es, pb, nx=4, nhn=2)
        NW = 4
        wr = [Slot(nc, es, f"p1w{i}", [128, KC, 512], BF16) for i in range(NW)]
        xnTs = [sbuf(kx, f"p1xnT{i}", [128, KC, 512], BF16, es) for i in range(2)]
        cur = dict(x=None, ready=None)
        cs_ = [Slot(nc, es, f"p1cs{i}", [128, 2, 512], F32) for i in range(3)]
        fst = [Slot(nc, es, f"p1fst{i}", [128, 512], BF16, "st") for i in range(4)]
        tst = [Slot(nc, es, f"p1tst{i}", [128, 1024], BF16, "st") for i in range(3)]
        qraw = [sbuf(kx, f"p1qraw{i}", [128, 512], BF16, es) for i in range(2)]
        tm1 = [sbuf(kx, f"p1tm1{i}", [128, 512], F32, es) for i in range(2)]
        tm2 = [sbuf(kx, f"p1tm2{i}", [128, 512], F32, es) for i in range(2)]
        st = dict(bi=0, fi=0, ti=0, ri=0, xi=0, wi=0, ci=0)
        qraw_free = [None, None]
        tm_free = [None, None]
        xn_readers = [[], []]

        def next_bank():
            b = banks[st["bi"] % NB1]
            st["bi"] += 1
            return b

        def x_tiles(src, t0):
            def mk(tt):
                def f():
                    s = ncx.xs[st["xi"] % len(ncx.xs)]
                    st["xi"] += 1
                    lt = kx.load(s, s.t[:], src[t0 + tt * 128:t0 + (tt + 1) * 128, :])

                    def rel(toks):
                        s.busy = list(toks)
                    return s.t[:], [lt], rel
                return f
            return [mk(tt) for tt in range(4)]

        def do_norm(src, t0, par):
            return norm_part(kx, ncx, x_tiles(src, t0), 4, par)

        def do_trans(hn_toks, par):
            toks = trans_part(kx, ncx, hn_toks, kx.c["g_mix"], xnTs[par], xn_readers[par], 4, par)
            xn_readers[par] = []
            return toks

        def fm_proj(wslot, wtok, c):
            bk = next_bank()
            xnT = cur["x"]
            PE.wait(wtok, cur["ready"], bk.free)
            for kc in range(KC):
                ins = PE.e.matmul(bk.ap, lhsT=wslot.t[:, kc, c * 128:(c + 1) * 128], rhs=xnT[:, kc, :],
                                  start=(kc == 0), stop=(kc == KC - 1))
            tp = PE.mark(ins)
            return bk, tp

        def tm_proj(wslot, wtok, tt):
            bk = next_bank()
            xnT = cur["x"]
            PE.wait(wtok, cur["ready"], bk.free)
            for kc in range(KC):
                ins = PE.e.matmul(bk.ap, lhsT=xnT[:, kc, tt * 128:(tt + 1) * 128], rhs=wslot.t[:, kc, :],
                                  start=(kc == 0), stop=(kc == KC - 1))
            tp = PE.mark(ins)
            return bk, tp

        def plain_fm(wslot, wtok, c, dst_ap):
            bk, tp = fm_proj(wslot, wtok, c)
            s = fst[st["fi"] % len(fst)]
            st["fi"] += 1
            E = kx.alt()
            E.wait(tp, s.busy)
            s.busy = []
            if E is A:
                ins = E.e.activation(out=s.t[:], in_=bk.ap, func=AF.Copy)
            else:
                ins = E.e.tensor_copy(out=s.t[:], in_=bk.ap)
            te = E.mark(ins)
            bk.free = te
            kx.store(s, dst_ap, s.t[:], te)
            return tp

        rope_pend = []

        def rope_flush():
            while rope_pend:
                rope_pend.pop(0)()

        def rope_fm(wslot, wtok, c, cslot, ctok, dst_ap):
            bk, tp = fm_proj(wslot, wtok, c)
            r = st["ri"] % 2
            st["ri"] += 1
            A.wait(tp, qraw_free[r])
            ta = A.mark(A.e.activation(out=qraw[r][:], in_=bk.ap, func=AF.Copy))
            bk.free = ta
            res = {}

            def tail():
                bk2 = next_bank()
                PE.wait(ta, bk2.free)
                tr = PE.mark(PE.e.matmul(bk2.ap, lhsT=kx.c["rotm"][:], rhs=qraw[r][:], start=True, stop=True))
                V.wait(ta, ctok, tm_free[r])
                t1 = V.mark(V.e.tensor_tensor(out=tm1[r][:], in0=qraw[r][:], in1=cslot.t[:, 0, :], op=ALU.mult))
                V.wait(tr)
                t2 = V.mark(V.e.tensor_tensor(out=tm2[r][:], in0=bk2.ap, in1=cslot.t[:, 1, :], op=ALU.mult))
                bk2.free = t2
                qraw_free[r] = [tr, t1]
                s_ = fst[st["fi"] % len(fst)]
                st["fi"] += 1
                V.wait(t1, t2, s_.busy)
                s_.busy = []
                t3 = V.mark(V.e.tensor_tensor(out=s_.t[:], in0=tm1[r][:], in1=tm2[r][:], op=ALU.add))
                tm_free[r] = t3
                kx.store(s_, dst_ap, s_.t[:], t3)
                res["t3"] = t3

            prev = list(rope_pend)
            del rope_pend[:]
            for f in prev:
                f()
            rope_pend.append(tail)
            return tp, res

        def tm_out(wslots, wtoks, tt, dst_ap, head_major=None):
            s = tst[st["ti"] % len(tst)]
            st["ti"] += 1
            toks = []
            tps = []
            for cb in range(2):
                bk, tp = tm_proj(wslots[cb], wtoks[cb], tt)
                tps.append(tp)
                E = kx.alt()
                E.wait(tp, s.busy)
                if E is A:
                    ins = E.e.activation(out=s.t[:, cb * 512:(cb + 1) * 512], in_=bk.ap, func=AF.Copy)
                else:
                    ins = E.e.tensor_copy(out=s.t[:, cb * 512:(cb + 1) * 512], in_=bk.ap)
                te = E.mark(ins)
                bk.free = te
                toks.append(te)
            s.busy = []
            if head_major is not None:
                src = s.t[:].rearrange("p (h d) -> p h d", d=128)
            else:
                src = s.t[:]
            kx.store(s, dst_ap, src, toks)
            return tps

        def load_w(nb):
            s = wr[st["wi"] % NW]
            st["wi"] += 1
            t = kx.load(s, s.t[:], d["Wb_in"][nb])
            return s, t

        def load_cs(j, t0):
            s = cs_[st["ci"] % 3]
            st["ci"] += 1
            kx.load(s, s.t[:, 0, :], d[f"cos{j}"][:, t0:t0 + 512])
            t = kx.load(s, s.t[:, 1, :], d[f"sin{j}"][:, t0:t0 + 512], first=False)
            return s, t

        items = []
        for j, (T, OWN) in enumerate(cfg.jobs):
            for b in range(T // 512):
                items.append(("kv", j, b))
        for j, (T, OWN) in enumerate(cfg.jobs):
            for b in range(OWN // 512):
                items.append(("own", j, b))
            items.append(("halo", j, 0))

        def prepA(i):
            kind, j, b = items[i]
            if kind == "halo":
                return do_norm(d[f"xh{j}"], 0, i % 2), None
            return do_norm(d[f"x{j}"], b * 512, i % 2), load_cs(j, b * 512)

        def prepB(i, pa):
            hn_toks, csl = pa
            return do_trans(hn_toks, i % 2), csl

        kvw = [load_w(nb) for nb in (8, 9, 10, 11)]
        kv_last = [None]

        def run_item(i, ready, csl):
            kind, j, b = items[i]
            T, OWN = cfg.jobs[j]
            t0 = b * 512
            cur["x"] = xnTs[i % 2]
            cur["ready"] = ready
            last = None
            if kind == "kv":
                cslot, ctok = csl
                res = None
                for h in range(8):
                    ws, wt = kvw[h // 4]
                    tp, res = rope_fm(ws, wt, h % 4, cslot, ctok, d[f"KT{j}"][h, :, t0:t0 + 512])
                rope_flush()
                cslot.busy = [res["t3"]]
                for tt in range(4):
                    kt = (t0 // 128) + tt
                    tps = tm_out([kvw[2][0], kvw[3][0]], [kvw[2][1], kvw[3][1]], tt,
                                 d[f"VH{j}"][:, :, kt, :].rearrange("h p d -> p h d"), head_major=True)
                    last = tps[-1]
                kv_last[0] = last
                if i + 1 < len(items) and items[i + 1][0] != "kv":
                    for (s_, t_) in kvw:
                        s_.busy = [last]
                xn_readers[i % 2] = [last]
                return
            if kind == "own":
                cslot, ctok = csl
                for nb in (0, 1):
                    ws, wt = load_w(nb)
                    for c in range(4):
                        h = (nb % 2) * 4 + c
                        last = plain_fm(ws, wt, c, d[f"NAQT{j}"][h, :, t0:t0 + 512])
                    ws.busy = [last]
            for nb in (2, 3):
                ws, wt = load_w(nb)
                for c in range(4):
                    h = (nb % 2) * 4 + c
                    if kind == "own":
                        last = plain_fm(ws, wt, c, d[f"NAKT{j}"][h, :, 256 + t0:256 + t0 + 512])
                    else:
                        bk, tp = fm_proj(ws, wt, c)
                        s_ = fst[st["fi"] % len(fst)]
                        st["fi"] += 1
                        E = kx.alt()
                        E.wait(tp, s_.busy)
                        s_.busy = []
                        if E is A:
                            ins = E.e.activation(out=s_.t[:], in_=bk.ap, func=AF.Copy)
                        else:
                            ins = E.e.tensor_copy(out=s_.t[:], in_=bk.ap)
                        te = E.mark(ins)
                        bk.free = te
                        kx.store(s_, d[f"NAKT{j}"][h, :, 0:256], s_.t[:, 0:256], te)
                        kx.store(s_, d[f"NAKT{j}"][h, :, 256 + OWN:512 + OWN], s_.t[:, 256:512], te)
                        last = tp
                ws.busy = [last]
            wv = [load_w(nb) for nb in (4, 5)]
            for tt in range(4):
                if kind == "own":
                    r0 = 256 + t0 + tt * 128
                else:
                    r0 = tt * 128 if tt < 2 else 256 + OWN + (tt - 2) * 128
                tps = tm_out([wv[0][0], wv[1][0]], [wv[0][1], wv[1][1]], tt,
                             d[f"NAV{j}"][r0:r0 + 128, :])
                last = tps[-1]
            for (s_, t_) in wv:
                s_.busy = [last]
            if kind == "own":
                res = None
                for nb in (6, 7):
                    ws, wt = load_w(nb)
                    for c in range(4):
                        h = (nb % 2) * 4 + c
                        last, res = rope_fm(ws, wt, c, cslot, ctok, d[f"QT{j}"][h, :, t0:t0 + 512])
                    ws.busy = [last]
                rope_flush()
                cslot.busy = [res["t3"]]
            xn_readers[i % 2] = [last]

        n_it = len(items)
        pa = {0: prepA(0)}
        if n_it > 1:
            pa[1] = prepA(1)
        pbs = {0: prepB(0, pa.pop(0))}
        for i in range(n_it):
            if i + 1 < n_it:
                pbs[i + 1] = prepB(i + 1, pa.pop(i + 1))
            if i + 2 < n_it:
                pa[i + 2] = prepA(i + 2)
            ready, csl = pbs.pop(i)
            run_item(i, ready, csl)
        kx.barrier()


def phase2_na(kx):
    nc, cfg, d = kx.nc, kx.cfg, kx.dram
    PE, A, V = kx.PE, kx.ACT, kx.DVE
    scale = 128.0 ** -0.5
    es = contextlib.ExitStack()
    with es:
        psS = [Bank(es.enter_context(nc.psum_tensor(f"naS{i}", [128, 1024], F32))[:]) for i in range(2)]
        pbT = [Bank(es.enter_context(nc.psum_tensor(f"naT{i}", [128, 1024], BF16))[:]) for i in range(2)]
        psO = Bank(es.enter_context(nc.psum_tensor("naO", [128, 1024], F32))[:])
        tabI = Slot(nc, es, "naTabI", [128, 8, 640], F32)
        tabE = [Slot(nc, es, f"naTabE{i}", [128, 8, 640], F32) for i in range(2)]
        Qs = [Slot(nc, es, f"naQ{i}", [128, 8, 128], BF16) for i in range(2)]
        Ks = [Slot(nc, es, f"naK{i}", [128, 8, 640], BF16) for i in range(2)]
        Vs = [Slot(nc, es, f"naV{i}", [128, 5, 1024], BF16) for i in range(2)]
        sb = [sbuf(kx, f"naSb{i}", [128, 640], F32, es) for i in range(4)]
        sb_free = [None] * 4
        p = [sbuf(kx, f"naP{i}", [128, 8, 640], BF16, es) for i in range(2)]
        p_free = [None, None]
        ssum = sbuf(kx, "naSum", [128, 16], F32, es)
        rs = sbuf(kx, "naRs", [128, 16], F32, es)
        pT = [sbuf(kx, f"naPT{i}", [128, 5, 128], BF16, es) for i in range(2)]
        pT_free = [None, None]
        ost = [Slot(nc, es, f"naOst{i}", [128, 8, 128], BF16, "st") for i in range(2)]
        tI = kx.load(tabI, tabI.t[:], d["nabi"])
        tiles = []
        for j, (T, OWN) in enumerate(cfg.jobs):
            nt = OWN // 128
            for jt in range(nt):
                e = {0: 0, 1: 1, nt - 2: 2, nt - 1: 3}.get(jt)
                tiles.append((j, jt, e))
        state = {}
        cnt = dict(e=0, sbi=0)

        def stage1(i):
            j, jt, e = tiles[i]
            par = i % 2
            q, k, v = Qs[par], Ks[par], Vs[par]
            kx.load(q, q.t[:], d[f"NAQT{j}"][:, :, jt * 128:(jt + 1) * 128].rearrange("h p q -> p h q"))
            tq = kx.load(k, k.t[:], d[f"NAKT{j}"][:, :, jt * 128:jt * 128 + 640].rearrange("h p q -> p h q"))
            tk = k.tok()
            tq = q.tok()
            tv = kx.load(v, v.t[:], d[f"NAV{j}"][jt * 128:jt * 128 + 640, :].rearrange("(k p) c -> p k c", p=128))
            if e is None:
                tab, ttab = tabI, tI
            else:
                tab = tabE[cnt["e"] % 2]
                cnt["e"] += 1
                ttab = kx.load(tab, tab.t[:], d[f"nab{j}"][e])
            t4s = []
            lastS = None
            lastT1 = None
            for h in range(8):
                bk = psS[h % 2]
                PE.wait(tq, tk, bk.free)
                PE.e.matmul(bk.ap[:, 0:512], lhsT=q.t[:, h, :], rhs=k.t[:, h, 0:512], start=True, stop=True)
                tp = PE.mark(PE.e.matmul(bk.ap[:, 512:640], lhsT=q.t[:, h, :], rhs=k.t[:, h, 512:640],
                                         start=True, stop=True))
                lastS = tp
                si = cnt["sbi"] % 4
                cnt["sbi"] += 1
                V.wait(tp, ttab, sb_free[si])
                t1 = V.mark(V.e.scalar_tensor_tensor(out=sb[si][:], in0=bk.ap[:, 0:640], scalar=scale,
                                                     in1=tab.t[:, h, :], op0=ALU.mult, op1=ALU.add))
                bk.free = t1
                lastT1 = t1
                col = (i % 2) * 8 + h
                A.wait(t1, p_free[par])
                t2 = A.mark(A.e.activation(out=p[par][:, h, :], in_=sb[si][:], func=AF.Exp,
                                           accum_out=ssum[:, col:col + 1]))
                sb_free[si] = t2
                V.wait(t2)
                t3 = V.mark(V.e.reciprocal(out=rs[:, col:col + 1], in_=ssum[:, col:col + 1]))
                V.wait(t3)
                t4 = V.mark(V.e.tensor_scalar(out=p[par][:, h, :], in0=p[par][:, h, :],
                                              scalar1=rs[:, col:col + 1], scalar2=None, op0=ALU.mult))
                t4s.append(t4)
            p_free[par] = None
            q.busy = [lastS]
            k.busy = [lastS]
            if e is not None:
                tab.busy = [lastT1]
            state[i] = dict(t4s=t4s, tv=tv, v=v)

        def stage2(i):
            j, jt, e = tiles[i]
            par = i % 2
            stt = state.pop(i)
            v = stt["v"]
            last = None
            for h in range(8):
                bk = pbT[h % 2]
                PE.wait(stt["t4s"][h], bk.free)
                for k5 in range(5):
                    ins = PE.e.transpose(bk.ap[:, k5 * 128:(k5 + 1) * 128], p[par][:, h, k5 * 128:(k5 + 1) * 128],
                                         kx.c["ident"][:])
                tp2 = PE.mark(ins)
                E = kx.alt()
                E.wait(tp2, pT_free[h % 2])
                src = bk.ap[:, 0:640].rearrange("p (k q) -> p k q", q=128)
                if E is A:
                    ins = E.e.activation(out=pT[h % 2][:], in_=src, func=AF.Copy)
                else:
                    ins = E.e.tensor_copy(out=pT[h % 2][:], in_=src)
                t5 = E.mark(ins)
                bk.free = t5
                PE.wait(t5, stt["tv"], psO.free if h == 0 else None)
                for k5 in range(5):
                    ins = PE.e.matmul(psO.ap[:, h * 128:(h + 1) * 128], lhsT=v.t[:, k5, h * 128:(h + 1) * 128],
                                      rhs=pT[h % 2][:, k5, :], start=(k5 == 0), stop=(k5 == 4))
                tp3 = PE.mark(ins)
                pT_free[h % 2] = tp3
                last = tp3
            p_free[par] = last
            v.busy = [last]
            o = ost[i % 2]
            E = kx.alt()
            E.wait(last, o.busy)
            o.busy = []
            src = psO.ap.rearrange("p (h q) -> p h q", q=128)
            if E is A:
                ins = E.e.activation(out=o.t[:], in_=src, func=AF.Copy)
            else:
                ins = E.e.tensor_copy(out=o.t[:], in_=src)
            t6 = E.mark(ins)
            psO.free = t6
            kx.store(o, d[f"MIXT{j}"][0:8, :, jt * 128:(jt + 1) * 128].rearrange("h p q -> p h q"), o.t[:], t6)

        n = len(tiles)
        stage1(0)
        for i in range(n):
            if i + 1 < n:
                stage1(i + 1)
            stage2(i)
        kx.barrier()


def phase2_diff(kx):
    nc, cfg, d = kx.nc, kx.cfg, kx.dram
    PE, A, V = kx.PE, kx.ACT, kx.DVE
    scale = 64.0 ** -0.5
    c = kx.c
    Tmax = max(T for T, _ in cfg.jobs)
    es = contextlib.ExitStack()
    with es:
        psS = [Bank(es.enter_context(nc.psum_tensor(f"dfS{i}", [128, 1024], F32))[:]) for i in range(2)]
        psO1 = Bank(es.enter_context(nc.psum_tensor("dfO1", [128, 512], F32))[:])
        psO2 = Bank(es.enter_context(nc.psum_tensor("dfO2", [128, 512], F32))[:])
        psX = Bank(es.enter_context(nc.psum_tensor("dfX", [128, 512], F32))[:])
        KTs = [Slot(nc, es, f"dfK{i}", [128, Tmax], BF16) for i in range(2)]
        VHs = [Slot(nc, es, f"dfV{i}", [128, Tmax // 128, 128], BF16) for i in range(2)]
        QTs = [Slot(nc, es, f"dfQ{i}", [128, 512], BF16) for i in range(2)]
        NPT = 4
        PT = [sbuf(kx, f"dfPT{i}", [128, 1024], BF16, es) for i in range(NPT)]
        PT_pe = [None] * NPT
        PT_dve = [None] * NPT
        acc = [sbuf(kx, f"dfacc{i}", [128, 1024], F32, es) for i in range(2)]
        acc_free = [None, None]
        acc_tok = [None, None]
        tmpb = [sbuf(kx, f"dftmp{i}", [128, 1024], BF16, es) for i in range(3)]
        tmp_tok = [None, None, None]
        pair_tok = [None, None]
        o1 = sbuf(kx, "dfo1", [128, 512], F32, es)
        o2 = sbuf(kx, "dfo2", [128, 512], F32, es)
        tt_ = sbuf(kx, "dft", [128, 512], F32, es)
        uu = sbuf(kx, "dfu", [128, 512], F32, es)
        oo = sbuf(kx, "dfo", [128, 512], F32, es)
        sq = sbuf(kx, "dfsq", [128, 512], F32, es)
        rt = sbuf(kx, "dfrt", [128, 512], F32, es)
        ost = [Slot(nc, es, f"dfOst{i}", [128, 512], BF16, "st") for i in range(2)]

        heads = [(j, h) for j in range(len(cfg.jobs)) for h in range(8)]
        steps = []
        for hi, (j, h) in enumerate(heads):
            T, OWN = cfg.jobs[j]
            for qc in range(OWN // 512):
                for kt in range(T // 128):
                    steps.append((hi, j, h, qc, kt))
        N = len(steps)
        kv_tok = {}
        q_tok = {}
        cnt = dict(q=0, ost=0, pair=0, grp=0)

        PW = 2048
        P = kx.POOL
        wfin = [Slot(nc, es, f"dfwf{i}", [128, PW], F32, "st") for i in range(2)]
        wfout = [Slot(nc, es, f"dfwb{i}", [128, PW], BF16, "st") for i in range(2)]
        pieces = []
        if 0 in cfg.phases:
            for name, R, C in weight_specs(cfg):
                if name == "w_in":
                    continue
                for kc in range(R // 128):
                    for c0 in range(0, C, PW):
                        pieces.append((name, kc, c0, min(PW, C - c0)))
        wstate = dict(k=0, prev=None)

        def wcast_tick():
            k = wstate["k"]
            prev = wstate["prev"]
            if k < len(pieces):
                name, kc, c0, cw = pieces[k]
                si = wfin[k % 2]
                lt = kx.load(si, si.t[:, 0:cw], d[name][kc * 128:(kc + 1) * 128, c0:c0 + cw], eng=P)
                wstate["prev"] = (k, lt)
                wstate["k"] = k + 1
            else:
                wstate["prev"] = None
            if prev is not None:
                pk, plt = prev
                name, kc, c0, cw = pieces[pk]
                si, so = wfin[pk % 2], wfout[pk % 2]
                P.wait(plt, so.busy)
                so.busy = []
                ct = P.mark(P.e.tensor_copy(out=so.t[:, 0:cw], in_=si.t[:, 0:cw]))
                si.busy = [ct]
                dst = d["Wb_" + name[2:]]
                if name == "w_down":
                    fb_, kcl = kc // 4, kc % 4
                    dap = dst[fb_, :, kcl, c0:c0 + cw]
                    sap = so.t[:, 0:cw]
                else:
                    nb0, nbn = c0 // 512, cw // 512
                    dap = dst[nb0:nb0 + nbn, :, kc, :].rearrange("nb p c -> p nb c")
                    sap = so.t[:, 0:cw].rearrange("p (nb c) -> p nb c", c=512)
                kx.store(so, dap, sap, ct)

        def wcast_pending():
            return wstate["k"] < len(pieces) or wstate["prev"] is not None

        def load_kv(hi):
            j, h = heads[hi]
            T = cfg.jobs[j][0]
            ks, vs = KTs[hi % 2], VHs[hi % 2]
            NQ = 4 if T >= 2048 else 1
            w = T // NQ
            for i in range(NQ):
                kx.load(ks, ks.t[:, i * w:(i + 1) * w], d[f"KT{j}"][h, :, i * w:(i + 1) * w], first=(i == 0))
            for i in range(NQ):
                kx.load(vs, vs.t[:, i * (w // 128):(i + 1) * (w // 128), :],
                        d[f"VH{j}"][h, :, i * (w // 128):(i + 1) * (w // 128), :], first=(i == 0))
            kv_tok[hi] = (ks.tok(), vs.tok())

        def load_q(hi, qc):
            j, h = heads[hi]
            s = QTs[cnt["q"] % 2]
            cnt["q"] += 1
            t = kx.load(s, s.t[:], d[f"QT{j}"][h, :, qc * 512:(qc + 1) * 512])
            q_tok[(hi, qc)] = (s, t)

        tok_qk = {}
        tok_exp = {}
        pend = []
        ep_free = dict(o=None)
        Obanks_free = [None]

        def emit_qk(n):
            hi, j, h, qc, kt = steps[n]
            ks = KTs[hi % 2]
            s, tq = q_tok[(hi, qc)]
            bk = psS[n % 2]
            PE.wait(kv_tok[hi][0], tq, bk.free)
            PE.e.matmul(bk.ap[:, 0:512], lhsT=ks.t[0:64, kt * 128:(kt + 1) * 128], rhs=s.t[0:64, :],
                        start=True, stop=True)
            tok_qk[n] = PE.mark(PE.e.matmul(bk.ap[:, 512:1024], lhsT=ks.t[64:128, kt * 128:(kt + 1) * 128],
                                            rhs=s.t[64:128, :], start=True, stop=True))
            T = cfg.jobs[j][0]
            if kt == T // 128 - 1:
                s.busy = [tok_qk[n]]
                if qc == cfg.jobs[j][1] // 512 - 1:
                    ks.busy = [tok_qk[n]]

        def emit_exp(n):
            bk = psS[n % 2]
            A.wait(tok_qk.pop(n), PT_pe[n % NPT], PT_dve[n % NPT])
            tok_exp[n] = A.mark(A.e.activation(out=PT[n % NPT][:], in_=bk.ap, func=AF.Exp, scale=scale))
            bk.free = tok_exp[n]

        def emit_av(n):
            hi, j, h, qc, kt = steps[n]
            T, OWN = cfg.jobs[j]
            NT = T // 128
            vs = VHs[hi % 2]
            pt = PT[n % NPT]
            PE.wait(tok_exp[n], kv_tok[hi][1], Obanks_free[0] if kt == 0 else None)
            st_, sp_ = (kt == 0), (kt == NT - 1)
            PE.e.matmul(psO1.ap, lhsT=vs.t[:, kt, :], rhs=pt[:, 0:512], start=st_, stop=sp_)
            tp = PE.mark(PE.e.matmul(psO2.ap, lhsT=vs.t[:, kt, :], rhs=pt[:, 512:1024], start=st_, stop=sp_))
            PT_pe[n % NPT] = tp
            if sp_:
                while pend:
                    pend.pop(0)()
            tk0 = epilogue_s0(tp) if sp_ else None
            if kt % 2 == 1:
                g = cnt["grp"] % 2
                pi = (kt // 2) % 2
                V.wait(tok_exp[n - 1], tok_exp[n], tmp_tok[pi])
                t1 = V.mark(V.e.tensor_tensor(out=tmpb[pi][:], in0=PT[(n - 1) % NPT][:], in1=pt[:], op=ALU.add))
                PT_dve[(n - 1) % NPT] = t1
                PT_dve[n % NPT] = t1
                pair_tok[pi] = t1
                tok_exp.pop(n - 1, None)
                tok_exp.pop(n, None)
                if kt % 4 == 3:
                    V.wait(pair_tok[0], pair_tok[1], tmp_tok[2])
                    t3 = V.mark(V.e.tensor_tensor(out=tmpb[2][:], in0=tmpb[0][:], in1=tmpb[1][:], op=ALU.add))
                    tmp_tok[0] = t3
                    tmp_tok[1] = t3
                    V.wait(t3, acc_tok[g], acc_free[g] if kt == 3 else None)
                    if kt == 3:
                        t2 = V.mark(V.e.tensor_copy(out=acc[g][:], in_=tmpb[2][:]))
                    else:
                        t2 = V.mark(V.e.tensor_tensor(out=acc[g][:], in0=acc[g][:], in1=tmpb[2][:], op=ALU.add))
                    acc_tok[g] = t2
                    tmp_tok[2] = t2
            if sp_:
                if qc == OWN // 512 - 1:
                    vs.busy = [tp]
                g = cnt["grp"] % 2
                cnt["grp"] += 1
                epilogue(j, h, qc, tk0, g, acc_tok[g])

        def epilogue_s0(tp):
            tk = {}
            V.wait(tp, ep_free["o"])
            tk["e1"] = V.mark(V.e.tensor_copy(out=o1[:], in_=psO1.ap))
            A.wait(tp, ep_free["o"])
            tk["e2"] = A.mark(A.e.activation(out=o2[:], in_=psO2.ap, func=AF.Copy))
            Obanks_free[0] = [tk["e1"], tk["e2"]]
            return tk

        def epilogue(j, h, qc, tk, g, tacc):
            def k1():
                PE.wait(tacc, psX.free)
                tk["r1"] = PE.mark(PE.e.matmul(psX.ap, lhsT=c["ones_f"][:], rhs=acc[g][:, 0:512], start=True, stop=True))

            def k2():
                V.wait(tk["r1"])
                tk["rc1"] = V.mark(V.e.reciprocal(out=sq[:], in_=psX.ap))
                psX.free = tk["rc1"]
                V.wait(tk["rc1"], tk["e1"])
                tk["t"] = V.mark(V.e.tensor_tensor(out=tt_[:], in0=o1[:], in1=sq[:], op=ALU.mult))

            def k3():
                PE.wait(psX.free)
                tk["r2"] = PE.mark(PE.e.matmul(psX.ap, lhsT=c["ones_f"][:], rhs=acc[g][:, 512:1024], start=True, stop=True))
                acc_free[g] = tk["r2"]

            def k4():
                V.wait(tk["r2"], tk["t"])
                tk["rc2"] = V.mark(V.e.reciprocal(out=sq[:], in_=psX.ap))
                psX.free = tk["rc2"]
                V.wait(tk["rc2"], tk["e2"])
                tk["u"] = V.mark(V.e.tensor_tensor(out=uu[:], in0=o2[:], in1=sq[:], op=ALU.mult))
                V.wait(tk["u"], tk["t"])
                tk["o"] = V.mark(V.e.scalar_tensor_tensor(out=oo[:], in0=uu[:], scalar=c["nlam"][:, 0:1],
                                                          in1=tt_[:], op0=ALU.mult, op1=ALU.add))
                V.wait(tk["o"])
                tk["sq"] = V.mark(V.e.tensor_tensor(out=sq[:], in0=oo[:], in1=oo[:], op=ALU.mult))

            def k5():
                PE.wait(tk["sq"], psX.free)
                tk["ss"] = PE.mark(PE.e.matmul(psX.ap, lhsT=c["ones_f"][:], rhs=sq[:], start=True, stop=True))

            def k6():
                A.wait(tk["ss"])
                tk["ln"] = A.mark(A.e.activation(out=rt[:], in_=psX.ap, func=AF.Ln, scale=1.0 / 128,
                                                 bias=c["eps_sub"][:]))
                psX.free = tk["ln"]
                A.wait(tk["ln"])
                tk["rs"] = A.mark(A.e.activation(out=rt[:], in_=rt[:], func=AF.Exp, scale=-0.5))

            def k7():
                s = ost[cnt["ost"] % 2]
                cnt["ost"] += 1
                V.wait(tk["rs"], s.busy)
                s.busy = []
                tk["on"] = V.mark(V.e.scalar_tensor_tensor(out=s.t[:], in0=oo[:], scalar=c["gsub"][:, 0:1],
                                                           in1=rt[:], op0=ALU.mult, op1=ALU.mult))
                ep_free["o"] = tk["on"]
                kx.store(s, d[f"MIXT{j}"][8 + h, :, qc * 512:(qc + 1) * 512], s.t[:], tk["on"])

            pend.extend([k1, k2, k3, k4, k5, k6, k7])

        load_kv(0)
        load_q(0, 0)

        def prefetch_for(n):
            hi, j, h, qc, kt = steps[n]
            if kt == 0:
                nq = cfg.jobs[j][1] // 512
                if qc + 1 < nq:
                    load_q(hi, qc + 1)
                elif hi + 1 < len(heads):
                    load_q(hi + 1, 0)
                if qc == 0 and hi + 1 < len(heads):
                    load_kv(hi + 1)

        prefetch_for(0)
        emit_qk(0)
        if N > 1:
            emit_qk(1)
        for n in range(N):
            hi, j, h, qc, kt = steps[n]
            if n > 0:
                prefetch_for(n)
            emit_exp(n)
            if n + 2 < N:
                emit_qk(n + 2)
            emit_av(n)
            if pend and kt >= 1 and kt % 2 == 0:
                pend.pop(0)()
            if n % 4 == 1 and wcast_pending():
                wcast_tick()
        while wcast_pending():
            wcast_tick()
        while pend:
            pend.pop(0)()
        kx.barrier()


def phase3(kx):
    nc, cfg, d = kx.nc, kx.cfg, kx.dram
    D, KC, DFF, MC, NFB = cfg.D, cfg.KC, cfg.DFF, cfg.MC, cfg.NFB
    PE, A, V = kx.PE, kx.ACT, kx.DVE
    c = kx.c
    NCB = D // 512
    mscale = float(cfg.MEMHD) ** -0.5
    es = contextlib.ExitStack()
    with es:
        NB3 = 6
        banks = [Bank(es.enter_context(nc.psum_tensor(f"p3s{i}", [128, 512], F32))[:]) for i in range(NB3)]
        pbt = [es.enter_context(nc.psum_tensor(f"p3b{i}", [128, 1024], BF16)) for i in range(2)]
        pb = [Bank(pbt[0][:, 0:512]), Bank(pbt[1][:, 0:512])]
        ncx = NormCtx(kx, es, pb, nx=0)
        h = Slot(nc, es, "p3h", [128, 4, D], F32)
        fa = Slot(nc, es, "p3fa", [128, 16, 512], BF16)
        fb = sbuf(kx, "p3fb", [128, KC, 512], BF16, es)
        KmT = sbuf(kx, "p3KmT", [128, KC, 256], BF16, es)
        Vm = sbuf(kx, "p3Vm", [128, 2, D], BF16, es)
        PTm = [sbuf(kx, f"p3PTm{i}", [128, 2, 512], BF16, es) for i in range(2)]
        Rm = [sbuf(kx, f"p3Rm{i}", [128, 512], F32, es) for i in range(2)]
        actT = [sbuf(kx, f"p3act{i}", [128, 4, 512], BF16, es) for i in range(2)]
        sg = [sbuf(kx, f"p3sg{i}", [128, 512], F32, es) for i in range(2)]
        NW = 4
        wr = [Slot(nc, es, f"p3w{i}", [128, 8192], BF16) for i in range(NW)]
        gfin = Slot(nc, es, "p3gfin", [128, D], F32)
        ystat = sbuf(kx, "p3ystat", [128, 8], F32, es)
        yrstd = sbuf(kx, "p3yrstd", [128, 8], F32, es)
        tg = kx.load(gfin, gfin.t[:], d["g_final"])
        ysems = [kx.ysem, kx.ysem2]
        ys_tok = [None, None]
        st = dict(bi=0, wi=0, pi=0, ai=0, si=0, yi=0)
        htok = {}
        fa_free = [None]
        fb_free = [None]
        PTm_free = [None, None]
        Rm_free = [None, None]
        act_free = [None, None]
        sg_free = [None, None]

        def next_bank():
            b = banks[st["bi"] % NB3]
            st["bi"] += 1
            return b

        def lw(name, idx, kc_n, cols):
            s = wr[st["wi"] % NW]
            st["wi"] += 1
            view = s.t[:, 0:kc_n * cols].rearrange("p (k c) -> p k c", c=cols)
            t = kx.load(s, view, d[name][idx])
            return s, t, view

        def h_tiles(ntt, extra_toks=None):
            def mk(tt):
                def f():
                    toks = [htok.get((tt, cb)) for cb in range(NCB)]
                    if extra_toks:
                        toks = toks + list(extra_toks)

                    def rel(tk):
                        for cb in range(NCB):
                            htok[(tt, cb)] = list(tk)
                    return h.t[:, tt, :], toks, rel
                return f
            return [mk(tt) for tt in range(ntt)]

        def evac_copy(dst, bk, tp, extra=None):
            E = kx.alt()
            E.wait(tp, extra)
            if E is A:
                ins = E.e.activation(out=dst, in_=bk.ap if not isinstance(bk, tuple) else bk[0], func=AF.Copy)
            else:
                ins = E.e.tensor_copy(out=dst, in_=bk.ap if not isinstance(bk, tuple) else bk[0])
            te = E.mark(ins)
            return te

        def proj_tm_add(src, src_toks, nk, wname):
            last = None
            for cb in range(NCB):
                ws, wt, wv = lw(wname, cb, nk, 512)
                for tt in range(4):
                    bk = next_bank()
                    PE.wait(wt, src_toks, bk.free)
                    for kc in range(nk):
                        ins = PE.e.matmul(bk.ap, lhsT=src[:, kc, tt * 128:(tt + 1) * 128], rhs=wv[:, kc, :],
                                          start=(kc == 0), stop=(kc == nk - 1))
                    tp = PE.mark(ins)
                    last = tp
                    V.wait(tp, htok.get((tt, cb)))
                    reg = h.t[:, tt, cb * 512:(cb + 1) * 512]
                    ta = V.mark(V.e.tensor_tensor(out=reg, in0=bk.ap, in1=reg, op=ALU.add))
                    bk.free = ta
                    htok[(tt, cb)] = [ta]
                ws.busy = [last]
            return last

        for j, (T, OWN) in enumerate(cfg.jobs):
            kx.load(h, h.t[:, 0, :], d[f"mem{j}"][0:128, :])
            tm = kx.load(h, h.t[:, 1, :], d[f"mem{j}"][128:256, :], first=False)
            htok.clear()
            mtoks = make_xnT(kx, ncx, h_tiles(2, [tm]), c["g_mem"], fb, fb_free[0], ntt=2)
            fb_free[0] = None
            last = None
            kdone = []
            for m in range(KC):
                if m % 4 == 0:
                    ws, wt, wv = lw("Wb_mkv", m // 4, KC, 512)
                bk = next_bank()
                PE.wait(wt, mtoks, bk.free)
                for kc in range(KC):
                    ins = PE.e.matmul(bk.ap[:, 0:256], lhsT=wv[:, kc, (m % 4) * 128:(m % 4 + 1) * 128],
                                      rhs=fb[:, kc, 0:256], start=(kc == 0), stop=(kc == KC - 1))
                tp = PE.mark(ins)
                last = tp
                te = evac_copy(KmT[:, m, :], (bk.ap[:, 0:256],), tp, fa_free[0] if m == 0 else None)
                bk.free = te
                kdone.append(te)
                if m % 4 == 3 or m == KC - 1:
                    ws.busy = [last]
            for cb in range(NCB):
                ws, wt, wv = lw("Wb_mkv", NCB + cb, KC, 512)
                for tt in range(2):
                    bk = next_bank()
                    PE.wait(wt, mtoks, bk.free)
                    for kc in range(KC):
                        ins = PE.e.matmul(bk.ap, lhsT=fb[:, kc, tt * 128:(tt + 1) * 128], rhs=wv[:, kc, :],
                                          start=(kc == 0), stop=(kc == KC - 1))
                    tp = PE.mark(ins)
                    last = tp
                    te = evac_copy(Vm[:, tt, cb * 512:(cb + 1) * 512], bk, tp)
                    bk.free = te
                    kdone.append(te)
                ws.busy = [last]
            fb_free[0] = [last]
            mem_ready = kdone[-2:] + kdone[KC - 2:KC]
            for b in range(OWN // 512):
                t0 = b * 512
                fa.busy = list(fa_free[0] or []) if fa_free[0] else []
                tmix = kx.load(fa, fa.t[:], d[f"MIXT{j}"][:, :, t0:t0 + 512].rearrange("c p q -> p c q"))
                h.busy = h.busy + [t for v_ in htok.values() if v_ for t in v_]
                for tt in range(4):
                    tx = kx.load(h, h.t[:, tt, :], d[f"x{j}"][t0 + tt * 128:t0 + (tt + 1) * 128, :], first=(tt == 0),
                                 eng=kx.ACT)
                htok.clear()
                for tt in range(4):
                    for cb in range(NCB):
                        htok[(tt, cb)] = [tx]
                last = proj_tm_add(fa.t, [tmix], 16, "Wb_out")
                fa_free[0] = [last]
                n1 = make_xnT(kx, ncx, h_tiles(4), c["g_xattn"], fb, fb_free[0])
                qdone = []
                for m in range(KC):
                    if m % 4 == 0:
                        ws, wt, wv = lw("Wb_mq", m // 4, KC, 512)
                    bk = next_bank()
                    PE.wait(wt, n1, bk.free)
                    for kc in range(KC):
                        ins = PE.e.matmul(bk.ap, lhsT=wv[:, kc, (m % 4) * 128:(m % 4 + 1) * 128], rhs=fb[:, kc, :],
                                          start=(kc == 0), stop=(kc == KC - 1))
                    tp = PE.mark(ins)
                    last = tp
                    te = evac_copy(fa.t[:, m, :], bk, tp, fa_free[0] if m == 0 else None)
                    bk.free = te
                    qdone.append(te)
                    if m % 4 == 3 or m == KC - 1:
                        ws.busy = [last]
                fb_free[0] = [last]
                qtoks = qdone[-2:]
                lastS = None
                for hm in range(4):
                    pi = st["pi"] % 2
                    st["pi"] += 1
                    pt = PTm[pi]
                    texp = []
                    for kc2 in range(2):
                        bk = next_bank()
                        PE.wait(qtoks, mem_ready, bk.free)
                        for dc in range(MC):
                            ins = PE.e.matmul(bk.ap, lhsT=KmT[:, hm * MC + dc, kc2 * 128:(kc2 + 1) * 128],
                                              rhs=fa.t[:, hm * MC + dc, :], start=(dc == 0), stop=(dc == MC - 1))
                        tp = PE.mark(ins)
                        lastS = tp
                        A.wait(tp, PTm_free[pi])
                        te = A.mark(A.e.activation(out=pt[:, kc2, :], in_=bk.ap, func=AF.Exp, scale=mscale))
                        bk.free = te
                        texp.append(te)
                    PTm_free[pi] = None
                    bks = next_bank()
                    PE.wait(texp, bks.free)
                    PE.e.matmul(bks.ap, lhsT=c["ones_bf"][:], rhs=pt[:, 0, :], start=True, stop=False)
                    tps = PE.mark(PE.e.matmul(bks.ap, lhsT=c["ones_bf"][:], rhs=pt[:, 1, :], start=False, stop=True))
                    V.wait(tps, Rm_free[pi])
                    tr = V.mark(V.e.reciprocal(out=Rm[pi][:], in_=bks.ap))
                    bks.free = tr
                    lastO = None
                    for dvc in range(MC):
                        ch = hm * MC + dvc
                        bk = next_bank()
                        PE.wait(bk.free)
                        PE.e.matmul(bk.ap, lhsT=Vm[:, 0, ch * 128:(ch + 1) * 128], rhs=pt[:, 0, :], start=True, stop=False)
                        tp = PE.mark(PE.e.matmul(bk.ap, lhsT=Vm[:, 1, ch * 128:(ch + 1) * 128], rhs=pt[:, 1, :],
                                                 start=False, stop=True))
                        lastO = tp
                        V.wait(tp, tr, fb_free[0])
                        to = V.mark(V.e.tensor_tensor(out=fb[:, ch, :], in0=bk.ap, in1=Rm[pi][:], op=ALU.mult))
                        bk.free = to
                    PTm_free[pi] = lastO
                    Rm_free[pi] = to
                fb_free[0] = None
                fa_free[0] = [lastS]
                om_toks = [to]
                last = proj_tm_add(fb, om_toks, KC, "Wb_mo")
                fb_free[0] = [last]
                n2 = make_xnT(kx, ncx, h_tiles(4), c["g_ffn"], fa.t, fa_free[0])
                lastG = None
                for fblk in range(NFB):
                    wg, tg_, vg = lw("Wb_gu", fblk, KC, 512)
                    wu, tu_, vu = lw("Wb_gu", NFB + fblk, KC, 512)
                    wd, td_, vd = lw("Wb_down", fblk, 4, D)
                    ai = st["ai"] % 2
                    st["ai"] += 1
                    at = actT[ai]
                    tacts = []
                    for cc in range(4):
                        bg = next_bank()
                        PE.wait(tg_, n2, bg.free)
                        for kc in range(KC):
                            ins = PE.e.matmul(bg.ap, lhsT=vg[:, kc, cc * 128:(cc + 1) * 128], rhs=fa.t[:, kc, :],
                                              start=(kc == 0), stop=(kc == KC - 1))
                        tpg = PE.mark(ins)
                        bu = next_bank()
                        PE.wait(tu_, bu.free)
                        for kc in range(KC):
                            ins = PE.e.matmul(bu.ap, lhsT=vu[:, kc, cc * 128:(cc + 1) * 128], rhs=fa.t[:, kc, :],
                                              start=(kc == 0), stop=(kc == KC - 1))
                        tpu = PE.mark(ins)
                        lastG = tpu
                        si = st["si"] % 2
                        st["si"] += 1
                        A.wait(tpg, sg_free[si])
                        tsg = A.mark(A.e.activation(out=sg[si][:], in_=bg.ap, func=AF.Silu))
                        bg.free = tsg
                        V.wait(tsg, tpu, act_free[ai] if cc == 0 else None)
                        tact = V.mark(V.e.tensor_tensor(out=at[:, cc, :], in0=bu.ap, in1=sg[si][:], op=ALU.mult))
                        bu.free = tact
                        sg_free[si] = tact
                        tacts.append(tact)
                    wg.busy = [lastG]
                    wu.busy = [lastG]
                    lastD = None
                    for cb in range(NCB):
                        for tt in range(4):
                            bk = next_bank()
                            PE.wait(td_, tacts[-1], bk.free)
                            for cc in range(4):
                                ins = PE.e.matmul(bk.ap, lhsT=at[:, cc, tt * 128:(tt + 1) * 128],
                                                  rhs=vd[:, cc, cb * 512:(cb + 1) * 512], start=(cc == 0), stop=(cc == 3))
                            tp = PE.mark(ins)
                            lastD = tp
                            V.wait(tp, htok.get((tt, cb)))
                            reg = h.t[:, tt, cb * 512:(cb + 1) * 512]
                            ta = V.mark(V.e.tensor_tensor(out=reg, in0=bk.ap, in1=reg, op=ALU.add))
                            bk.free = ta
                            htok[(tt, cb)] = [ta]
                    wd.busy = [lastD]
                    act_free[ai] = lastD
                fa_free[0] = [lastG]
                ystage = ncx.hns[0][:].rearrange("p a d -> p (a d)").bitcast(F32)
                stoks = []
                rdtoks = []
                for tt in range(4):
                    col = st["yi"] % 8
                    st["yi"] += 1
                    toks = [htok.get((tt, cb)) for cb in range(NCB)]
                    A.wait(ncx.junk_tok)
                    ta, tc = rms_rstd(kx, h.t[:, tt, :], ystat[:, col:col + 1], yrstd[:, col:col + 1], ncx.junk[:],
                                      D, toks)
                    ncx.junk_tok = ta
                    ys = ystage[:, (tt % 2) * D:(tt % 2 + 1) * D]
                    V.wait(ta, tc, tg, ncx.hn_free[0], ys_tok[tt % 2])
                    ty = V.mark(V.e.scalar_tensor_tensor(out=ys, in0=h.t[:, tt, :],
                                                         scalar=yrstd[:, col:col + 1], in1=gfin.t[:],
                                                         op0=ALU.mult, op1=ALU.mult))
                    rdtoks += [ta, ty]
                    kx.POOL.wait(ty)
                    kx.POOL.e.dma_start(out=d[f"y{j}"][t0 + tt * 128:t0 + (tt + 1) * 128, :],
                                        in_=ys).then_inc(ysems[tt % 2].sem, 16)
                    ysems[tt % 2].cnt += 16
                    stk = (ysems[tt % 2].sem, ysems[tt % 2].cnt)
                    ys_tok[tt % 2] = stk
                    kx.store_toks.append(stk)
                    stoks.append(stk)
                ncx.hn_free[0] = [ncx.hn_free[0], stoks[-1], stoks[-2]]
                htok.clear()
                h.busy = rdtoks
        kx.barrier()


def rope_tables(pos):
    inv = (1.0 / (10000.0 ** (np.arange(0, 64, 2, dtype=np.float32) / np.float32(64)))).astype(np.float32)
    ang = pos.astype(np.float32)[:, None] * inv[None, :]
    ang = np.concatenate([ang, ang, ang, ang], axis=-1)
    return np.ascontiguousarray(np.cos(ang).T.astype(np.float32)), \
        np.ascontiguousarray(np.sin(ang).T.astype(np.float32))


def na_table(rpb, j, nt):
    R = 2 * nt
    q = np.arange(128)
    r = 2 * j + q // 64
    cq = q % 64
    rs = np.clip(r - 4, 0, R - 8)
    cst = np.clip(cq - 8, 0, 64 - 16)
    tab = np.full((128, 8, 640), NEG, dtype=np.float32)
    for s in range(5):
        kt = j - 2 + s
        if j == 0 and s == 0:
            kt = 3
        if j == nt - 1 and s == 4:
            kt = nt - 4
        if kt < 0 or kt >= nt:
            continue
        k = np.arange(128)
        kr = 2 * kt + k // 64
        kcn = k % 64
        valid = ((kr[None, :] >= rs[:, None]) & (kr[None, :] < rs[:, None] + 8) &
                 (kcn[None, :] >= cst[:, None]) & (kcn[None, :] < cst[:, None] + 16))
        dr = np.clip(kr[None, :] - r[:, None] + 7, 0, 14)
        dc = np.clip(kcn[None, :] - cq[:, None] + 15, 0, 30)
        g = rpb[:, dr, dc]
        g = np.transpose(g, (1, 0, 2))
        blk = tab[:, :, s * 128:(s + 1) * 128]
        blk[...] = np.where(valid[:, None, :], g, blk)
    return tab


def rot_matrix():
    Rm = np.zeros((128, 128), dtype=np.float32)
    for p in range(128):
        if (p % 64) < 32:
            Rm[p + 32, p] = -1.0
        else:
            Rm[p - 32, p] = 1.0
    return Rm.astype(ml_dtypes.bfloat16)


def prepare_core_inputs(cfg, core, nparts, xs, mems, shared):
    m = dict(shared)
    for j, (T, OWN) in enumerate(cfg.jobs):
        part, npart = nparts[j]
        x = xs[j]
        a = part * OWN
        own = np.arange(a, a + OWN)
        rest = np.concatenate([np.arange(0, a), np.arange(a + OWN, T)])
        perm = np.concatenate([own, rest])
        m[f"x{j}"] = np.ascontiguousarray(x[perm])
        cosT, sinT = rope_tables(perm)
        m[f"cos{j}"] = cosT
        m[f"sin{j}"] = sinT
        nt = T // 128
        ta, tb = a // 128, (a + OWN) // 128
        xh = np.zeros((512, cfg.D), dtype=np.float32)

        def tile(k):
            return x[k * 128:(k + 1) * 128]
        if ta == 0:
            xh[0:128] = tile(3)
        else:
            xh[0:128] = tile(ta - 2)
            xh[128:256] = tile(ta - 1)
        if tb == nt:
            xh[384:512] = tile(nt - 4)
        else:
            xh[256:384] = tile(tb)
            xh[384:512] = tile(tb + 1)
        m[f"xh{j}"] = xh
        m[f"mem{j}"] = np.ascontiguousarray(mems[j])
        rpb = shared["_rpb"]
        m[f"nab{j}"] = np.stack([na_table(rpb, g, nt) for g in (ta, ta + 1, tb - 2, tb - 1)])
    del m["_rpb"]
    return m


def shared_inputs(cfg, inp):
    KC = cfg.KC
    sh = {}
    sh["w_in"] = np.ascontiguousarray(inp["w_in"][0])
    sh["w_out"] = np.ascontiguousarray(inp["w_out"][0])
    sh["w_mq"] = np.ascontiguousarray(inp["w_mq"][0])
    sh["w_mkv"] = np.ascontiguousarray(inp["w_mkv"][0])
    sh["w_mo"] = np.ascontiguousarray(inp["w_mo"][0])
    sh["w_gu"] = np.ascontiguousarray(inp["w_gate_up"][0])
    sh["w_down"] = np.ascontiguousarray(inp["w_down"][0])
    for g, src in (("g_mix", "g_mix"), ("g_xattn", "g_xattn"), ("g_mem", "g_mem"), ("g_ffn", "g_ffn")):
        sh[g] = np.ascontiguousarray(np.asarray(inp[src][0], dtype=np.float32).reshape(KC, 128).T)
    sh["g_final"] = np.ascontiguousarray(np.broadcast_to(np.asarray(inp["g_final"], dtype=np.float32)[None, :], (128, cfg.D)))
    sh["g_subln"] = np.ascontiguousarray(np.asarray(inp["g_subln"][0], dtype=np.float32).reshape(128, 1))
    lv = np.stack([inp["lam_q1"][0], inp["lam_k1"][0], inp["lam_q2"][0], inp["lam_k2"][0]]).astype(np.float32)
    sh["lamv"] = np.ascontiguousarray(np.broadcast_to(lv[None], (128, 4, 64)))
    sh["ident"] = np.eye(128, dtype=np.float32).astype(ml_dtypes.bfloat16)
    sh["rotm"] = rot_matrix()
    rpb = np.asarray(inp["rpb"][0], dtype=np.float32)
    sh["_rpb"] = rpb
    sh["nabi"] = na_table(rpb, 8, 32)
    return sh


_PROGRAM_CACHE = {}


def kernel(**inputs):
    cfg = Cfg()
    inp = {k: np.asarray(v) for k, v in inputs.items()}
    sh = shared_inputs(cfg, inp)
    in_maps = []
    for c in range(8):
        xs = [inp["x_prompt"][c // 4], inp["x_sample"][c // 2]]
        mems = [inp["mem_prompt"][c // 4], inp["mem_sample"][c // 2]]
        in_maps.append(prepare_core_inputs(cfg, c, [(c % 4, 4), (c % 2, 2)], xs, mems, sh))
    nc = build_program(cfg)
    res = run_bass_kernel_spmd(nc, in_maps, core_ids=list(range(8)))
    yp = np.zeros((2, 16384, cfg.D), dtype=np.float32)
    ysm = np.zeros((4, 4096, cfg.D), dtype=np.float32)
    for c in range(8):
        r = res.results[c]
        yp[c // 4, (c % 4) * 4096:(c % 4 + 1) * 4096] = r["y0"]
        ysm[c // 2, (c % 2) * 2048:(c % 2 + 1) * 2048] = r["y1"]
    return (yp, ysm)
```

```python
import contextlib
import math

import numpy as np
import ml_dtypes

import concourse.bass as bass
import concourse.mybir as mybir
from concourse.bass_utils import run_bass_kernel_spmd

F32 = mybir.dt.float32
BF16 = mybir.dt.bfloat16
AF = mybir.ActivationFunctionType
ALU = mybir.AluOpType
AX = mybir.AxisListType

NEG = -30000.0
NA_BACKGROUND_CAST = True
BACKGROUND_CAST = False
RMS_EPS = 1e-6
SUBLN_EPS = 1e-5
LAM_INIT = 0.8 - 0.6 * math.exp(-0.3 * 0)


class Cfg:
    def __init__(self, D=2048, DFF=5632, jobs=((16384, 4096), (4096, 2048)), debug=False,
                 phases=(0, 1, 2, 3)):
        self.D = D
        self.DFF = DFF
        self.jobs = tuple(jobs)
        self.debug = debug
        self.KC = D // 128
        self.MEMHD = D // 4
        self.MC = self.MEMHD // 128
        self.NFB = DFF // 512
        self.phases = tuple(phases)


class Eng:
    def __init__(self, nc, e, name, es):
        self.e = e
        self.name = name
        self.sem = es.enter_context(nc.semaphore("es_" + name))
        self.n = 0
        self.seen = {}

    def wait(self, *toks):
        best = {}

        def walk(ts):
            for t in ts:
                if t is None:
                    continue
                if isinstance(t, list):
                    walk(t)
                    continue
                sem, v = t
                if best.get(sem, 0) < v:
                    best[sem] = v
        walk(toks)
        for sem, v in best.items():
            if self.seen.get(sem, 0) >= v:
                continue
            self.e.wait_ge(sem, v)
            self.seen[sem] = v

    def mark(self, ins):
        self.n += 1
        ins.then_inc(self.sem, 1)
        return (self.sem, self.n)


class SemC:
    pool = {"ld": [], "st": []}
    nc = None
    es = None
    nalloc = 0

    @classmethod
    def get(cls, kind):
        if cls.pool[kind]:
            return cls.pool[kind].pop()
        s = SemC()
        s.sem = cls.es.enter_context(cls.nc.semaphore(f"dsem{cls.nalloc}"))
        cls.nalloc += 1
        s.cnt = 0
        return s


class Slot:
    def __init__(self, nc, es, name, shape, dt, kind="ld"):
        self.t = es.enter_context(nc.sbuf_tensor(name, shape, dt))
        self.sc = SemC.get(kind)
        es.callback(SemC.pool[kind].append, self.sc)
        self.busy = []

    @property
    def sem(self):
        return self.sc.sem

    @property
    def cnt(self):
        return self.sc.cnt

    @cnt.setter
    def cnt(self, v):
        self.sc.cnt = v

    def tok(self):
        return (self.sc.sem, self.sc.cnt)


class Bank:
    def __init__(self, ap):
        self.ap = ap
        self.free = None


class K:
    def __init__(self, nc, cfg, es):
        self.nc = nc
        self.cfg = cfg
        self.es = es
        SemC.pool = {"ld": [], "st": []}
        SemC.nc = nc
        SemC.es = es
        SemC.nalloc = 0
        self.PE = Eng(nc, nc.tensor, "pe", es)
        self.ACT = Eng(nc, nc.scalar, "act", es)
        self.DVE = Eng(nc, nc.vector, "dve", es)
        self.POOL = Eng(nc, nc.gpsimd, "pool", es)
        self.SP = Eng(nc, nc.sync, "sp", es)
        self.store_toks = []
        self.load_last = {}
        self.rr = 0

    def load(self, slot, dst, src, first=True, eng=None):
        q = eng or self.SP
        if first:
            q.wait(slot.busy)
            slot.busy = []
        q.e.dma_start(out=dst, in_=src).then_inc(slot.sem, 16)
        slot.cnt += 16
        self.load_last[slot.sem] = slot.cnt
        return slot.tok()

    def store(self, slot, dst, src, wait):
        q = self.POOL
        q.wait(wait)
        q.e.dma_start(out=dst, in_=src).then_inc(slot.sem, 16)
        slot.cnt += 16
        t = slot.tok()
        slot.busy = [t]
        self.last_store = t
        self.store_toks.append(t)
        return t

    def drain_stores(self, engines):
        last = {}
        for sem, v in self.store_toks:
            last[sem] = max(last.get(sem, 0), v)
        for e in engines:
            for sem, v in last.items():
                e.wait((sem, v))

    def barrier(self):
        toks = []
        A, V, P = self.ACT, self.DVE, self.POOL
        A.wait((A.sem, A.n))
        toks.append(A.mark(A.e.activation(out=self.dummy[:, 0:1], in_=self.c["eps_rms"][:], func=AF.Copy)))
        V.wait((V.sem, V.n))
        toks.append(V.mark(V.e.tensor_copy(out=self.dummy[:, 1:2], in_=self.c["eps_rms"][:])))
        P.wait((P.sem, P.n))
        toks.append(P.mark(P.e.tensor_copy(out=self.dummy[:, 2:3], in_=self.c["eps_rms"][:])))
        toks.append((self.PE.sem, self.PE.n))
        engines = (self.PE, A, V, P, self.SP)
        self.drain_stores(engines)
        for e in engines:
            e.wait(toks)
            for sem, v in self.load_last.items():
                e.wait((sem, v))

    def alt(self):
        self.rr ^= 1
        return self.ACT if self.rr else self.DVE


_UID = [0]


def sbuf(kx, name, shape, dt, es=None):
    _UID[0] += 1
    return (es or kx.es).enter_context(kx.nc.sbuf_tensor(f"{name}_{_UID[0]}", shape, dt))


W_SPECS = None


def weight_specs(cfg):
    D, DFF = cfg.D, cfg.DFF
    return [("w_in", D, 6144), ("w_out", 2048, D), ("w_mq", D, D), ("w_mkv", D, 2 * D),
            ("w_mo", D, D), ("w_gu", D, 2 * DFF), ("w_down", DFF, D)]


def build_program(cfg):
    nc = bass.Bass("TRN2", target_bir_lowering=False)
    D, DFF, KC = cfg.D, cfg.DFF, cfg.KC
    dbg = cfg.debug
    SK = "ExternalOutput" if dbg else "Internal"
    dram = {}

    def din(name, shape, dt=F32):
        dram[name] = nc.dram_tensor(name, list(shape), dt, kind="ExternalInput").ap()
        return dram[name]

    def dscr(name, shape, dt=BF16):
        dram[name] = nc.dram_tensor(name, list(shape), dt, kind=SK).ap()
        return dram[name]

    def dout(name, shape, dt=F32):
        dram[name] = nc.dram_tensor(name, list(shape), dt, kind="ExternalOutput").ap()
        return dram[name]

    NJ = len(cfg.jobs)
    for j, (T, OWN) in enumerate(cfg.jobs):
        din(f"x{j}", [T, D])
        din(f"xh{j}", [512, D])
        din(f"mem{j}", [256, D])
        din(f"cos{j}", [128, T])
        din(f"sin{j}", [128, T])
        din(f"nab{j}", [4, 128, 8, 640])
        dout(f"y{j}", [OWN, D])
        dscr(f"KT{j}", [8, 128, T])
        dscr(f"VH{j}", [8, 128, T // 128, 128])
        dscr(f"QT{j}", [8, 128, OWN])
        dscr(f"NAQT{j}", [8, 128, OWN])
        dscr(f"NAKT{j}", [8, 128, OWN + 512])
        dscr(f"NAV{j}", [OWN + 512, 1024])
        dscr(f"MIXT{j}", [16, 128, OWN])
    din("nabi", [128, 8, 640])
    for name, R, C in weight_specs(cfg):
        din(name, [R, C])
    for g in ("g_mix", "g_xattn", "g_mem", "g_ffn"):
        din(g, [128, KC])
    din("g_final", [128, D])
    din("g_subln", [128, 1])
    din("lamv", [128, 4, 64])
    din("ident", [128, 128], BF16)
    din("rotm", [128, 128], BF16)
    dscr("Wb_in", [12, 128, KC, 512])
    dscr("Wb_out", [D // 512, 128, 16, 512])
    dscr("Wb_mq", [D // 512, 128, KC, 512])
    dscr("Wb_mkv", [2 * D // 512, 128, KC, 512])
    dscr("Wb_mo", [D // 512, 128, KC, 512])
    dscr("Wb_gu", [2 * DFF // 512, 128, KC, 512])
    dscr("Wb_down", [DFF // 512, 128, 4, D])

    es = contextlib.ExitStack()
    with es:
        kx = K(nc, cfg, es)
        kx.dram = dram
        setup_consts(kx)
        if 0 in cfg.phases:
            phase0_weights(kx, ("w_in",) if NA_BACKGROUND_CAST else tuple(n for n, _, _ in weight_specs(cfg)))
        if 1 in cfg.phases:
            phase1(kx)
        if 2 in cfg.phases:
            phase2_na(kx)
            phase2_diff(kx)
        if 3 in cfg.phases:
            phase3(kx)
        finish(kx)
    return nc


def setup_consts(kx):
    nc, es, cfg = kx.nc, kx.es, kx.cfg
    c = kx.c = {}
    d = kx.dram
    KC = cfg.KC
    cs = Slot(nc, es, "cst", [128, 1], F32)
    kx.cslot = cs
    kx.dummy = sbuf(kx, "bar_dummy", [128, 4], F32)
    kx.ysem = SemC.get("st")
    kx.ysem2 = SemC.get("st")

    def ld(name, shape, dt, src):
        t = sbuf(kx, "c_" + name, shape, dt)
        kx.load(cs, t[:], src, first=False)
        c[name] = t
        return t

    ld("ident", [128, 128], BF16, d["ident"])
    ld("rotm", [128, 128], BF16, d["rotm"])
    for g in ("g_mix", "g_xattn", "g_mem", "g_ffn"):
        ld(g, [128, KC], F32, d[g])
    ld("g_subln", [128, 1], F32, d["g_subln"])
    ld("lamv", [128, 4, 64], F32, d["lamv"])
    ctok = cs.tok()
    kx.ctok = ctok
    P = kx.POOL
    ones_bf = sbuf(kx, "ones_bf", [128, 128], BF16)
    ones_f = sbuf(kx, "ones_f", [128, 128], F32)
    sel1 = sbuf(kx, "sel1", [64, 128], F32)
    sel2 = sbuf(kx, "sel2", [64, 128], F32)
    P.e.memset(ones_bf[:], 1.0)
    P.e.memset(ones_f[:], 1.0)
    P.e.memset(sel1[:], 0.0)
    t0 = P.mark(P.e.memset(sel2[:], 0.0))
    P.wait(t0)
    P.e.memset(sel1[0:1, :], 1.0)
    P.e.memset(sel2[32:33, :], 1.0)
    eps_rms = sbuf(kx, "eps_rms", [128, 1], F32)
    eps_sub = sbuf(kx, "eps_sub", [128, 1], F32)
    P.e.memset(eps_rms[:], RMS_EPS)
    t2 = P.mark(P.e.memset(eps_sub[:], SUBLN_EPS))
    c.update(ones_bf=ones_bf, ones_f=ones_f, sel1=sel1, sel2=sel2, eps_rms=eps_rms, eps_sub=eps_sub)
    V, A = kx.DVE, kx.ACT
    lt = sbuf(kx, "lam_t", [128, 2, 64], F32)
    ls = sbuf(kx, "lam_s", [128, 2], F32)
    le = sbuf(kx, "lam_e", [128, 2], F32)
    nlam = sbuf(kx, "nlam", [128, 1], F32)
    gsub = sbuf(kx, "gsub", [128, 1], F32)
    V.wait(ctok)
    lv = c["lamv"]
    V.e.tensor_tensor(out=lt[:, 0, :], in0=lv[:, 0, :], in1=lv[:, 1, :], op=ALU.mult)
    ta = V.mark(V.e.tensor_tensor(out=lt[:, 1, :], in0=lv[:, 2, :], in1=lv[:, 3, :], op=ALU.mult))
    V.e.wait_ge(V.sem, ta[1])
    tb = V.mark(V.e.reduce_sum(out=ls[:], in_=lt[:], axis=AX.X))
    A.wait(tb)
    tc = A.mark(A.e.activation(out=le[:], in_=ls[:], func=AF.Exp))
    V.wait(tc)
    td = V.mark(V.e.tensor_tensor(out=nlam[:], in0=le[:, 1:2], in1=le[:, 0:1], op=ALU.subtract))
    V.e.wait_ge(V.sem, td[1])
    te = V.mark(V.e.tensor_scalar_add(out=nlam[:], in0=nlam[:], scalar1=-LAM_INIT))
    tf = V.mark(V.e.tensor_scalar_mul(out=gsub[:], in0=c["g_subln"][:], scalar1=(1.0 - LAM_INIT)))
    c.update(nlam=nlam, gsub=gsub)
    kx.const_toks = [ctok, t2, tf, te]
    for e in (kx.PE, kx.ACT, kx.DVE, kx.POOL):
        e.wait(kx.const_toks)


def finish(kx):
    kx.drain_stores([kx.POOL])
    t = kx.POOL.mark(kx.POOL.e.memset(kx.cslot.t[:], 0.0))
    for e in (kx.SP, kx.ACT, kx.DVE, kx.PE):
        e.wait(t)


def phase0_weights(kx, names):
    nc, cfg, d = kx.nc, kx.cfg, kx.dram
    es = contextlib.ExitStack()
    with es:
        NS = 3
        fin = [Slot(nc, es, f"p0f{i}", [128, 4096], F32) for i in range(NS)]
        fout = [Slot(nc, es, f"p0b{i}", [128, 4096], BF16, "st") for i in range(NS)]
        it = 0
        for name, R, C in weight_specs(cfg):
            if name not in names:
                continue
            src = d[name]
            dstn = "Wb_" + name[2:]
            dst = d[dstn]
            for kc in range(R // 128):
                for c0 in range(0, C, 4096):
                    cw = min(4096, C - c0)
                    si, so = fin[it % NS], fout[it % NS]
                    lt = kx.load(si, si.t[:, 0:cw], src[kc * 128:(kc + 1) * 128, c0:c0 + cw])
                    E = kx.ACT if (it % 2) else kx.DVE
                    E.wait(lt, so.busy)
                    so.busy = []
                    if E is kx.ACT:
                        ins = E.e.activation(out=so.t[:, 0:cw], in_=si.t[:, 0:cw], func=AF.Copy)
                    else:
                        ins = E.e.tensor_copy(out=so.t[:, 0:cw], in_=si.t[:, 0:cw])
                    ct = E.mark(ins)
                    si.busy = [ct]
                    if name == "w_down":
                        fb, kcl = kc // 4, kc % 4
                        dap = dst[fb, :, kcl, :]
                        sap = so.t[:, 0:cw]
                    else:
                        nb0 = c0 // 512
                        nbn = cw // 512
                        dap = dst[nb0:nb0 + nbn, :, kc, :].rearrange("nb p c -> p nb c")
                        sap = so.t[:, 0:cw].rearrange("p (nb c) -> p nb c", c=512)
                    kx.store(so, dap, sap, ct)
                    it += 1
        kx.barrier()


class NormCtx:
    def __init__(self, kx, es, pb, nx=2, nhn=1):
        nc, cfg = kx.nc, kx.cfg
        D = cfg.D
        _UID[0] += 1
        self.xs = [Slot(nc, es, f"nx{i}_{_UID[0]}", [128, D], F32) for i in range(nx)]
        self.junk = sbuf(kx, "njunk", [128, D], BF16, es)
        self.hns = [sbuf(kx, f"nhn{i}", [128, 4, D], BF16, es) for i in range(nhn)]
        self.ss = sbuf(kx, "nss", [128, 8], F32, es)
        self.rstd = sbuf(kx, "nrstd", [128, 8], F32, es)
        self.pb = pb
        self.hn_free = [None] * nhn
        self.i = 0
        self.junk_tok = None


def rms_rstd(kx, src_ap, ss_ap, rstd_ap, junk_ap, width, in_toks, eps_ap=None):
    A, V = kx.ACT, kx.DVE
    A.wait(in_toks)
    ta = A.mark(A.e.activation(out=junk_ap, in_=src_ap, func=AF.Square, accum_out=ss_ap))
    A.wait(ta)
    tb = A.mark(A.e.activation(out=ss_ap, in_=ss_ap, func=AF.Sqrt, scale=1.0 / width,
                               bias=(eps_ap if eps_ap is not None else kx.c["eps_rms"][:])))
    V.wait(tb)
    tc = V.mark(V.e.reciprocal(out=rstd_ap, in_=ss_ap))
    return ta, tc


def norm_part(kx, ncx, tiles, ntt=4, hp=0):
    cfg = kx.cfg
    D = cfg.D
    A, V = kx.ACT, kx.DVE
    hn = ncx.hns[hp]
    hn_toks = []
    for tt in range(ntt):
        ap, in_toks, rel = tiles[tt]()
        col = (ncx.i % 8)
        ncx.i += 1
        A.wait(ncx.junk_tok)
        ta, tc = rms_rstd(kx, ap, ncx.ss[:, col:col + 1], ncx.rstd[:, col:col + 1], ncx.junk[:],
                          D, in_toks)
        ncx.junk_tok = ta
        V.wait(ncx.hn_free[hp], tc)
        th = V.mark(V.e.tensor_scalar(out=hn[:, tt, :], in0=ap, scalar1=ncx.rstd[:, col:col + 1],
                                      scalar2=None, op0=ALU.mult))
        hn_toks.append(th)
        rel([ta, th])
    ncx.hn_free[hp] = None
    return hn_toks


def trans_part(kx, ncx, hn_toks, gT, dstT, dst_free, ntt=4, hp=0):
    cfg = kx.cfg
    KC = cfg.KC
    A, PE = kx.ACT, kx.PE
    hn = ncx.hns[hp]
    out_toks = []
    last_pe = None
    for kc in range(KC):
        bk = ncx.pb[kc % len(ncx.pb)]
        PE.wait(hn_toks, bk.free)
        for tt in range(ntt):
            ins = PE.e.transpose(bk.ap[:, tt * 128:(tt + 1) * 128], hn[:, tt, kc * 128:(kc + 1) * 128],
                                 kx.c["ident"][:])
        tp = PE.mark(ins)
        last_pe = tp
        E = kx.alt()
        E.wait(tp, dst_free)
        if E is A:
            ins = E.e.activation(out=dstT[:, kc, 0:ntt * 128], in_=bk.ap[:, 0:ntt * 128], func=AF.Copy,
                                 scale=gT[:, kc:kc + 1])
        else:
            ins = E.e.tensor_scalar(out=dstT[:, kc, 0:ntt * 128], in0=bk.ap[:, 0:ntt * 128],
                                    scalar1=gT[:, kc:kc + 1], scalar2=None, op0=ALU.mult)
        te = E.mark(ins)
        bk.free = te
        out_toks.append(te)
    ncx.hn_free[hp] = last_pe
    return out_toks[-2:]


def make_xnT(kx, ncx, tiles, gT, dstT, dst_free, ntt=4):
    hn_toks = norm_part(kx, ncx, tiles, ntt, 0)
    return trans_part(kx, ncx, hn_toks, gT, dstT, dst_free, ntt, 0)


def phase1(kx):
    nc, cfg, d = kx.nc, kx.cfg, kx.dram
    D, KC = cfg.D, cfg.KC
    PE, A, V = kx.PE, kx.ACT, kx.DVE
    es = contextlib.ExitStack()
    with es:
        NB1 = 5
        ps = [es.enter_context(nc.psum_tensor(f"p1s{i}", [128, 512], F32)) for i in range(NB1)]
        pbt = [es.enter_context(nc.psum_tensor(f"p1b{i}", [128, 1024], BF16)) for i in range(3)]
        banks = [Bank(p[:]) for p in ps]
        pb = [Bank(pbt[i][:, 0:512]) for i in range(3)]
        ncx = NormCtx(kx, es, pb, nx=4, nhn=2)
        NW = 4
        wr = [Slot(nc, es, f"p1w{i}", [128, KC, 512], BF16) for i in range(NW)]
        xnTs = [sbuf(kx, f"p1xnT{i}", [128, KC, 512], BF16, es) for i in range(2)]
        cur = dict(x=None, ready=None)
        cs_ = [Slot(nc, es, f"p1cs{i}", [128, 2, 512], F32) for i in range(3)]
        fst = [Slot(nc, es, f"p1fst{i}", [128, 512], BF16, "st") for i in range(4)]
        tst = [Slot(nc, es, f"p1tst{i}", [128, 1024], BF16, "st") for i in range(3)]
        qraw = [sbuf(kx, f"p1qraw{i}", [128, 512], BF16, es) for i in range(2)]
        tm1 = [sbuf(kx, f"p1tm1{i}", [128, 512], F32, es) for i in range(2)]
        tm2 = [sbuf(kx, f"p1tm2{i}", [128, 512], F32, es) for i in range(2)]
        st = dict(bi=0, fi=0, ti=0, ri=0, xi=0, wi=0, ci=0)
        qraw_free = [None, None]
        tm_free = [None, None]
        xn_readers = [[], []]

        def next_bank():
            b = banks[st["bi"] % NB1]
            st["bi"] += 1
            return b

        def x_tiles(src, t0):
            def mk(tt):
                def f():
                    s = ncx.xs[st["xi"] % len(ncx.xs)]
                    st["xi"] += 1
                    lt = kx.load(s, s.t[:], src[t0 + tt * 128:t0 + (tt + 1) * 128, :])

                    def rel(toks):
                        s.busy = list(toks)
                    return s.t[:], [lt], rel
                return f
            return [mk(tt) for tt in range(4)]

        def do_norm(src, t0, par):
            return norm_part(kx, ncx, x_tiles(src, t0), 4, par)

        def do_trans(hn_toks, par):
            toks = trans_part(kx, ncx, hn_toks, kx.c["g_mix"], xnTs[par], xn_readers[par], 4, par)
            xn_readers[par] = []
            return toks

        def fm_proj(wslot, wtok, c):
            bk = next_bank()
            xnT = cur["x"]
            PE.wait(wtok, cur["ready"], bk.free)
            for kc in range(KC):
                ins = PE.e.matmul(bk.ap, lhsT=wslot.t[:, kc, c * 128:(c + 1) * 128], rhs=xnT[:, kc, :],
                                  start=(kc == 0), stop=(kc == KC - 1))
            tp = PE.mark(ins)
            return bk, tp

        def tm_proj(wslot, wtok, tt):
            bk = next_bank()
            xnT = cur["x"]
            PE.wait(wtok, cur["ready"], bk.free)
            for kc in range(KC):
                ins = PE.e.matmul(bk.ap, lhsT=xnT[:, kc, tt * 128:(tt + 1) * 128], rhs=wslot.t[:, kc, :],
                                  start=(kc == 0), stop=(kc == KC - 1))
            tp = PE.mark(ins)
            return bk, tp

        def plain_fm(wslot, wtok, c, dst_ap):
            bk, tp = fm_proj(wslot, wtok, c)
            s = fst[st["fi"] % len(fst)]
            st["fi"] += 1
            E = kx.alt()
            E.wait(tp, s.busy)
            s.busy = []
            if E is A:
                ins = E.e.activation(out=s.t[:], in_=bk.ap, func=AF.Copy)
            else:
                ins = E.e.tensor_copy(out=s.t[:], in_=bk.ap)
            te = E.mark(ins)
            bk.free = te
            kx.store(s, dst_ap, s.t[:], te)
            return tp

        rope_pend = []

        def rope_flush():
            while rope_pend:
                rope_pend.pop(0)()

        def rope_fm(wslot, wtok, c, cslot, ctok, dst_ap):
            bk, tp = fm_proj(wslot, wtok, c)
            r = st["ri"] % 2
            st["ri"] += 1
            A.wait(tp, qraw_free[r])
            ta = A.mark(A.e.activation(out=qraw[r][:], in_=bk.ap, func=AF.Copy))
            bk.free = ta
            res = {}

            def tail():
                bk2 = next_bank()
                PE.wait(ta, bk2.free)
                tr = PE.mark(PE.e.matmul(bk2.ap, lhsT=kx.c["rotm"][:], rhs=qraw[r][:], start=True, stop=True))
                V.wait(ta, ctok, tm_free[r])
                t1 = V.mark(V.e.tensor_tensor(out=tm1[r][:], in0=qraw[r][:], in1=cslot.t[:, 0, :], op=ALU.mult))
                V.wait(tr)
                t2 = V.mark(V.e.tensor_tensor(out=tm2[r][:], in0=bk2.ap, in1=cslot.t[:, 1, :], op=ALU.mult))
                bk2.free = t2
                qraw_free[r] = [tr, t1]
                s_ = fst[st["fi"] % len(fst)]
                st["fi"] += 1
                V.wait(t1, t2, s_.busy)
                s_.busy = []
                t3 = V.mark(V.e.tensor_tensor(out=s_.t[:], in0=tm1[r][:], in1=tm2[r][:], op=ALU.add))
                tm_free[r] = t3
                kx.store(s_, dst_ap, s_.t[:], t3)
                res["t3"] = t3

            prev = list(rope_pend)
            del rope_pend[:]
            for f in prev:
                f()
            rope_pend.append(tail)
            return tp, res

        def tm_out(wslots, wtoks, tt, dst_ap, head_major=None):
            s = tst[st["ti"] % len(tst)]
            st["ti"] += 1
            toks = []
            tps = []
            for cb in range(2):
                bk, tp = tm_proj(wslots[cb], wtoks[cb], tt)
                tps.append(tp)
                E = kx.alt()
                E.wait(tp, s.busy)
                if E is A:
                    ins = E.e.activation(out=s.t[:, cb * 512:(cb + 1) * 512], in_=bk.ap, func=AF.Copy)
                else:
                    ins = E.e.tensor_copy(out=s.t[:, cb * 512:(cb + 1) * 512], in_=bk.ap)
                te = E.mark(ins)
                bk.free = te
                toks.append(te)
            s.busy = []
            if head_major is not None:
                src = s.t[:].rearrange("p (h d) -> p h d", d=128)
            else:
                src = s.t[:]
            kx.store(s, dst_ap, src, toks)
            return tps

        def load_w(nb):
            s = wr[st["wi"] % NW]
            st["wi"] += 1
            t = kx.load(s, s.t[:], d["Wb_in"][nb])
            return s, t

        def load_cs(j, t0):
            s = cs_[st["ci"] % 3]
            st["ci"] += 1
            kx.load(s, s.t[:, 0, :], d[f"cos{j}"][:, t0:t0 + 512])
            t = kx.load(s, s.t[:, 1, :], d[f"sin{j}"][:, t0:t0 + 512], first=False)
            return s, t

        items = []
        for j, (T, OWN) in enumerate(cfg.jobs):
            for b in range(T // 512):
                items.append(("kv", j, b))
        for j, (T, OWN) in enumerate(cfg.jobs):
            for b in range(OWN // 512):
                items.append(("own", j, b))
            items.append(("halo", j, 0))

        def prepA(i):
            kind, j, b = items[i]
            if kind == "halo":
                return do_norm(d[f"xh{j}"], 0, i % 2), None
            return do_norm(d[f"x{j}"], b * 512, i % 2), load_cs(j, b * 512)

        def prepB(i, pa):
            hn_toks, csl = pa
            return do_trans(hn_toks, i % 2), csl

        kvw = [load_w(nb) for nb in (8, 9, 10, 11)]
        kv_last = [None]

        def run_item(i, ready, csl):
            kind, j, b = items[i]
            T, OWN = cfg.jobs[j]
            t0 = b * 512
            cur["x"] = xnTs[i % 2]
            cur["ready"] = ready
            last = None
            if kind == "kv":
                cslot, ctok = csl
                res = None
                for h in range(8):
                    ws, wt = kvw[h // 4]
                    tp, res = rope_fm(ws, wt, h % 4, cslot, ctok, d[f"KT{j}"][h, :, t0:t0 + 512])
                rope_flush()
                cslot.busy = [res["t3"]]
                for tt in range(4):
                    kt = (t0 // 128) + tt
                    tps = tm_out([kvw[2][0], kvw[3][0]], [kvw[2][1], kvw[3][1]], tt,
                                 d[f"VH{j}"][:, :, kt, :].rearrange("h p d -> p h d"), head_major=True)
                    last = tps[-1]
                kv_last[0] = last
                if i + 1 < len(items) and items[i + 1][0] != "kv":
                    for (s_, t_) in kvw:
                        s_.busy = [last]
                xn_readers[i % 2] = [last]
                return
            if kind == "own":
                cslot, ctok = csl
                for nb in (0, 1):
                    ws, wt = load_w(nb)
                    for c in range(4):
                        h = (nb % 2) * 4 + c
                        last = plain_fm(ws, wt, c, d[f"NAQT{j}"][h, :, t0:t0 + 512])
                    ws.busy = [last]
            for nb in (2, 3):
                ws, wt = load_w(nb)
                for c in range(4):
                    h = (nb % 2) * 4 + c
                    if kind == "own":
                        last = plain_fm(ws, wt, c, d[f"NAKT{j}"][h, :, 256 + t0:256 + t0 + 512])
                    else:
                        bk, tp = fm_proj(ws, wt, c)
                        s_ = fst[st["fi"] % len(fst)]
                        st["fi"] += 1
                        E = kx.alt()
                        E.wait(tp, s_.busy)
                        s_.busy = []
                        if E is A:
                            ins = E.e.activation(out=s_.t[:], in_=bk.ap, func=AF.Copy)
                        else:
                            ins = E.e.tensor_copy(out=s_.t[:], in_=bk.ap)
                        te = E.mark(ins)
                        bk.free = te
                        kx.store(s_, d[f"NAKT{j}"][h, :, 0:256], s_.t[:, 0:256], te)
                        kx.store(s_, d[f"NAKT{j}"][h, :, 256 + OWN:512 + OWN], s_.t[:, 256:512], te)
                        last = tp
                ws.busy = [last]
            wv = [load_w(nb) for nb in (4, 5)]
            for tt in range(4):
                if kind == "own":
                    r0 = 256 + t0 + tt * 128
                else:
                    r0 = tt * 128 if tt < 2 else 256 + OWN + (tt - 2) * 128
                tps = tm_out([wv[0][0], wv[1][0]], [wv[0][1], wv[1][1]], tt,
                             d[f"NAV{j}"][r0:r0 + 128, :])
                last = tps[-1]
            for (s_, t_) in wv:
                s_.busy = [last]
            if kind == "own":
                res = None
                for nb in (6, 7):
                    ws, wt = load_w(nb)
                    for c in range(4):
                        h = (nb % 2) * 4 + c
                        last, res = rope_fm(ws, wt, c, cslot, ctok, d[f"QT{j}"][h, :, t0:t0 + 512])
                    ws.busy = [last]
                rope_flush()
                cslot.busy = [res["t3"]]
            xn_readers[i % 2] = [last]

        n_it = len(items)
        pa = {0: prepA(0)}
        if n_it > 1:
            pa[1] = prepA(1)
        pbs = {0: prepB(0, pa.pop(0))}
        for i in range(n_it):
            if i + 1 < n_it:
                pbs[i + 1] = prepB(i + 1, pa.pop(i + 1))
            if i + 2 < n_it:
                pa[i + 2] = prepA(i + 2)
            ready, csl = pbs.pop(i)
            run_item(i, ready, csl)
        kx.barrier()


def phase2_na(kx):
    nc, cfg, d = kx.nc, kx.cfg, kx.dram
    PE, A, V = kx.PE, kx.ACT, kx.DVE
    scale = 128.0 ** -0.5
    es = contextlib.ExitStack()
    with es:
        psS = [Bank(es.enter_context(nc.psum_tensor(f"naS{i}", [128, 1024], F32))[:]) for i in range(2)]
        pbT = [Bank(es.enter_context(nc.psum_tensor(f"naT{i}", [128, 1024], BF16))[:]) for i in range(2)]
        psO = Bank(es.enter_context(nc.psum_tensor("naO", [128, 1024], F32))[:])
        tabI = Slot(nc, es, "naTabI", [128, 8, 640], F32)
        tabE = [Slot(nc, es, f"naTabE{i}", [128, 8, 640], F32) for i in range(2)]
        Qs = [Slot(nc, es, f"naQ{i}", [128, 8, 128], BF16) for i in range(2)]
        Ks = [Slot(nc, es, f"naK{i}", [128, 8, 640], BF16) for i in range(2)]
        Vs = [Slot(nc, es, f"naV{i}", [128, 5, 1024], BF16) for i in range(2)]
        sb = [sbuf(kx, f"naSb{i}", [128, 640], F32, es) for i in range(4)]
        sb_free = [None] * 4
        p = [sbuf(kx, f"naP{i}", [128, 8, 640], BF16, es) for i in range(2)]
        p_free = [None, None]
        ssum = sbuf(kx, "naSum", [128, 16], F32, es)
        rs = sbuf(kx, "naRs", [128, 16], F32, es)
        pT = [sbuf(kx, f"naPT{i}", [128, 5, 128], BF16, es) for i in range(2)]
        pT_free = [None, None]
        ost = [Slot(nc, es, f"naOst{i}", [128, 8, 128], BF16, "st") for i in range(2)]
        tI = kx.load(tabI, tabI.t[:], d["nabi"])
        PW = 2048
        NWS = 3
        pieces = []
        if NA_BACKGROUND_CAST and 0 in cfg.phases:
            wfin = [Slot(nc, es, f"nawf{i}", [128, PW], F32) for i in range(NWS)]
            wfout = [Slot(nc, es, f"nawb{i}", [128, PW], BF16, "st") for i in range(NWS)]
            for name, R, C in weight_specs(cfg):
                if name == "w_in":
                    continue
                for kc in range(R // 128):
                    for c0 in range(0, C, PW):
                        pieces.append((name, kc, c0, min(PW, C - c0)))
        wstate = dict(k=0, loaded={})

        def wcast_tick():
            k = wstate["k"]
            wstate["k"] = k + 1
            if k < len(pieces):
                name, kc, c0, cw = pieces[k]
                si = wfin[k % NWS]
                wstate["loaded"][k] = kx.load(si, si.t[:, 0:cw], d[name][kc * 128:(kc + 1) * 128, c0:c0 + cw])
            pk = k - 2
            if 0 <= pk < len(pieces):
                name, kc, c0, cw = pieces[pk]
                si, so = wfin[pk % NWS], wfout[pk % NWS]
                E = A if (pk % 2) else V
                E.wait(wstate["loaded"].pop(pk), so.busy)
                so.busy = []
                if E is A:
                    ins = E.e.activation(out=so.t[:, 0:cw], in_=si.t[:, 0:cw], func=AF.Copy)
                else:
                    ins = E.e.tensor_copy(out=so.t[:, 0:cw], in_=si.t[:, 0:cw])
                ct = E.mark(ins)
                si.busy = [ct]
                dst = d["Wb_" + name[2:]]
                if name == "w_down":
                    fb_, kcl = kc // 4, kc % 4
                    dap = dst[fb_, :, kcl, c0:c0 + cw]
                    sap = so.t[:, 0:cw]
                else:
                    nb0, nbn = c0 // 512, cw // 512
                    dap = dst[nb0:nb0 + nbn, :, kc, :].rearrange("nb p c -> p nb c")
                    sap = so.t[:, 0:cw].rearrange("p (nb c) -> p nb c", c=512)
                kx.store(so, dap, sap, ct)

        def wcast_pending():
            return wstate["k"] < len(pieces) + 2

        tiles = []
        for j, (T, OWN) in enumerate(cfg.jobs):
            nt = OWN // 128
            for jt in range(nt):
                e = {0: 0, 1: 1, nt - 2: 2, nt - 1: 3}.get(jt)
                tiles.append((j, jt, e))
        state = {}
        cnt = dict(e=0, sbi=0)

        def stage1(i):
            j, jt, e = tiles[i]
            par = i % 2
            q, k, v = Qs[par], Ks[par], Vs[par]
            kx.load(q, q.t[:], d[f"NAQT{j}"][:, :, jt * 128:(jt + 1) * 128].rearrange("h p q -> p h q"))
            tq = kx.load(k, k.t[:], d[f"NAKT{j}"][:, :, jt * 128:jt * 128 + 640].rearrange("h p q -> p h q"))
            tk = k.tok()
            tq = q.tok()
            tv = kx.load(v, v.t[:], d[f"NAV{j}"][jt * 128:jt * 128 + 640, :].rearrange("(k p) c -> p k c", p=128))
            if e is None:
                tab, ttab = tabI, tI
            else:
                tab = tabE[cnt["e"] % 2]
                cnt["e"] += 1
                ttab = kx.load(tab, tab.t[:], d[f"nab{j}"][e])
            t4s = []
            lastS = None
            lastT1 = None
            for h in range(8):
                bk = psS[h % 2]
                PE.wait(tq, tk, bk.free)
                PE.e.matmul(bk.ap[:, 0:512], lhsT=q.t[:, h, :], rhs=k.t[:, h, 0:512], start=True, stop=True)
                tp = PE.mark(PE.e.matmul(bk.ap[:, 512:640], lhsT=q.t[:, h, :], rhs=k.t[:, h, 512:640],
                                         start=True, stop=True))
                lastS = tp
                si = cnt["sbi"] % 4
                cnt["sbi"] += 1
                V.wait(tp, ttab, sb_free[si])
                t1 = V.mark(V.e.scalar_tensor_tensor(out=sb[si][:], in0=bk.ap[:, 0:640], scalar=scale,
                                                     in1=tab.t[:, h, :], op0=ALU.mult, op1=ALU.add))
                bk.free = t1
                lastT1 = t1
                col = (i % 2) * 8 + h
                A.wait(t1, p_free[par])
                t2 = A.mark(A.e.activation(out=p[par][:, h, :], in_=sb[si][:], func=AF.Exp,
                                           accum_out=ssum[:, col:col + 1]))
                sb_free[si] = t2
                V.wait(t2)
                t3 = V.mark(V.e.reciprocal(out=rs[:, col:col + 1], in_=ssum[:, col:col + 1]))
                V.wait(t3)
                t4 = V.mark(V.e.tensor_scalar(out=p[par][:, h, :], in0=p[par][:, h, :],
                                              scalar1=rs[:, col:col + 1], scalar2=None, op0=ALU.mult))
                t4s.append(t4)
                if pieces and wcast_pending():
                    wcast_tick()
            p_free[par] = None
            q.busy = [lastS]
            k.busy = [lastS]
            if e is not None:
                tab.busy = [lastT1]
            state[i] = dict(t4s=t4s, tv=tv, v=v)

        def stage2(i):
            j, jt, e = tiles[i]
            par = i % 2
            stt = state.pop(i)
            v = stt["v"]
            last = None
            for h in range(8):
                bk = pbT[h % 2]
                PE.wait(stt["t4s"][h], bk.free)
                for k5 in range(5):
                    ins = PE.e.transpose(bk.ap[:, k5 * 128:(k5 + 1) * 128], p[par][:, h, k5 * 128:(k5 + 1) * 128],
                                         kx.c["ident"][:])
                tp2 = PE.mark(ins)
                E = kx.alt()
                E.wait(tp2, pT_free[h % 2])
                src = bk.ap[:, 0:640].rearrange("p (k q) -> p k q", q=128)
                if E is A:
                    ins = E.e.activation(out=pT[h % 2][:], in_=src, func=AF.Copy)
                else:
                    ins = E.e.tensor_copy(out=pT[h % 2][:], in_=src)
                t5 = E.mark(ins)
                bk.free = t5
                PE.wait(t5, stt["tv"], psO.free if h == 0 else None)
                for k5 in range(5):
                    ins = PE.e.matmul(psO.ap[:, h * 128:(h + 1) * 128], lhsT=v.t[:, k5, h * 128:(h + 1) * 128],
                                      rhs=pT[h % 2][:, k5, :], start=(k5 == 0), stop=(k5 == 4))
                tp3 = PE.mark(ins)
                pT_free[h % 2] = tp3
                last = tp3
            p_free[par] = last
            v.busy = [last]
            o = ost[i % 2]
            E = kx.alt()
            E.wait(last, o.busy)
            o.busy = []
            src = psO.ap.rearrange("p (h q) -> p h q", q=128)
            if E is A:
                ins = E.e.activation(out=o.t[:], in_=src, func=AF.Copy)
            else:
                ins = E.e.tensor_copy(out=o.t[:], in_=src)
            t6 = E.mark(ins)
            psO.free = t6
            kx.store(o, d[f"MIXT{j}"][0:8, :, jt * 128:(jt + 1) * 128].rearrange("h p q -> p h q"), o.t[:], t6)

        n = len(tiles)
        stage1(0)
        for i in range(n):
            if i + 1 < n:
                stage1(i + 1)
            stage2(i)
        while pieces and wcast_pending():
            wcast_tick()
        kx.barrier()


def phase2_diff(kx):
    nc, cfg, d = kx.nc, kx.cfg, kx.dram
    PE, A, V = kx.PE, kx.ACT, kx.DVE
    scale = 64.0 ** -0.5
    c = kx.c
    Tmax = max(T for T, _ in cfg.jobs)
    es = contextlib.ExitStack()
    with es:
        psS = [Bank(es.enter_context(nc.psum_tensor(f"dfS{i}", [128, 1024], F32))[:]) for i in range(2)]
        psO1 = Bank(es.enter_context(nc.psum_tensor("dfO1", [128, 512], F32))[:])
        psO2 = Bank(es.enter_context(nc.psum_tensor("dfO2", [128, 512], F32))[:])
        psX = Bank(es.enter_context(nc.psum_tensor("dfX", [128, 512], F32))[:])
        KTs = [Slot(nc, es, f"dfK{i}", [128, Tmax], BF16) for i in range(2)]
        VHs = [Slot(nc, es, f"dfV{i}", [128, Tmax // 128, 128], BF16) for i in range(2)]
        QTs = [Slot(nc, es, f"dfQ{i}", [128, 512], BF16) for i in range(2)]
        NPT = 4
        PT = [sbuf(kx, f"dfPT{i}", [128, 1024], BF16, es) for i in range(NPT)]
        PT_pe = [None] * NPT
        PT_dve = [None] * NPT
        acc = [sbuf(kx, f"dfacc{i}", [128, 1024], F32, es) for i in range(2)]
        acc_free = [None, None]
        acc_tok = [None, None]
        tmpb = [sbuf(kx, f"dftmp{i}", [128, 1024], BF16, es) for i in range(3)]
        tmp_tok = [None, None, None]
        pair_tok = [None, None]
        o1 = sbuf(kx, "dfo1", [128, 512], F32, es)
        o2 = sbuf(kx, "dfo2", [128, 512], F32, es)
        tt_ = sbuf(kx, "dft", [128, 512], F32, es)
        uu = sbuf(kx, "dfu", [128, 512], F32, es)
        oo = sbuf(kx, "dfo", [128, 512], F32, es)
        sq = sbuf(kx, "dfsq", [128, 512], F32, es)
        rt = sbuf(kx, "dfrt", [128, 512], F32, es)
        ost = [Slot(nc, es, f"dfOst{i}", [128, 512], BF16, "st") for i in range(2)]

        heads = [(j, h) for j in range(len(cfg.jobs)) for h in range(8)]
        steps = []
        for hi, (j, h) in enumerate(heads):
            T, OWN = cfg.jobs[j]
            for qc in range(OWN // 512):
                for kt in range(T // 128):
                    steps.append((hi, j, h, qc, kt))
        N = len(steps)
        kv_tok = {}
        q_tok = {}
        cnt = dict(q=0, ost=0, pair=0, grp=0)

        PW = 2048
        P = kx.POOL
        wfin = [Slot(nc, es, f"dfwf{i}", [128, PW if BACKGROUND_CAST else 8], F32, "st") for i in range(2)]
        wfout = [Slot(nc, es, f"dfwb{i}", [128, PW if BACKGROUND_CAST else 8], BF16, "st") for i in range(2)]
        pieces = []
        if BACKGROUND_CAST and 0 in cfg.phases:
            for name, R, C in weight_specs(cfg):
                if name == "w_in":
                    continue
                for kc in range(R // 128):
                    for c0 in range(0, C, PW):
                        pieces.append((name, kc, c0, min(PW, C - c0)))
        wstate = dict(k=0, prev=None)

        def wcast_tick():
            k = wstate["k"]
            prev = wstate["prev"]
            if k < len(pieces):
                name, kc, c0, cw = pieces[k]
                si = wfin[k % 2]
                lt = kx.load(si, si.t[:, 0:cw], d[name][kc * 128:(kc + 1) * 128, c0:c0 + cw], eng=P)
                wstate["prev"] = (k, lt)
                wstate["k"] = k + 1
            else:
                wstate["prev"] = None
            if prev is not None:
                pk, plt = prev
                name, kc, c0, cw = pieces[pk]
                si, so = wfin[pk % 2], wfout[pk % 2]
                P.wait(plt, so.busy)
                so.busy = []
                ct = P.mark(P.e.tensor_copy(out=so.t[:, 0:cw], in_=si.t[:, 0:cw]))
                si.busy = [ct]
                dst = d["Wb_" + name[2:]]
                if name == "w_down":
                    fb_, kcl = kc // 4, kc % 4
                    dap = dst[fb_, :, kcl, c0:c0 + cw]
                    sap = so.t[:, 0:cw]
                else:
                    nb0, nbn = c0 // 512, cw // 512
                    dap = dst[nb0:nb0 + nbn, :, kc, :].rearrange("nb p c -> p nb c")
                    sap = so.t[:, 0:cw].rearrange("p (nb c) -> p nb c", c=512)
                kx.store(so, dap, sap, ct)

        def wcast_pending():
            return wstate["k"] < len(pieces) or wstate["prev"] is not None

        def load_kv(hi):
            j, h = heads[hi]
            T = cfg.jobs[j][0]
            ks, vs = KTs[hi % 2], VHs[hi % 2]
            NQ = 4 if T >= 2048 else 1
            w = T // NQ
            for i in range(NQ):
                kx.load(ks, ks.t[:, i * w:(i + 1) * w], d[f"KT{j}"][h, :, i * w:(i + 1) * w], first=(i == 0))
            for i in range(NQ):
                kx.load(vs, vs.t[:, i * (w // 128):(i + 1) * (w // 128), :],
                        d[f"VH{j}"][h, :, i * (w // 128):(i + 1) * (w // 128), :], first=(i == 0))
            kv_tok[hi] = (ks.tok(), vs.tok())

        def load_q(hi, qc):
            j, h = heads[hi]
            s = QTs[cnt["q"] % 2]
            cnt["q"] += 1
            t = kx.load(s, s.t[:], d[f"QT{j}"][h, :, qc * 512:(qc + 1) * 512])
            q_tok[(hi, qc)] = (s, t)

        tok_qk = {}
        tok_exp = {}
        pend = []
        ep_free = dict(o=None)
        Obanks_free = [None]

        def emit_qk(n):
            hi, j, h, qc, kt = steps[n]
            ks = KTs[hi % 2]
            s, tq = q_tok[(hi, qc)]
            bk = psS[n % 2]
            PE.wait(kv_tok[hi][0], tq, bk.free)
            PE.e.matmul(bk.ap[:, 0:512], lhsT=ks.t[0:64, kt * 128:(kt + 1) * 128], rhs=s.t[0:64, :],
                        start=True, stop=True)
            tok_qk[n] = PE.mark(PE.e.matmul(bk.ap[:, 512:1024], lhsT=ks.t[64:128, kt * 128:(kt + 1) * 128],
                                            rhs=s.t[64:128, :], start=True, stop=True))
            T = cfg.jobs[j][0]
            if kt == T // 128 - 1:
                s.busy = [tok_qk[n]]
                if qc == cfg.jobs[j][1] // 512 - 1:
                    ks.busy = [tok_qk[n]]

        def emit_exp(n):
            bk = psS[n % 2]
            A.wait(tok_qk.pop(n), PT_pe[n % NPT], PT_dve[n % NPT])
            tok_exp[n] = A.mark(A.e.activation(out=PT[n % NPT][:], in_=bk.ap, func=AF.Exp, scale=scale))
            bk.free = tok_exp[n]

        def emit_av(n):
            hi, j, h, qc, kt = steps[n]
            T, OWN = cfg.jobs[j]
            NT = T // 128
            vs = VHs[hi % 2]
            pt = PT[n % NPT]
            PE.wait(tok_exp[n], kv_tok[hi][1], Obanks_free[0] if kt == 0 else None)
            st_, sp_ = (kt == 0), (kt == NT - 1)
            PE.e.matmul(psO1.ap, lhsT=vs.t[:, kt, :], rhs=pt[:, 0:512], start=st_, stop=sp_)
            tp = PE.mark(PE.e.matmul(psO2.ap, lhsT=vs.t[:, kt, :], rhs=pt[:, 512:1024], start=st_, stop=sp_))
            PT_pe[n % NPT] = tp
            if sp_:
                while pend:
                    pend.pop(0)()
            tk0 = epilogue_s0(tp) if sp_ else None
            if kt % 2 == 1:
                g = cnt["grp"] % 2
                pi = (kt // 2) % 2
                V.wait(tok_exp[n - 1], tok_exp[n], tmp_tok[pi])
                t1 = V.mark(V.e.tensor_tensor(out=tmpb[pi][:], in0=PT[(n - 1) % NPT][:], in1=pt[:], op=ALU.add))
                PT_dve[(n - 1) % NPT] = t1
                PT_dve[n % NPT] = t1
                pair_tok[pi] = t1
                tok_exp.pop(n - 1, None)
                tok_exp.pop(n, None)
                if kt % 4 == 3:
                    V.wait(pair_tok[0], pair_tok[1], tmp_tok[2])
                    t3 = V.mark(V.e.tensor_tensor(out=tmpb[2][:], in0=tmpb[0][:], in1=tmpb[1][:], op=ALU.add))
                    tmp_tok[0] = t3
                    tmp_tok[1] = t3
                    V.wait(t3, acc_tok[g], acc_free[g] if kt == 3 else None)
                    if kt == 3:
                        t2 = V.mark(V.e.tensor_copy(out=acc[g][:], in_=tmpb[2][:]))
                    else:
                        t2 = V.mark(V.e.tensor_tensor(out=acc[g][:], in0=acc[g][:], in1=tmpb[2][:], op=ALU.add))
                    acc_tok[g] = t2
                    tmp_tok[2] = t2
            if sp_:
                if qc == OWN // 512 - 1:
                    vs.busy = [tp]
                g = cnt["grp"] % 2
                cnt["grp"] += 1
                epilogue(j, h, qc, tk0, g, acc_tok[g])

        def epilogue_s0(tp):
            tk = {}
            V.wait(tp, ep_free["o"])
            tk["e1"] = V.mark(V.e.tensor_copy(out=o1[:], in_=psO1.ap))
            A.wait(tp, ep_free["o"])
            tk["e2"] = A.mark(A.e.activation(out=o2[:], in_=psO2.ap, func=AF.Copy))
            Obanks_free[0] = [tk["e1"], tk["e2"]]
            return tk

        def epilogue(j, h, qc, tk, g, tacc):
            def k1():
                PE.wait(tacc, psX.free)
                tk["r1"] = PE.mark(PE.e.matmul(psX.ap, lhsT=c["ones_f"][:], rhs=acc[g][:, 0:512], start=True, stop=True))

            def k2():
                V.wait(tk["r1"])
                tk["rc1"] = V.mark(V.e.reciprocal(out=sq[:], in_=psX.ap))
                psX.free = tk["rc1"]
                V.wait(tk["rc1"], tk["e1"])
                tk["t"] = V.mark(V.e.tensor_tensor(out=tt_[:], in0=o1[:], in1=sq[:], op=ALU.mult))

            def k3():
                PE.wait(psX.free)
                tk["r2"] = PE.mark(PE.e.matmul(psX.ap, lhsT=c["ones_f"][:], rhs=acc[g][:, 512:1024], start=True, stop=True))
                acc_free[g] = tk["r2"]

            def k4():
                V.wait(tk["r2"], tk["t"])
                tk["rc2"] = V.mark(V.e.reciprocal(out=sq[:], in_=psX.ap))
                psX.free = tk["rc2"]
                V.wait(tk["rc2"], tk["e2"])
                tk["u"] = V.mark(V.e.tensor_tensor(out=uu[:], in0=o2[:], in1=sq[:], op=ALU.mult))
                V.wait(tk["u"], tk["t"])
                tk["o"] = V.mark(V.e.scalar_tensor_tensor(out=oo[:], in0=uu[:], scalar=c["nlam"][:, 0:1],
                                                          in1=tt_[:], op0=ALU.mult, op1=ALU.add))
                V.wait(tk["o"])
                tk["sq"] = V.mark(V.e.tensor_tensor(out=sq[:], in0=oo[:], in1=oo[:], op=ALU.mult))

            def k5():
                PE.wait(tk["sq"], psX.free)
                tk["ss"] = PE.mark(PE.e.matmul(psX.ap, lhsT=c["ones_f"][:], rhs=sq[:], start=True, stop=True))

            def k6():
                A.wait(tk["ss"])
                tk["ln"] = A.mark(A.e.activation(out=rt[:], in_=psX.ap, func=AF.Ln, scale=1.0 / 128,
                                                 bias=c["eps_sub"][:]))
                psX.free = tk["ln"]
                A.wait(tk["ln"])
                tk["rs"] = A.mark(A.e.activation(out=rt[:], in_=rt[:], func=AF.Exp, scale=-0.5))

            def k7():
                s = ost[cnt["ost"] % 2]
                cnt["ost"] += 1
                V.wait(tk["rs"], s.busy)
                s.busy = []
                tk["on"] = V.mark(V.e.scalar_tensor_tensor(out=s.t[:], in0=oo[:], scalar=c["gsub"][:, 0:1],
                                                           in1=rt[:], op0=ALU.mult, op1=ALU.mult))
                ep_free["o"] = tk["on"]
                kx.store(s, d[f"MIXT{j}"][8 + h, :, qc * 512:(qc + 1) * 512], s.t[:], tk["on"])

            pend.extend([k1, k2, k3, k4, k5, k6, k7])

        load_kv(0)
        load_q(0, 0)

        def prefetch_for(n):
            hi, j, h, qc, kt = steps[n]
            if kt == 0:
                nq = cfg.jobs[j][1] // 512
                if qc + 1 < nq:
                    load_q(hi, qc + 1)
                elif hi + 1 < len(heads):
                    load_q(hi + 1, 0)
                if qc == 0 and hi + 1 < len(heads):
                    load_kv(hi + 1)

        prefetch_for(0)
        emit_qk(0)
        if N > 1:
            emit_qk(1)
        for n in range(N):
            hi, j, h, qc, kt = steps[n]
            if n > 0:
                prefetch_for(n)
            emit_exp(n)
            if n + 2 < N:
                emit_qk(n + 2)
            emit_av(n)
            if pend and kt >= 1 and kt % 2 == 0:
                pend.pop(0)()
            if n % 4 == 1 and wcast_pending():
                wcast_tick()
        while wcast_pending():
            wcast_tick()
        while pend:
            pend.pop(0)()
        kx.barrier()


def phase3(kx):
    nc, cfg, d = kx.nc, kx.cfg, kx.dram
    D, KC, DFF, MC, NFB = cfg.D, cfg.KC, cfg.DFF, cfg.MC, cfg.NFB
    PE, A, V = kx.PE, kx.ACT, kx.DVE
    c = kx.c
    NCB = D // 512
    mscale = float(cfg.MEMHD) ** -0.5
    es = contextlib.ExitStack()
    with es:
        NB3 = 6
        banks = [Bank(es.enter_context(nc.psum_tensor(f"p3s{i}", [128, 512], F32))[:]) for i in range(NB3)]
        pbt = [es.enter_context(nc.psum_tensor(f"p3b{i}", [128, 1024], BF16)) for i in range(2)]
        pb = [Bank(pbt[0][:, 0:512]), Bank(pbt[1][:, 0:512])]
        ncx = NormCtx(kx, es, pb, nx=0)
        h = Slot(nc, es, "p3h", [128, 4, D], F32)
        fa = Slot(nc, es, "p3fa", [128, 16, 512], BF16)
        fb = sbuf(kx, "p3fb", [128, KC, 512], BF16, es)
        KmT = sbuf(kx, "p3KmT", [128, KC, 256], BF16, es)
        Vm = sbuf(kx, "p3Vm", [128, 2, D], BF16, es)
        PTm = [sbuf(kx, f"p3PTm{i}", [128, 2, 512], BF16, es) for i in range(2)]
        Rm = [sbuf(kx, f"p3Rm{i}", [128, 512], F32, es) for i in range(2)]
        actT = [sbuf(kx, f"p3act{i}", [128, 4, 512], BF16, es) for i in range(2)]
        sg = [sbuf(kx, f"p3sg{i}", [128, 512], F32, es) for i in range(2)]
        NW = 4
        wr = [Slot(nc, es, f"p3w{i}", [128, 8192], BF16) for i in range(NW)]
        gfin = Slot(nc, es, "p3gfin", [128, D], F32)
        ystat = sbuf(kx, "p3ystat", [128, 8], F32, es)
        yrstd = sbuf(kx, "p3yrstd", [128, 8], F32, es)
        tg = kx.load(gfin, gfin.t[:], d["g_final"])
        ysems = [kx.ysem, kx.ysem2]
        ys_tok = [None, None]
        st = dict(bi=0, wi=0, pi=0, ai=0, si=0, yi=0)
        htok = {}
        fa_free = [None]
        fb_free = [None]
        PTm_free = [None, None]
        Rm_free = [None, None]
        act_free = [None, None]
        sg_free = [None, None]

        def next_bank():
            b = banks[st["bi"] % NB3]
            st["bi"] += 1
            return b

        def lw(name, idx, kc_n, cols):
            s = wr[st["wi"] % NW]
            st["wi"] += 1
            view = s.t[:, 0:kc_n * cols].rearrange("p (k c) -> p k c", c=cols)
            t = kx.load(s, view, d[name][idx])
            return s, t, view

        def h_tiles(ntt, extra_toks=None):
            def mk(tt):
                def f():
                    toks = [htok.get((tt, cb)) for cb in range(NCB)]
                    if extra_toks:
                        toks = toks + list(extra_toks)

                    def rel(tk):
                        for cb in range(NCB):
                            htok[(tt, cb)] = list(tk)
                    return h.t[:, tt, :], toks, rel
                return f
            return [mk(tt) for tt in range(ntt)]

        def evac_copy(dst, bk, tp, extra=None):
            E = kx.alt()
            E.wait(tp, extra)
            if E is A:
                ins = E.e.activation(out=dst, in_=bk.ap if not isinstance(bk, tuple) else bk[0], func=AF.Copy)
            else:
                ins = E.e.tensor_copy(out=dst, in_=bk.ap if not isinstance(bk, tuple) else bk[0])
            te = E.mark(ins)
            return te

        def proj_tm_add(src, src_toks, nk, wname):
            last = None
            for cb in range(NCB):
                ws, wt, wv = lw(wname, cb, nk, 512)
                for tt in range(4):
                    bk = next_bank()
                    PE.wait(wt, src_toks, bk.free)
                    for kc in range(nk):
                        ins = PE.e.matmul(bk.ap, lhsT=src[:, kc, tt * 128:(tt + 1) * 128], rhs=wv[:, kc, :],
                                          start=(kc == 0), stop=(kc == nk - 1))
                    tp = PE.mark(ins)
                    last = tp
                    V.wait(tp, htok.get((tt, cb)))
                    reg = h.t[:, tt, cb * 512:(cb + 1) * 512]
                    ta = V.mark(V.e.tensor_tensor(out=reg, in0=bk.ap, in1=reg, op=ALU.add))
                    bk.free = ta
                    htok[(tt, cb)] = [ta]
                ws.busy = [last]
            return last

        for j, (T, OWN) in enumerate(cfg.jobs):
            kx.load(h, h.t[:, 0, :], d[f"mem{j}"][0:128, :])
            tm = kx.load(h, h.t[:, 1, :], d[f"mem{j}"][128:256, :], first=False)
            htok.clear()
            mtoks = make_xnT(kx, ncx, h_tiles(2, [tm]), c["g_mem"], fb, fb_free[0], ntt=2)
            fb_free[0] = None
            last = None
            kdone = []
            for m in range(KC):
                if m % 4 == 0:
                    ws, wt, wv = lw("Wb_mkv", m // 4, KC, 512)
                bk = next_bank()
                PE.wait(wt, mtoks, bk.free)
                for kc in range(KC):
                    ins = PE.e.matmul(bk.ap[:, 0:256], lhsT=wv[:, kc, (m % 4) * 128:(m % 4 + 1) * 128],
                                      rhs=fb[:, kc, 0:256], start=(kc == 0), stop=(kc == KC - 1))
                tp = PE.mark(ins)
                last = tp
                te = evac_copy(KmT[:, m, :], (bk.ap[:, 0:256],), tp, fa_free[0] if m == 0 else None)
                bk.free = te
                kdone.append(te)
                if m % 4 == 3 or m == KC - 1:
                    ws.busy = [last]
            for cb in range(NCB):
                ws, wt, wv = lw("Wb_mkv", NCB + cb, KC, 512)
                for tt in range(2):
                    bk = next_bank()
                    PE.wait(wt, mtoks, bk.free)
                    for kc in range(KC):
                        ins = PE.e.matmul(bk.ap, lhsT=fb[:, kc, tt * 128:(tt + 1) * 128], rhs=wv[:, kc, :],
                                          start=(kc == 0), stop=(kc == KC - 1))
                    tp = PE.mark(ins)
                    last = tp
                    te = evac_copy(Vm[:, tt, cb * 512:(cb + 1) * 512], bk, tp)
                    bk.free = te
                    kdone.append(te)
                ws.busy = [last]
            fb_free[0] = [last]
            mem_ready = kdone[-2:] + kdone[KC - 2:KC]
            for b in range(OWN // 512):
                t0 = b * 512
                fa.busy = list(fa_free[0] or []) if fa_free[0] else []
                tmix = kx.load(fa, fa.t[:], d[f"MIXT{j}"][:, :, t0:t0 + 512].rearrange("c p q -> p c q"))
                h.busy = h.busy + [t for v_ in htok.values() if v_ for t in v_]
                for tt in range(4):
                    tx = kx.load(h, h.t[:, tt, :], d[f"x{j}"][t0 + tt * 128:t0 + (tt + 1) * 128, :], first=(tt == 0),
                                 eng=kx.ACT)
                htok.clear()
                for tt in range(4):
                    for cb in range(NCB):
                        htok[(tt, cb)] = [tx]
                last = proj_tm_add(fa.t, [tmix], 16, "Wb_out")
                fa_free[0] = [last]
                n1 = make_xnT(kx, ncx, h_tiles(4), c["g_xattn"], fb, fb_free[0])
                qdone = []
                for m in range(KC):
                    if m % 4 == 0:
                        ws, wt, wv = lw("Wb_mq", m // 4, KC, 512)
                    bk = next_bank()
                    PE.wait(wt, n1, bk.free)
                    for kc in range(KC):
                        ins = PE.e.matmul(bk.ap, lhsT=wv[:, kc, (m % 4) * 128:(m % 4 + 1) * 128], rhs=fb[:, kc, :],
                                          start=(kc == 0), stop=(kc == KC - 1))
                    tp = PE.mark(ins)
                    last = tp
                    te = evac_copy(fa.t[:, m, :], bk, tp, fa_free[0] if m == 0 else None)
                    bk.free = te
                    qdone.append(te)
                    if m % 4 == 3 or m == KC - 1:
                        ws.busy = [last]
                fb_free[0] = [last]
                qtoks = qdone[-2:]
                lastS = None
                for hm in range(4):
                    pi = st["pi"] % 2
                    st["pi"] += 1
                    pt = PTm[pi]
                    texp = []
                    for kc2 in range(2):
                        bk = next_bank()
                        PE.wait(qtoks, mem_ready, bk.free)
                        for dc in range(MC):
                            ins = PE.e.matmul(bk.ap, lhsT=KmT[:, hm * MC + dc, kc2 * 128:(kc2 + 1) * 128],
                                              rhs=fa.t[:, hm * MC + dc, :], start=(dc == 0), stop=(dc == MC - 1))
                        tp = PE.mark(ins)
                        lastS = tp
                        A.wait(tp, PTm_free[pi])
                        te = A.mark(A.e.activation(out=pt[:, kc2, :], in_=bk.ap, func=AF.Exp, scale=mscale))
                        bk.free = te
                        texp.append(te)
                    PTm_free[pi] = None
                    bks = next_bank()
                    PE.wait(texp, bks.free)
                    PE.e.matmul(bks.ap, lhsT=c["ones_bf"][:], rhs=pt[:, 0, :], start=True, stop=False)
                    tps = PE.mark(PE.e.matmul(bks.ap, lhsT=c["ones_bf"][:], rhs=pt[:, 1, :], start=False, stop=True))
                    V.wait(tps, Rm_free[pi])
                    tr = V.mark(V.e.reciprocal(out=Rm[pi][:], in_=bks.ap))
                    bks.free = tr
                    lastO = None
                    for dvc in range(MC):
                        ch = hm * MC + dvc
                        bk = next_bank()
                        PE.wait(bk.free)
                        PE.e.matmul(bk.ap, lhsT=Vm[:, 0, ch * 128:(ch + 1) * 128], rhs=pt[:, 0, :], start=True, stop=False)
                        tp = PE.mark(PE.e.matmul(bk.ap, lhsT=Vm[:, 1, ch * 128:(ch + 1) * 128], rhs=pt[:, 1, :],
                                                 start=False, stop=True))
                        lastO = tp
                        V.wait(tp, tr, fb_free[0])
                        to = V.mark(V.e.tensor_tensor(out=fb[:, ch, :], in0=bk.ap, in1=Rm[pi][:], op=ALU.mult))
                        bk.free = to
                    PTm_free[pi] = lastO
                    Rm_free[pi] = to
                fb_free[0] = None
                fa_free[0] = [lastS]
                om_toks = [to]
                last = proj_tm_add(fb, om_toks, KC, "Wb_mo")
                fb_free[0] = [last]
                n2 = make_xnT(kx, ncx, h_tiles(4), c["g_ffn"], fa.t, fa_free[0])
                lastG = None
                for fblk in range(NFB):
                    wg, tg_, vg = lw("Wb_gu", fblk, KC, 512)
                    wu, tu_, vu = lw("Wb_gu", NFB + fblk, KC, 512)
                    wd, td_, vd = lw("Wb_down", fblk, 4, D)
                    ai = st["ai"] % 2
                    st["ai"] += 1
                    at = actT[ai]
                    tacts = []
                    for cc in range(4):
                        bg = next_bank()
                        PE.wait(tg_, n2, bg.free)
                        for kc in range(KC):
                            ins = PE.e.matmul(bg.ap, lhsT=vg[:, kc, cc * 128:(cc + 1) * 128], rhs=fa.t[:, kc, :],
                                              start=(kc == 0), stop=(kc == KC - 1))
                        tpg = PE.mark(ins)
                        bu = next_bank()
                        PE.wait(tu_, bu.free)
                        for kc in range(KC):
                            ins = PE.e.matmul(bu.ap, lhsT=vu[:, kc, cc * 128:(cc + 1) * 128], rhs=fa.t[:, kc, :],
                                              start=(kc == 0), stop=(kc == KC - 1))
                        tpu = PE.mark(ins)
                        lastG = tpu
                        si = st["si"] % 2
                        st["si"] += 1
                        A.wait(tpg, sg_free[si])
                        tsg = A.mark(A.e.activation(out=sg[si][:], in_=bg.ap, func=AF.Silu))
                        bg.free = tsg
                        V.wait(tsg, tpu, act_free[ai] if cc == 0 else None)
                        tact = V.mark(V.e.tensor_tensor(out=at[:, cc, :], in0=bu.ap, in1=sg[si][:], op=ALU.mult))
                        bu.free = tact
                        sg_free[si] = tact
                        tacts.append(tact)
                    wg.busy = [lastG]
                    wu.busy = [lastG]
                    lastD = None
                    for cb in range(NCB):
                        for tt in range(4):
                            bk = next_bank()
                            PE.wait(td_, tacts[-1], bk.free)
                            for cc in range(4):
                                ins = PE.e.matmul(bk.ap, lhsT=at[:, cc, tt * 128:(tt + 1) * 128],
                                                  rhs=vd[:, cc, cb * 512:(cb + 1) * 512], start=(cc == 0), stop=(cc == 3))
                            tp = PE.mark(ins)
                            lastD = tp
                            V.wait(tp, htok.get((tt, cb)))
                            reg = h.t[:, tt, cb * 512:(cb + 1) * 512]
                            ta = V.mark(V.e.tensor_tensor(out=reg, in0=bk.ap, in1=reg, op=ALU.add))
                            bk.free = ta
                            htok[(tt, cb)] = [ta]
                    wd.busy = [lastD]
                    act_free[ai] = lastD
                fa_free[0] = [lastG]
                ystage = ncx.hns[0][:].rearrange("p a d -> p (a d)").bitcast(F32)
                stoks = []
                rdtoks = []
                for tt in range(4):
                    col = st["yi"] % 8
                    st["yi"] += 1
                    toks = [htok.get((tt, cb)) for cb in range(NCB)]
                    A.wait(ncx.junk_tok)
                    ta, tc = rms_rstd(kx, h.t[:, tt, :], ystat[:, col:col + 1], yrstd[:, col:col + 1], ncx.junk[:],
                                      D, toks)
                    ncx.junk_tok = ta
                    ys = ystage[:, (tt % 2) * D:(tt % 2 + 1) * D]
                    V.wait(ta, tc, tg, ncx.hn_free[0], ys_tok[tt % 2])
                    ty = V.mark(V.e.scalar_tensor_tensor(out=ys, in0=h.t[:, tt, :],
                                                         scalar=yrstd[:, col:col + 1], in1=gfin.t[:],
                                                         op0=ALU.mult, op1=ALU.mult))
                    rdtoks += [ta, ty]
                    kx.POOL.wait(ty)
                    kx.POOL.e.dma_start(out=d[f"y{j}"][t0 + tt * 128:t0 + (tt + 1) * 128, :],
                                        in_=ys).then_inc(ysems[tt % 2].sem, 16)
                    ysems[tt % 2].cnt += 16
                    stk = (ysems[tt % 2].sem, ysems[tt % 2].cnt)
                    ys_tok[tt % 2] = stk
                    kx.store_toks.append(stk)
                    stoks.append(stk)
                ncx.hn_free[0] = [ncx.hn_free[0], stoks[-1], stoks[-2]]
                htok.clear()
                h.busy = rdtoks
        kx.barrier()


def rope_tables(pos):
    inv = (1.0 / (10000.0 ** (np.arange(0, 64, 2, dtype=np.float32) / np.float32(64)))).astype(np.float32)
    ang = pos.astype(np.float32)[:, None] * inv[None, :]
    ang = np.concatenate([ang, ang, ang, ang], axis=-1)
    return np.ascontiguousarray(np.cos(ang).T.astype(np.float32)), \
        np.ascontiguousarray(np.sin(ang).T.astype(np.float32))


def na_table(rpb, j, nt):
    R = 2 * nt
    q = np.arange(128)
    r = 2 * j + q // 64
    cq = q % 64
    rs = np.clip(r - 4, 0, R - 8)
    cst = np.clip(cq - 8, 0, 64 - 16)
    tab = np.full((128, 8, 640), NEG, dtype=np.float32)
    for s in range(5):
        kt = j - 2 + s
        if j == 0 and s == 0:
            kt = 3
        if j == nt - 1 and s == 4:
            kt = nt - 4
        if kt < 0 or kt >= nt:
            continue
        k = np.arange(128)
        kr = 2 * kt + k // 64
        kcn = k % 64
        valid = ((kr[None, :] >= rs[:, None]) & (kr[None, :] < rs[:, None] + 8) &
                 (kcn[None, :] >= cst[:, None]) & (kcn[None, :] < cst[:, None] + 16))
        dr = np.clip(kr[None, :] - r[:, None] + 7, 0, 14)
        dc = np.clip(kcn[None, :] - cq[:, None] + 15, 0, 30)
        g = rpb[:, dr, dc]
        g = np.transpose(g, (1, 0, 2))
        blk = tab[:, :, s * 128:(s + 1) * 128]
        blk[...] = np.where(valid[:, None, :], g, blk)
    return tab


def rot_matrix():
    Rm = np.zeros((128, 128), dtype=np.float32)
    for p in range(128):
        if (p % 64) < 32:
            Rm[p + 32, p] = -1.0
        else:
            Rm[p - 32, p] = 1.0
    return Rm.astype(ml_dtypes.bfloat16)


def prepare_core_inputs(cfg, core, nparts, xs, mems, shared):
    m = dict(shared)
    for j, (T, OWN) in enumerate(cfg.jobs):
        part, npart = nparts[j]
        x = xs[j]
        a = part * OWN
        own = np.arange(a, a + OWN)
        rest = np.concatenate([np.arange(0, a), np.arange(a + OWN, T)])
        perm = np.concatenate([own, rest])
        m[f"x{j}"] = np.ascontiguousarray(x[perm])
        cosT, sinT = rope_tables(perm)
        m[f"cos{j}"] = cosT
        m[f"sin{j}"] = sinT
        nt = T // 128
        ta, tb = a // 128, (a + OWN) // 128
        xh = np.zeros((512, cfg.D), dtype=np.float32)

        def tile(k):
            return x[k * 128:(k + 1) * 128]
        if ta == 0:
            xh[0:128] = tile(3)
        else:
            xh[0:128] = tile(ta - 2)
            xh[128:256] = tile(ta - 1)
        if tb == nt:
            xh[384:512] = tile(nt - 4)
        else:
            xh[256:384] = tile(tb)
            xh[384:512] = tile(tb + 1)
        m[f"xh{j}"] = xh
        m[f"mem{j}"] = np.ascontiguousarray(mems[j])
        rpb = shared["_rpb"]
        m[f"nab{j}"] = np.stack([na_table(rpb, g, nt) for g in (ta, ta + 1, tb - 2, tb - 1)])
    del m["_rpb"]
    return m


def shared_inputs(cfg, inp):
    KC = cfg.KC
    sh = {}
    sh["w_in"] = np.ascontiguousarray(inp["w_in"][0])
    sh["w_out"] = np.ascontiguousarray(inp["w_out"][0])
    sh["w_mq"] = np.ascontiguousarray(inp["w_mq"][0])
    sh["w_mkv"] = np.ascontiguousarray(inp["w_mkv"][0])
    sh["w_mo"] = np.ascontiguousarray(inp["w_mo"][0])
    sh["w_gu"] = np.ascontiguousarray(inp["w_gate_up"][0])
    sh["w_down"] = np.ascontiguousarray(inp["w_down"][0])
    for g, src in (("g_mix", "g_mix"), ("g_xattn", "g_xattn"), ("g_mem", "g_mem"), ("g_ffn", "g_ffn")):
        sh[g] = np.ascontiguousarray(np.asarray(inp[src][0], dtype=np.float32).reshape(KC, 128).T)
    sh["g_final"] = np.ascontiguousarray(np.broadcast_to(np.asarray(inp["g_final"], dtype=np.float32)[None, :], (128, cfg.D)))
    sh["g_subln"] = np.ascontiguousarray(np.asarray(inp["g_subln"][0], dtype=np.float32).reshape(128, 1))
    lv = np.stack([inp["lam_q1"][0], inp["lam_k1"][0], inp["lam_q2"][0], inp["lam_k2"][0]]).astype(np.float32)
    sh["lamv"] = np.ascontiguousarray(np.broadcast_to(lv[None], (128, 4, 64)))
    sh["ident"] = np.eye(128, dtype=np.float32).astype(ml_dtypes.bfloat16)
    sh["rotm"] = rot_matrix()
    rpb = np.asarray(inp["rpb"][0], dtype=np.float32)
    sh["_rpb"] = rpb
    sh["nabi"] = na_table(rpb, 8, 32)
    return sh


_PROGRAM_CACHE = {}


def kernel(**inputs):
    cfg = Cfg()
    inp = {k: np.asarray(v) for k, v in inputs.items()}
    sh = shared_inputs(cfg, inp)
    in_maps = []
    for c in range(8):
        xs = [inp["x_prompt"][c // 4], inp["x_sample"][c // 2]]
        mems = [inp["mem_prompt"][c // 4], inp["mem_sample"][c // 2]]
        in_maps.append(prepare_core_inputs(cfg, c, [(c % 4, 4), (c % 2, 2)], xs, mems, sh))
    nc = build_program(cfg)
    res = run_bass_kernel_spmd(nc, in_maps, core_ids=list(range(8)))
    yp = np.zeros((2, 16384, cfg.D), dtype=np.float32)
    ysm = np.zeros((4, 4096, cfg.D), dtype=np.float32)
    for c in range(8):
        r = res.results[c]
        yp[c // 4, (c % 4) * 4096:(c % 4 + 1) * 4096] = r["y0"]
        ysm[c // 2, (c % 2) * 2048:(c % 2 + 1) * 2048] = r["y1"]
    return (yp, ysm)
```

```python
import contextlib
import math

import numpy as np
import ml_dtypes

import concourse.bass as bass
import concourse.mybir as mybir
from concourse.bass_utils import run_bass_kernel_spmd

F32 = mybir.dt.float32
BF16 = mybir.dt.bfloat16
AF = mybir.ActivationFunctionType
ALU = mybir.AluOpType
AX = mybir.AxisListType

NEG = -30000.0
NA_BACKGROUND_CAST = True
BACKGROUND_CAST = False
RMS_EPS = 1e-6
SUBLN_EPS = 1e-5
LAM_INIT = 0.8 - 0.6 * math.exp(-0.3 * 0)


class Cfg:
    def __init__(self, D=2048, DFF=5632, jobs=((16384, 4096), (4096, 2048)), debug=False,
                 phases=(0, 1, 2, 3)):
        self.D = D
        self.DFF = DFF
        self.jobs = tuple(jobs)
        self.debug = debug
        self.KC = D // 128
        self.MEMHD = D // 4
        self.MC = self.MEMHD // 128
        self.NFB = DFF // 512
        self.phases = tuple(phases)


class Eng:
    def __init__(self, nc, e, name, es):
        self.e = e
        self.name = name
        self.sem = es.enter_context(nc.semaphore("es_" + name))
        self.n = 0
        self.seen = {}

    def wait(self, *toks):
        best = {}

        def walk(ts):
            for t in ts:
                if t is None:
                    continue
                if isinstance(t, list):
                    walk(t)
                    continue
                sem, v = t
                if best.get(sem, 0) < v:
                    best[sem] = v
        walk(toks)
        for sem, v in best.items():
            if self.seen.get(sem, 0) >= v:
                continue
            self.e.wait_ge(sem, v)
            self.seen[sem] = v

    def mark(self, ins):
        self.n += 1
        ins.then_inc(self.sem, 1)
        return (self.sem, self.n)


class SemC:
    pool = {"ld": [], "st": []}
    nc = None
    es = None
    nalloc = 0

    @classmethod
    def get(cls, kind):
        if cls.pool[kind]:
            return cls.pool[kind].pop()
        s = SemC()
        s.sem = cls.es.enter_context(cls.nc.semaphore(f"dsem{cls.nalloc}"))
        cls.nalloc += 1
        s.cnt = 0
        return s


class Slot:
    def __init__(self, nc, es, name, shape, dt, kind="ld"):
        self.t = es.enter_context(nc.sbuf_tensor(name, shape, dt))
        self.sc = SemC.get(kind)
        es.callback(SemC.pool[kind].append, self.sc)
        self.busy = []

    @property
    def sem(self):
        return self.sc.sem

    @property
    def cnt(self):
        return self.sc.cnt

    @cnt.setter
    def cnt(self, v):
        self.sc.cnt = v

    def tok(self):
        return (self.sc.sem, self.sc.cnt)


class Bank:
    def __init__(self, ap):
        self.ap = ap
        self.free = None


class K:
    def __init__(self, nc, cfg, es):
        self.nc = nc
        self.cfg = cfg
        self.es = es
        SemC.pool = {"ld": [], "st": []}
        SemC.nc = nc
        SemC.es = es
        SemC.nalloc = 0
        self.PE = Eng(nc, nc.tensor, "pe", es)
        self.ACT = Eng(nc, nc.scalar, "act", es)
        self.DVE = Eng(nc, nc.vector, "dve", es)
        self.POOL = Eng(nc, nc.gpsimd, "pool", es)
        self.SP = Eng(nc, nc.sync, "sp", es)
        self.store_toks = []
        self.load_last = {}
        self.rr = 0

    def load(self, slot, dst, src, first=True, eng=None):
        q = eng or self.SP
        if first:
            q.wait(slot.busy)
            slot.busy = []
        q.e.dma_start(out=dst, in_=src).then_inc(slot.sem, 16)
        slot.cnt += 16
        self.load_last[slot.sem] = slot.cnt
        return slot.tok()

    def store(self, slot, dst, src, wait):
        q = self.POOL
        q.wait(wait)
        q.e.dma_start(out=dst, in_=src).then_inc(slot.sem, 16)
        slot.cnt += 16
        t = slot.tok()
        slot.busy = [t]
        self.last_store = t
        self.store_toks.append(t)
        return t

    def drain_stores(self, engines):
        last = {}
        for sem, v in self.store_toks:
            last[sem] = max(last.get(sem, 0), v)
        for e in engines:
            for sem, v in last.items():
                e.wait((sem, v))

    def barrier(self):
        toks = []
        A, V, P = self.ACT, self.DVE, self.POOL
        A.wait((A.sem, A.n))
        toks.append(A.mark(A.e.activation(out=self.dummy[:, 0:1], in_=self.c["eps_rms"][:], func=AF.Copy)))
        V.wait((V.sem, V.n))
        toks.append(V.mark(V.e.tensor_copy(out=self.dummy[:, 1:2], in_=self.c["eps_rms"][:])))
        P.wait((P.sem, P.n))
        toks.append(P.mark(P.e.tensor_copy(out=self.dummy[:, 2:3], in_=self.c["eps_rms"][:])))
        toks.append((self.PE.sem, self.PE.n))
        engines = (self.PE, A, V, P, self.SP)
        self.drain_stores(engines)
        for e in engines:
            e.wait(toks)
            for sem, v in self.load_last.items():
                e.wait((sem, v))

    def alt(self):
        self.rr ^= 1
        return self.ACT if self.rr else self.DVE


_UID = [0]


def sbuf(kx, name, shape, dt, es=None):
    _UID[0] += 1
    return (es or kx.es).enter_context(kx.nc.sbuf_tensor(f"{name}_{_UID[0]}", shape, dt))


W_SPECS = None


def weight_specs(cfg):
    D, DFF = cfg.D, cfg.DFF
    return [("w_in", D, 6144), ("w_out", 2048, D), ("w_mq", D, D), ("w_mkv", D, 2 * D),
            ("w_mo", D, D), ("w_gu", D, 2 * DFF), ("w_down", DFF, D)]


def build_program(cfg):
    nc = bass.Bass("TRN2", target_bir_lowering=False)
    D, DFF, KC = cfg.D, cfg.DFF, cfg.KC
    dbg = cfg.debug
    SK = "ExternalOutput" if dbg else "Internal"
    dram = {}

    def din(name, shape, dt=F32):
        dram[name] = nc.dram_tensor(name, list(shape), dt, kind="ExternalInput").ap()
        return dram[name]

    def dscr(name, shape, dt=BF16):
        dram[name] = nc.dram_tensor(name, list(shape), dt, kind=SK).ap()
        return dram[name]

    def dout(name, shape, dt=F32):
        dram[name] = nc.dram_tensor(name, list(shape), dt, kind="ExternalOutput").ap()
        return dram[name]

    NJ = len(cfg.jobs)
    for j, (T, OWN) in enumerate(cfg.jobs):
        din(f"x{j}", [T, D])
        din(f"xh{j}", [512, D])
        din(f"mem{j}", [256, D])
        din(f"cos{j}", [128, T])
        din(f"sin{j}", [128, T])
        din(f"nab{j}", [4, 128, 8, 640])
        dout(f"y{j}", [OWN, D])
        dscr(f"KT{j}", [8, 128, T])
        dscr(f"VH{j}", [8, 128, T // 128, 128])
        dscr(f"QT{j}", [8, 128, OWN])
        dscr(f"NAQT{j}", [8, 128, OWN])
        dscr(f"NAKT{j}", [8, 128, OWN + 512])
        dscr(f"NAV{j}", [OWN + 512, 1024])
        dscr(f"MIXT{j}", [16, 128, OWN])
    din("nabi", [128, 8, 640])
    for name, R, C in weight_specs(cfg):
        din(name, [R, C])
    for g in ("g_mix", "g_xattn", "g_mem", "g_ffn"):
        din(g, [128, KC])
    din("g_final", [128, D])
    din("g_subln", [128, 1])
    din("lamv", [128, 4, 64])
    din("ident", [128, 128], BF16)
    din("rotm", [128, 128], BF16)
    dscr("Wb_in", [12, 128, KC, 512])
    dscr("Wb_out", [D // 512, 128, 16, 512])
    dscr("Wb_mq", [D // 512, 128, KC, 512])
    dscr("Wb_mkv", [2 * D // 512, 128, KC, 512])
    dscr("Wb_mo", [D // 512, 128, KC, 512])
    dscr("Wb_gu", [2 * DFF // 512, 128, KC, 512])
    dscr("Wb_down", [DFF // 512, 128, 4, D])

    es = contextlib.ExitStack()
    with es:
        kx = K(nc, cfg, es)
        kx.dram = dram
        setup_consts(kx)
        if 0 in cfg.phases:
            phase0_weights(kx, ("w_in",) if NA_BACKGROUND_CAST else tuple(n for n, _, _ in weight_specs(cfg)))
        if 1 in cfg.phases:
            phase1(kx)
        if 2 in cfg.phases:
            phase2_na(kx)
            phase2_diff(kx)
        if 3 in cfg.phases:
            phase3(kx)
        finish(kx)
    return nc


def setup_consts(kx):
    nc, es, cfg = kx.nc, kx.es, kx.cfg
    c = kx.c = {}
    d = kx.dram
    KC = cfg.KC
    cs = Slot(nc, es, "cst", [128, 1], F32)
    kx.cslot = cs
    kx.dummy = sbuf(kx, "bar_dummy", [128, 4], F32)
    kx.ysem = SemC.get("st")
    kx.ysem2 = SemC.get("st")

    def ld(name, shape, dt, src):
        t = sbuf(kx, "c_" + name, shape, dt)
        kx.load(cs, t[:], src, first=False)
        c[name] = t
        return t

    ld("ident", [128, 128], BF16, d["ident"])
    ld("rotm", [128, 128], BF16, d["rotm"])
    for g in ("g_mix", "g_xattn", "g_mem", "g_ffn"):
        ld(g, [128, KC], F32, d[g])
    ld("g_subln", [128, 1], F32, d["g_subln"])
    ld("lamv", [128, 4, 64], F32, d["lamv"])
    ctok = cs.tok()
    kx.ctok = ctok
    P = kx.POOL
    ones_bf = sbuf(kx, "ones_bf", [128, 128], BF16)
    ones_f = sbuf(kx, "ones_f", [128, 128], F32)
    sel1 = sbuf(kx, "sel1", [64, 128], F32)
    sel2 = sbuf(kx, "sel2", [64, 128], F32)
    P.e.memset(ones_bf[:], 1.0)
    P.e.memset(ones_f[:], 1.0)
    P.e.memset(sel1[:], 0.0)
    t0 = P.mark(P.e.memset(sel2[:], 0.0))
    P.wait(t0)
    P.e.memset(sel1[0:1, :], 1.0)
    P.e.memset(sel2[32:33, :], 1.0)
    eps_rms = sbuf(kx, "eps_rms", [128, 1], F32)
    eps_sub = sbuf(kx, "eps_sub", [128, 1], F32)
    P.e.memset(eps_rms[:], RMS_EPS)
    t2 = P.mark(P.e.memset(eps_sub[:], SUBLN_EPS))
    c.update(ones_bf=ones_bf, ones_f=ones_f, sel1=sel1, sel2=sel2, eps_rms=eps_rms, eps_sub=eps_sub)
    V, A = kx.DVE, kx.ACT
    lt = sbuf(kx, "lam_t", [128, 2, 64], F32)
    ls = sbuf(kx, "lam_s", [128, 2], F32)
    le = sbuf(kx, "lam_e", [128, 2], F32)
    nlam = sbuf(kx, "nlam", [128, 1], F32)
    gsub = sbuf(kx, "gsub", [128, 1], F32)
    V.wait(ctok)
    lv = c["lamv"]
    V.e.tensor_tensor(out=lt[:, 0, :], in0=lv[:, 0, :], in1=lv[:, 1, :], op=ALU.mult)
    ta = V.mark(V.e.tensor_tensor(out=lt[:, 1, :], in0=lv[:, 2, :], in1=lv[:, 3, :], op=ALU.mult))
    V.e.wait_ge(V.sem, ta[1])
    tb = V.mark(V.e.reduce_sum(out=ls[:], in_=lt[:], axis=AX.X))
    A.wait(tb)
    tc = A.mark(A.e.activation(out=le[:], in_=ls[:], func=AF.Exp))
    V.wait(tc)
    td = V.mark(V.e.tensor_tensor(out=nlam[:], in0=le[:, 1:2], in1=le[:, 0:1], op=ALU.subtract))
    V.e.wait_ge(V.sem, td[1])
    te = V.mark(V.e.tensor_scalar_add(out=nlam[:], in0=nlam[:], scalar1=-LAM_INIT))
    tf = V.mark(V.e.tensor_scalar_mul(out=gsub[:], in0=c["g_subln"][:], scalar1=(1.0 - LAM_INIT)))
    c.update(nlam=nlam, gsub=gsub)
    kx.const_toks = [ctok, t2, tf, te]
    for e in (kx.PE, kx.ACT, kx.DVE, kx.POOL):
        e.wait(kx.const_toks)


def finish(kx):
    kx.drain_stores([kx.POOL])
    t = kx.POOL.mark(kx.POOL.e.memset(kx.cslot.t[:], 0.0))
    for e in (kx.SP, kx.ACT, kx.DVE, kx.PE):
        e.wait(t)


def phase0_weights(kx, names):
    nc, cfg, d = kx.nc, kx.cfg, kx.dram
    es = contextlib.ExitStack()
    with es:
        NS = 3
        fin = [Slot(nc, es, f"p0f{i}", [128, 4096], F32) for i in range(NS)]
        fout = [Slot(nc, es, f"p0b{i}", [128, 4096], BF16, "st") for i in range(NS)]
        it = 0
        for name, R, C in weight_specs(cfg):
            if name not in names:
                continue
            src = d[name]
            dstn = "Wb_" + name[2:]
            dst = d[dstn]
            for kc in range(R // 128):
                for c0 in range(0, C, 4096):
                    cw = min(4096, C - c0)
                    si, so = fin[it % NS], fout[it % NS]
                    lt = kx.load(si, si.t[:, 0:cw], src[kc * 128:(kc + 1) * 128, c0:c0 + cw])
                    E = kx.ACT if (it % 2) else kx.DVE
                    E.wait(lt, so.busy)
                    so.busy = []
                    if E is kx.ACT:
                        ins = E.e.activation(out=so.t[:, 0:cw], in_=si.t[:, 0:cw], func=AF.Copy)
                    else:
                        ins = E.e.tensor_copy(out=so.t[:, 0:cw], in_=si.t[:, 0:cw])
                    ct = E.mark(ins)
                    si.busy = [ct]
                    if name == "w_down":
                        fb, kcl = kc // 4, kc % 4
                        dap = dst[fb, :, kcl, :]
                        sap = so.t[:, 0:cw]
                    else:
                        nb0 = c0 // 512
                        nbn = cw // 512
                        dap = dst[nb0:nb0 + nbn, :, kc, :].rearrange("nb p c -> p nb c")
                        sap = so.t[:, 0:cw].rearrange("p (nb c) -> p nb c", c=512)
                    kx.store(so, dap, sap, ct)
                    it += 1
        kx.barrier()


class NormCtx:
    def __init__(self, kx, es, pb, nx=2, nhn=1):
        nc, cfg = kx.nc, kx.cfg
        D = cfg.D
        _UID[0] += 1
        self.xs = [Slot(nc, es, f"nx{i}_{_UID[0]}", [128, D], F32) for i in range(nx)]
        self.junk = sbuf(kx, "njunk", [128, D], BF16, es)
        self.hns = [sbuf(kx, f"nhn{i}", [128, 4, D], BF16, es) for i in range(nhn)]
        self.ss = sbuf(kx, "nss", [128, 8], F32, es)
        self.rstd = sbuf(kx, "nrstd", [128, 8], F32, es)
        self.pb = pb
        self.hn_free = [None] * nhn
        self.i = 0
        self.junk_tok = None


def rms_rstd(kx, src_ap, ss_ap, rstd_ap, junk_ap, width, in_toks, eps_ap=None):
    A, V = kx.ACT, kx.DVE
    A.wait(in_toks)
    ta = A.mark(A.e.activation(out=junk_ap, in_=src_ap, func=AF.Square, accum_out=ss_ap))
    A.wait(ta)
    tb = A.mark(A.e.activation(out=ss_ap, in_=ss_ap, func=AF.Sqrt, scale=1.0 / width,
                               bias=(eps_ap if eps_ap is not None else kx.c["eps_rms"][:])))
    V.wait(tb)
    tc = V.mark(V.e.reciprocal(out=rstd_ap, in_=ss_ap))
    return ta, tc


def norm_part(kx, ncx, tiles, ntt=4, hp=0):
    cfg = kx.cfg
    D = cfg.D
    A, V = kx.ACT, kx.DVE
    hn = ncx.hns[hp]
    hn_toks = []
    for tt in range(ntt):
        ap, in_toks, rel = tiles[tt]()
        col = (ncx.i % 8)
        ncx.i += 1
        A.wait(ncx.junk_tok)
        ta, tc = rms_rstd(kx, ap, ncx.ss[:, col:col + 1], ncx.rstd[:, col:col + 1], ncx.junk[:],
                          D, in_toks)
        ncx.junk_tok = ta
        V.wait(ncx.hn_free[hp], tc)
        th = V.mark(V.e.tensor_scalar(out=hn[:, tt, :], in0=ap, scalar1=ncx.rstd[:, col:col + 1],
                                      scalar2=None, op0=ALU.mult))
        hn_toks.append(th)
        rel([ta, th])
    ncx.hn_free[hp] = None
    return hn_toks


def trans_part(kx, ncx, hn_toks, gT, dstT, dst_free, ntt=4, hp=0):
    cfg = kx.cfg
    KC = cfg.KC
    A, PE = kx.ACT, kx.PE
    hn = ncx.hns[hp]
    out_toks = []
    last_pe = None
    for kc in range(KC):
        bk = ncx.pb[kc % len(ncx.pb)]
        PE.wait(hn_toks, bk.free)
        for tt in range(ntt):
            ins = PE.e.transpose(bk.ap[:, tt * 128:(tt + 1) * 128], hn[:, tt, kc * 128:(kc + 1) * 128],
                                 kx.c["ident"][:])
        tp = PE.mark(ins)
        last_pe = tp
        E = kx.alt()
        E.wait(tp, dst_free)
        if E is A:
            ins = E.e.activation(out=dstT[:, kc, 0:ntt * 128], in_=bk.ap[:, 0:ntt * 128], func=AF.Copy,
                                 scale=gT[:, kc:kc + 1])
        else:
            ins = E.e.tensor_scalar(out=dstT[:, kc, 0:ntt * 128], in0=bk.ap[:, 0:ntt * 128],
                                    scalar1=gT[:, kc:kc + 1], scalar2=None, op0=ALU.mult)
        te = E.mark(ins)
        bk.free = te
        out_toks.append(te)
    ncx.hn_free[hp] = last_pe
    return out_toks[-2:]


def make_xnT(kx, ncx, tiles, gT, dstT, dst_free, ntt=4):
    hn_toks = norm_part(kx, ncx, tiles, ntt, 0)
    return trans_part(kx, ncx, hn_toks, gT, dstT, dst_free, ntt, 0)


def phase1(kx):
    nc, cfg, d = kx.nc, kx.cfg, kx.dram
    D, KC = cfg.D, cfg.KC
    PE, A, V = kx.PE, kx.ACT, kx.DVE
    es = contextlib.ExitStack()
    with es:
        NB1 = 5
        ps = [es.enter_context(nc.psum_tensor(f"p1s{i}", [128, 512], F32)) for i in range(NB1)]
        pbt = [es.enter_context(nc.psum_tensor(f"p1b{i}", [128, 1024], BF16)) for i in range(3)]
        banks = [Bank(p[:]) for p in ps]
        pb = [Bank(pbt[i][:, 0:512]) for i in range(3)]
        ncx = NormCtx(kx, es, pb, nx=4, nhn=2)
        NW = 4
        wr = [Slot(nc, es, f"p1w{i}", [128, KC, 512], BF16) for i in range(NW)]
        xnTs = [sbuf(kx, f"p1xnT{i}", [128, KC, 512], BF16, es) for i in range(2)]
        cur = dict(x=None, ready=None)
        cs_ = [Slot(nc, es, f"p1cs{i}", [128, 2, 512], F32) for i in range(3)]
        fst = [Slot(nc, es, f"p1fst{i}", [128, 512], BF16, "st") for i in range(4)]
        tst = [Slot(nc, es, f"p1tst{i}", [128, 1024], BF16, "st") for i in range(3)]
        qraw = [sbuf(kx, f"p1qraw{i}", [128, 512], BF16, es) for i in range(2)]
        tm1 = [sbuf(kx, f"p1tm1{i}", [128, 512], F32, es) for i in range(2)]
        tm2 = [sbuf(kx, f"p1tm2{i}", [128, 512], F32, es) for i in range(2)]
        st = dict(bi=0, fi=0, ti=0, ri=0, xi=0, wi=0, ci=0)
        qraw_free = [None, None]
        tm_free = [None, None]
        xn_readers = [[], []]

        def next_bank():
            b = banks[st["bi"] % NB1]
            st["bi"] += 1
            return b

        def x_tiles(src, t0):
            def mk(tt):
                def f():
                    s = ncx.xs[st["xi"] % len(ncx.xs)]
                    st["xi"] += 1
                    lt = kx.load(s, s.t[:], src[t0 + tt * 128:t0 + (tt + 1) * 128, :])

                    def rel(toks):
                        s.busy = list(toks)
                    return s.t[:], [lt], rel
                return f
            return [mk(tt) for tt in range(4)]

        def do_norm(src, t0, par):
            return norm_part(kx, ncx, x_tiles(src, t0), 4, par)

        def do_trans(hn_toks, par):
            toks = trans_part(kx, ncx, hn_toks, kx.c["g_mix"], xnTs[par], xn_readers[par], 4, par)
            xn_readers[par] = []
            return toks

        def fm_proj(wslot, wtok, c):
            bk = next_bank()
            xnT = cur["x"]
            PE.wait(wtok, cur["ready"], bk.free)
            for kc in range(KC):
                ins = PE.e.matmul(bk.ap, lhsT=wslot.t[:, kc, c * 128:(c + 1) * 128], rhs=xnT[:, kc, :],
                                  start=(kc == 0), stop=(kc == KC - 1))
            tp = PE.mark(ins)
            return bk, tp

        def tm_proj(wslot, wtok, tt):
            bk = next_bank()
            xnT = cur["x"]
            PE.wait(wtok, cur["ready"], bk.free)
            for kc in range(KC):
                ins = PE.e.matmul(bk.ap, lhsT=xnT[:, kc, tt * 128:(tt + 1) * 128], rhs=wslot.t[:, kc, :],
                                  start=(kc == 0), stop=(kc == KC - 1))
            tp = PE.mark(ins)
            return bk, tp

        def plain_fm(wslot, wtok, c, dst_ap):
            bk, tp = fm_proj(wslot, wtok, c)
            s = fst[st["fi"] % len(fst)]
            st["fi"] += 1
            E = kx.alt()
            E.wait(tp, s.busy)
            s.busy = []
            if E is A:
                ins = E.e.activation(out=s.t[:], in_=bk.ap, func=AF.Copy)
            else:
                ins = E.e.tensor_copy(out=s.t[:], in_=bk.ap)
            te = E.mark(ins)
            bk.free = te
            kx.store(s, dst_ap, s.t[:], te)
            return tp

        rope_pend = []

        def rope_flush():
            while rope_pend:
                rope_pend.pop(0)()

        def rope_fm(wslot, wtok, c, cslot, ctok, dst_ap):
            bk, tp = fm_proj(wslot, wtok, c)
            r = st["ri"] % 2
            st["ri"] += 1
            A.wait(tp, qraw_free[r])
            ta = A.mark(A.e.activation(out=qraw[r][:], in_=bk.ap, func=AF.Copy))
            bk.free = ta
            res = {}

            def tail():
                bk2 = next_bank()
                PE.wait(ta, bk2.free)
                tr = PE.mark(PE.e.matmul(bk2.ap, lhsT=kx.c["rotm"][:], rhs=qraw[r][:], start=True, stop=True))
                V.wait(ta, ctok, tm_free[r])
                t1 = V.mark(V.e.tensor_tensor(out=tm1[r][:], in0=qraw[r][:], in1=cslot.t[:, 0, :], op=ALU.mult))
                V.wait(tr)
                t2 = V.mark(V.e.tensor_tensor(out=tm2[r][:], in0=bk2.ap, in1=cslot.t[:, 1, :], op=ALU.mult))
                bk2.free = t2
                qraw_free[r] = [tr, t1]
                s_ = fst[st["fi"] % len(fst)]
                st["fi"] += 1
                V.wait(t1, t2, s_.busy)
                s_.busy = []
                t3 = V.mark(V.e.tensor_tensor(out=s_.t[:], in0=tm1[r][:], in1=tm2[r][:], op=ALU.add))
                tm_free[r] = t3
                kx.store(s_, dst_ap, s_.t[:], t3)
                res["t3"] = t3

            prev = list(rope_pend)
            del rope_pend[:]
            for f in prev:
                f()
            rope_pend.append(tail)
            return tp, res

        def tm_out(wslots, wtoks, tt, dst_ap, head_major=None):
            s = tst[st["ti"] % len(tst)]
            st["ti"] += 1
            toks = []
            tps = []
            for cb in range(2):
                bk, tp = tm_proj(wslots[cb], wtoks[cb], tt)
                tps.append(tp)
                E = kx.alt()
                E.wait(tp, s.busy)
                if E is A:
                    ins = E.e.activation(out=s.t[:, cb * 512:(cb + 1) * 512], in_=bk.ap, func=AF.Copy)
                else:
                    ins = E.e.tensor_copy(out=s.t[:, cb * 512:(cb + 1) * 512], in_=bk.ap)
                te = E.mark(ins)
                bk.free = te
                toks.append(te)
            s.busy = []
            if head_major is not None:
                src = s.t[:].rearrange("p (h d) -> p h d", d=128)
            else:
                src = s.t[:]
            kx.store(s, dst_ap, src, toks)
            return tps

        def load_w(nb):
            s = wr[st["wi"] % NW]
            st["wi"] += 1
            t = kx.load(s, s.t[:], d["Wb_in"][nb])
            return s, t

        def load_cs(j, t0):
            s = cs_[st["ci"] % 3]
            st["ci"] += 1
            kx.load(s, s.t[:, 0, :], d[f"cos{j}"][:, t0:t0 + 512])
            t = kx.load(s, s.t[:, 1, :], d[f"sin{j}"][:, t0:t0 + 512], first=False)
            return s, t

        items = []
        for j, (T, OWN) in enumerate(cfg.jobs):
            for b in range(T // 512):
                items.append(("kv", j, b))
        for j, (T, OWN) in enumerate(cfg.jobs):
            for b in range(OWN // 512):
                items.append(("own", j, b))
            items.append(("halo", j, 0))

        def prepA(i):
            kind, j, b = items[i]
            if kind == "halo":
                return do_norm(d[f"xh{j}"], 0, i % 2), None
            return do_norm(d[f"x{j}"], b * 512, i % 2), load_cs(j, b * 512)

        def prepB(i, pa):
            hn_toks, csl = pa
            return do_trans(hn_toks, i % 2), csl

        kvw = [load_w(nb) for nb in (8, 9, 10, 11)]
        kv_last = [None]

        def run_item(i, ready, csl):
            kind, j, b = items[i]
            T, OWN = cfg.jobs[j]
            t0 = b * 512
            cur["x"] = xnTs[i % 2]
            cur["ready"] = ready
            last = None
            if kind == "kv":
                cslot, ctok = csl
                res = None
                for h in range(8):
                    ws, wt = kvw[h // 4]
                    tp, res = rope_fm(ws, wt, h % 4, cslot, ctok, d[f"KT{j}"][h, :, t0:t0 + 512])
                rope_flush()
                cslot.busy = [res["t3"]]
                for tt in range(4):
                    kt = (t0 // 128) + tt
                    tps = tm_out([kvw[2][0], kvw[3][0]], [kvw[2][1], kvw[3][1]], tt,
                                 d[f"VH{j}"][:, :, kt, :].rearrange("h p d -> p h d"), head_major=True)
                    last = tps[-1]
                kv_last[0] = last
                if i + 1 < len(items) and items[i + 1][0] != "kv":
                    for (s_, t_) in kvw:
                        s_.busy = [last]
                xn_readers[i % 2] = [last]
                return
            if kind == "own":
                cslot, ctok = csl
                for nb in (0, 1):
                    ws, wt = load_w(nb)
                    for c in range(4):
                        h = (nb % 2) * 4 + c
                        last = plain_fm(ws, wt, c, d[f"NAQT{j}"][h, :, t0:t0 + 512])
                    ws.busy = [last]
            for nb in (2, 3):
                ws, wt = load_w(nb)
                for c in range(4):
                    h = (nb % 2) * 4 + c
                    if kind == "own":
                        last = plain_fm(ws, wt, c, d[f"NAKT{j}"][h, :, 256 + t0:256 + t0 + 512])
                    else:
                        bk, tp = fm_proj(ws, wt, c)
                        s_ = fst[st["fi"] % len(fst)]
                        st["fi"] += 1
                        E = kx.alt()
                        E.wait(tp, s_.busy)
                        s_.busy = []
                        if E is A:
                            ins = E.e.activation(out=s_.t[:], in_=bk.ap, func=AF.Copy)
                        else:
                            ins = E.e.tensor_copy(out=s_.t[:], in_=bk.ap)
                        te = E.mark(ins)
                        bk.free = te
                        kx.store(s_, d[f"NAKT{j}"][h, :, 0:256], s_.t[:, 0:256], te)
                        kx.store(s_, d[f"NAKT{j}"][h, :, 256 + OWN:512 + OWN], s_.t[:, 256:512], te)
                        last = tp
                ws.busy = [last]
            wv = [load_w(nb) for nb in (4, 5)]
            for tt in range(4):
                if kind == "own":
                    r0 = 256 + t0 + tt * 128
                else:
                    r0 = tt * 128 if tt < 2 else 256 + OWN + (tt - 2) * 128
                tps = tm_out([wv[0][0], wv[1][0]], [wv[0][1], wv[1][1]], tt,
                             d[f"NAV{j}"][r0:r0 + 128, :])
                last = tps[-1]
            for (s_, t_) in wv:
                s_.busy = [last]
            if kind == "own":
                res = None
                for nb in (6, 7):
                    ws, wt = load_w(nb)
                    for c in range(4):
                        h = (nb % 2) * 4 + c
                        last, res = rope_fm(ws, wt, c, cslot, ctok, d[f"QT{j}"][h, :, t0:t0 + 512])
                    ws.busy = [last]
                rope_flush()
                cslot.busy = [res["t3"]]
            xn_readers[i % 2] = [last]

        n_it = len(items)
        pa = {0: prepA(0)}
        if n_it > 1:
            pa[1] = prepA(1)
        pbs = {0: prepB(0, pa.pop(0))}
        for i in range(n_it):
            if i + 1 < n_it:
                pbs[i + 1] = prepB(i + 1, pa.pop(i + 1))
            if i + 2 < n_it:
                pa[i + 2] = prepA(i + 2)
            ready, csl = pbs.pop(i)
            run_item(i, ready, csl)
        kx.barrier()


def phase2_na(kx):
    nc, cfg, d = kx.nc, kx.cfg, kx.dram
    PE, A, V = kx.PE, kx.ACT, kx.DVE
    scale = 128.0 ** -0.5
    es = contextlib.ExitStack()
    with es:
        psS = [Bank(es.enter_context(nc.psum_tensor(f"naS{i}", [128, 1024], F32))[:]) for i in range(2)]
        pbT = [Bank(es.enter_context(nc.psum_tensor(f"naT{i}", [128, 1024], BF16))[:]) for i in range(2)]
        psO = Bank(es.enter_context(nc.psum_tensor("naO", [128, 1024], F32))[:])
        tabI = Slot(nc, es, "naTabI", [128, 8, 640], F32)
        tabE = [Slot(nc, es, f"naTabE{i}", [128, 8, 640], F32) for i in range(2)]
        Qs = [Slot(nc, es, f"naQ{i}", [128, 8, 128], BF16) for i in range(2)]
        Ks = [Slot(nc, es, f"naK{i}", [128, 8, 640], BF16) for i in range(2)]
        Vs = [Slot(nc, es, f"naV{i}", [128, 5, 1024], BF16) for i in range(2)]
        sb = [sbuf(kx, f"naSb{i}", [128, 640], F32, es) for i in range(4)]
        sb_free = [None] * 4
        p = [sbuf(kx, f"naP{i}", [128, 8, 640], BF16, es) for i in range(2)]
        p_free = [None, None]
        ssum = sbuf(kx, "naSum", [128, 16], F32, es)
        rs = sbuf(kx, "naRs", [128, 16], F32, es)
        pT = [sbuf(kx, f"naPT{i}", [128, 5, 128], BF16, es) for i in range(2)]
        pT_free = [None, None]
        ost = [Slot(nc, es, f"naOst{i}", [128, 8, 128], BF16, "st") for i in range(2)]
        tI = kx.load(tabI, tabI.t[:], d["nabi"])
        PW = 2048
        NWS = 4
        pieces = []
        if NA_BACKGROUND_CAST and 0 in cfg.phases:
            wfin = [Slot(nc, es, f"nawf{i}", [128, PW], F32) for i in range(NWS)]
            wfout = [Slot(nc, es, f"nawb{i}", [128, PW], BF16, "st") for i in range(NWS)]
            for name, R, C in weight_specs(cfg):
                if name == "w_in":
                    continue
                for kc in range(R // 128):
                    for c0 in range(0, C, PW):
                        pieces.append((name, kc, c0, min(PW, C - c0)))
        wstate = dict(k=0, loaded={})

        def wcast_tick():
            k = wstate["k"]
            wstate["k"] = k + 1
            if k < len(pieces):
                name, kc, c0, cw = pieces[k]
                si = wfin[k % NWS]
                wstate["loaded"][k] = kx.load(si, si.t[:, 0:cw], d[name][kc * 128:(kc + 1) * 128, c0:c0 + cw])
            pk = k - 3
            if 0 <= pk < len(pieces):
                name, kc, c0, cw = pieces[pk]
                si, so = wfin[pk % NWS], wfout[pk % NWS]
                E = A if (pk % 2) else V
                E.wait(wstate["loaded"].pop(pk), so.busy)
                so.busy = []
                if E is A:
                    ins = E.e.activation(out=so.t[:, 0:cw], in_=si.t[:, 0:cw], func=AF.Copy)
                else:
                    ins = E.e.tensor_copy(out=so.t[:, 0:cw], in_=si.t[:, 0:cw])
                ct = E.mark(ins)
                si.busy = [ct]
                dst = d["Wb_" + name[2:]]
                if name == "w_down":
                    fb_, kcl = kc // 4, kc % 4
                    dap = dst[fb_, :, kcl, c0:c0 + cw]
                    sap = so.t[:, 0:cw]
                else:
                    nb0, nbn = c0 // 512, cw // 512
                    dap = dst[nb0:nb0 + nbn, :, kc, :].rearrange("nb p c -> p nb c")
                    sap = so.t[:, 0:cw].rearrange("p (nb c) -> p nb c", c=512)
                kx.store(so, dap, sap, ct)

        def wcast_pending():
            return wstate["k"] < len(pieces) + 3

        tiles = []
        for j, (T, OWN) in enumerate(cfg.jobs):
            nt = OWN // 128
            for jt in range(nt):
                e = {0: 0, 1: 1, nt - 2: 2, nt - 1: 3}.get(jt)
                tiles.append((j, jt, e))
        state = {}
        cnt = dict(e=0, sbi=0)

        def stage1(i):
            j, jt, e = tiles[i]
            par = i % 2
            q, k, v = Qs[par], Ks[par], Vs[par]
            kx.load(q, q.t[:], d[f"NAQT{j}"][:, :, jt * 128:(jt + 1) * 128].rearrange("h p q -> p h q"))
            tq = kx.load(k, k.t[:], d[f"NAKT{j}"][:, :, jt * 128:jt * 128 + 640].rearrange("h p q -> p h q"))
            tk = k.tok()
            tq = q.tok()
            tv = kx.load(v, v.t[:], d[f"NAV{j}"][jt * 128:jt * 128 + 640, :].rearrange("(k p) c -> p k c", p=128))
            if e is None:
                tab, ttab = tabI, tI
            else:
                tab = tabE[cnt["e"] % 2]
                cnt["e"] += 1
                ttab = kx.load(tab, tab.t[:], d[f"nab{j}"][e])
            t4s = []
            lastS = None
            lastT1 = None
            for h in range(8):
                bk = psS[h % 2]
                PE.wait(tq, tk, bk.free)
                PE.e.matmul(bk.ap[:, 0:512], lhsT=q.t[:, h, :], rhs=k.t[:, h, 0:512], start=True, stop=True)
                tp = PE.mark(PE.e.matmul(bk.ap[:, 512:640], lhsT=q.t[:, h, :], rhs=k.t[:, h, 512:640],
                                         start=True, stop=True))
                lastS = tp
                si = cnt["sbi"] % 4
                cnt["sbi"] += 1
                V.wait(tp, ttab, sb_free[si])
                t1 = V.mark(V.e.scalar_tensor_tensor(out=sb[si][:], in0=bk.ap[:, 0:640], scalar=scale,
                                                     in1=tab.t[:, h, :], op0=ALU.mult, op1=ALU.add))
                bk.free = t1
                lastT1 = t1
                col = (i % 2) * 8 + h
                A.wait(t1, p_free[par])
                t2 = A.mark(A.e.activation(out=p[par][:, h, :], in_=sb[si][:], func=AF.Exp,
                                           accum_out=ssum[:, col:col + 1]))
                sb_free[si] = t2
                V.wait(t2)
                t3 = V.mark(V.e.reciprocal(out=rs[:, col:col + 1], in_=ssum[:, col:col + 1]))
                V.wait(t3)
                t4 = V.mark(V.e.tensor_scalar(out=p[par][:, h, :], in0=p[par][:, h, :],
                                              scalar1=rs[:, col:col + 1], scalar2=None, op0=ALU.mult))
                t4s.append(t4)
                if pieces and wcast_pending():
                    wcast_tick()
            p_free[par] = None
            q.busy = [lastS]
            k.busy = [lastS]
            if e is not None:
                tab.busy = [lastT1]
            state[i] = dict(t4s=t4s, tv=tv, v=v)

        def stage2(i):
            j, jt, e = tiles[i]
            par = i % 2
            stt = state.pop(i)
            v = stt["v"]
            last = None
            for h in range(8):
                bk = pbT[h % 2]
                PE.wait(stt["t4s"][h], bk.free)
                for k5 in range(5):
                    ins = PE.e.transpose(bk.ap[:, k5 * 128:(k5 + 1) * 128], p[par][:, h, k5 * 128:(k5 + 1) * 128],
                                         kx.c["ident"][:])
                tp2 = PE.mark(ins)
                E = kx.alt()
                E.wait(tp2, pT_free[h % 2])
                src = bk.ap[:, 0:640].rearrange("p (k q) -> p k q", q=128)
                if E is A:
                    ins = E.e.activation(out=pT[h % 2][:], in_=src, func=AF.Copy)
                else:
                    ins = E.e.tensor_copy(out=pT[h % 2][:], in_=src)
                t5 = E.mark(ins)
                bk.free = t5
                PE.wait(t5, stt["tv"], psO.free if h == 0 else None)
                for k5 in range(5):
                    ins = PE.e.matmul(psO.ap[:, h * 128:(h + 1) * 128], lhsT=v.t[:, k5, h * 128:(h + 1) * 128],
                                      rhs=pT[h % 2][:, k5, :], start=(k5 == 0), stop=(k5 == 4))
                tp3 = PE.mark(ins)
                pT_free[h % 2] = tp3
                last = tp3
            p_free[par] = last
            v.busy = [last]
            o = ost[i % 2]
            E = kx.alt()
            E.wait(last, o.busy)
            o.busy = []
            src = psO.ap.rearrange("p (h q) -> p h q", q=128)
            if E is A:
                ins = E.e.activation(out=o.t[:], in_=src, func=AF.Copy)
            else:
                ins = E.e.tensor_copy(out=o.t[:], in_=src)
            t6 = E.mark(ins)
            psO.free = t6
            kx.store(o, d[f"MIXT{j}"][0:8, :, jt * 128:(jt + 1) * 128].rearrange("h p q -> p h q"), o.t[:], t6)

        n = len(tiles)
        stage1(0)
        for i in range(n):
            if i + 1 < n:
                stage1(i + 1)
            stage2(i)
        while pieces and wcast_pending():
            wcast_tick()
        kx.barrier()


def phase2_diff(kx):
    nc, cfg, d = kx.nc, kx.cfg, kx.dram
    PE, A, V = kx.PE, kx.ACT, kx.DVE
    scale = 64.0 ** -0.5
    c = kx.c
    Tmax = max(T for T, _ in cfg.jobs)
    es = contextlib.ExitStack()
    with es:
        psS = [Bank(es.enter_context(nc.psum_tensor(f"dfS{i}", [128, 1024], F32))[:]) for i in range(2)]
        psO1 = Bank(es.enter_context(nc.psum_tensor("dfO1", [128, 512], F32))[:])
        psO2 = Bank(es.enter_context(nc.psum_tensor("dfO2", [128, 512], F32))[:])
        psX = Bank(es.enter_context(nc.psum_tensor("dfX", [128, 512], F32))[:])
        KTs = [Slot(nc, es, f"dfK{i}", [128, Tmax], BF16) for i in range(2)]
        VHs = [Slot(nc, es, f"dfV{i}", [128, Tmax // 128, 128], BF16) for i in range(2)]
        QTs = [Slot(nc, es, f"dfQ{i}", [128, 512], BF16) for i in range(2)]
        NPT = 4
        PT = [sbuf(kx, f"dfPT{i}", [128, 1024], BF16, es) for i in range(NPT)]
        PT_pe = [None] * NPT
        PT_dve = [None] * NPT
        acc = [sbuf(kx, f"dfacc{i}", [128, 1024], F32, es) for i in range(2)]
        acc_free = [None, None]
        acc_tok = [None, None]
        tmpb = [sbuf(kx, f"dftmp{i}", [128, 1024], BF16, es) for i in range(3)]
        tmp_tok = [None, None, None]
        pair_tok = [None, None]
        o1 = sbuf(kx, "dfo1", [128, 512], F32, es)
        o2 = sbuf(kx, "dfo2", [128, 512], F32, es)
        tt_ = sbuf(kx, "dft", [128, 512], F32, es)
        uu = sbuf(kx, "dfu", [128, 512], F32, es)
        oo = sbuf(kx, "dfo", [128, 512], F32, es)
        sq = sbuf(kx, "dfsq", [128, 512], F32, es)
        rt = sbuf(kx, "dfrt", [128, 512], F32, es)
        ost = [Slot(nc, es, f"dfOst{i}", [128, 512], BF16, "st") for i in range(2)]

        heads = [(j, h) for j in range(len(cfg.jobs)) for h in range(8)]
        steps = []
        for hi, (j, h) in enumerate(heads):
            T, OWN = cfg.jobs[j]
            for qc in range(OWN // 512):
                for kt in range(T // 128):
                    steps.append((hi, j, h, qc, kt))
        N = len(steps)
        kv_tok = {}
        q_tok = {}
        cnt = dict(q=0, ost=0, pair=0, grp=0)

        PW = 2048
        P = kx.POOL
        wfin = [Slot(nc, es, f"dfwf{i}", [128, PW if BACKGROUND_CAST else 8], F32, "st") for i in range(2)]
        wfout = [Slot(nc, es, f"dfwb{i}", [128, PW if BACKGROUND_CAST else 8], BF16, "st") for i in range(2)]
        pieces = []
        if BACKGROUND_CAST and 0 in cfg.phases:
            for name, R, C in weight_specs(cfg):
                if name == "w_in":
                    continue
                for kc in range(R // 128):
                    for c0 in range(0, C, PW):
                        pieces.append((name, kc, c0, min(PW, C - c0)))
        wstate = dict(k=0, prev=None)

        def wcast_tick():
            k = wstate["k"]
            prev = wstate["prev"]
            if k < len(pieces):
                name, kc, c0, cw = pieces[k]
                si = wfin[k % 2]
                lt = kx.load(si, si.t[:, 0:cw], d[name][kc * 128:(kc + 1) * 128, c0:c0 + cw], eng=P)
                wstate["prev"] = (k, lt)
                wstate["k"] = k + 1
            else:
                wstate["prev"] = None
            if prev is not None:
                pk, plt = prev
                name, kc, c0, cw = pieces[pk]
                si, so = wfin[pk % 2], wfout[pk % 2]
                P.wait(plt, so.busy)
                so.busy = []
                ct = P.mark(P.e.tensor_copy(out=so.t[:, 0:cw], in_=si.t[:, 0:cw]))
                si.busy = [ct]
                dst = d["Wb_" + name[2:]]
                if name == "w_down":
                    fb_, kcl = kc // 4, kc % 4
                    dap = dst[fb_, :, kcl, c0:c0 + cw]
                    sap = so.t[:, 0:cw]
                else:
                    nb0, nbn = c0 // 512, cw // 512
                    dap = dst[nb0:nb0 + nbn, :, kc, :].rearrange("nb p c -> p nb c")
                    sap = so.t[:, 0:cw].rearrange("p (nb c) -> p nb c", c=512)
                kx.store(so, dap, sap, ct)

        def wcast_pending():
            return wstate["k"] < len(pieces) or wstate["prev"] is not None

        def load_kv(hi):
            j, h = heads[hi]
            T = cfg.jobs[j][0]
            ks, vs = KTs[hi % 2], VHs[hi % 2]
            NQ = 4 if T >= 2048 else 1
            w = T // NQ
            for i in range(NQ):
                kx.load(ks, ks.t[:, i * w:(i + 1) * w], d[f"KT{j}"][h, :, i * w:(i + 1) * w], first=(i == 0))
            for i in range(NQ):
                kx.load(vs, vs.t[:, i * (w // 128):(i + 1) * (w // 128), :],
                        d[f"VH{j}"][h, :, i * (w // 128):(i + 1) * (w // 128), :], first=(i == 0))
            kv_tok[hi] = (ks.tok(), vs.tok())

        def load_q(hi, qc):
            j, h = heads[hi]
            s = QTs[cnt["q"] % 2]
            cnt["q"] += 1
            t = kx.load(s, s.t[:], d[f"QT{j}"][h, :, qc * 512:(qc + 1) * 512])
            q_tok[(hi, qc)] = (s, t)

        tok_qk = {}
        tok_exp = {}
        pend = []
        ep_free = dict(o=None)
        Obanks_free = [None]

        def emit_qk(n):
            hi, j, h, qc, kt = steps[n]
            ks = KTs[hi % 2]
            s, tq = q_tok[(hi, qc)]
            bk = psS[n % 2]
            PE.wait(kv_tok[hi][0], tq, bk.free)
            PE.e.matmul(bk.ap[:, 0:512], lhsT=ks.t[0:64, kt * 128:(kt + 1) * 128], rhs=s.t[0:64, :],
                        start=True, stop=True)
            tok_qk[n] = PE.mark(PE.e.matmul(bk.ap[:, 512:1024], lhsT=ks.t[64:128, kt * 128:(kt + 1) * 128],
                                            rhs=s.t[64:128, :], start=True, stop=True))
            T = cfg.jobs[j][0]
            if kt == T // 128 - 1:
                s.busy = [tok_qk[n]]
                if qc == cfg.jobs[j][1] // 512 - 1:
                    ks.busy = [tok_qk[n]]

        def emit_exp(n):
            bk = psS[n % 2]
            A.wait(tok_qk.pop(n), PT_pe[n % NPT], PT_dve[n % NPT])
            tok_exp[n] = A.mark(A.e.activation(out=PT[n % NPT][:], in_=bk.ap, func=AF.Exp, scale=scale))
            bk.free = tok_exp[n]

        def emit_av(n):
            hi, j, h, qc, kt = steps[n]
            T, OWN = cfg.jobs[j]
            NT = T // 128
            vs = VHs[hi % 2]
            pt = PT[n % NPT]
            PE.wait(tok_exp[n], kv_tok[hi][1], Obanks_free[0] if kt == 0 else None)
            st_, sp_ = (kt == 0), (kt == NT - 1)
            PE.e.matmul(psO1.ap, lhsT=vs.t[:, kt, :], rhs=pt[:, 0:512], start=st_, stop=sp_)
            tp = PE.mark(PE.e.matmul(psO2.ap, lhsT=vs.t[:, kt, :], rhs=pt[:, 512:1024], start=st_, stop=sp_))
            PT_pe[n % NPT] = tp
            if sp_:
                while pend:
                    pend.pop(0)()
            tk0 = epilogue_s0(tp) if sp_ else None
            if kt % 2 == 1:
                g = cnt["grp"] % 2
                pi = (kt // 2) % 2
                V.wait(tok_exp[n - 1], tok_exp[n], tmp_tok[pi])
                t1 = V.mark(V.e.tensor_tensor(out=tmpb[pi][:], in0=PT[(n - 1) % NPT][:], in1=pt[:], op=ALU.add))
                PT_dve[(n - 1) % NPT] = t1
                PT_dve[n % NPT] = t1
                pair_tok[pi] = t1
                tok_exp.pop(n - 1, None)
                tok_exp.pop(n, None)
                if kt % 4 == 3:
                    V.wait(pair_tok[0], pair_tok[1], tmp_tok[2])
                    t3 = V.mark(V.e.tensor_tensor(out=tmpb[2][:], in0=tmpb[0][:], in1=tmpb[1][:], op=ALU.add))
                    tmp_tok[0] = t3
                    tmp_tok[1] = t3
                    V.wait(t3, acc_tok[g], acc_free[g] if kt == 3 else None)
                    if kt == 3:
                        t2 = V.mark(V.e.tensor_copy(out=acc[g][:], in_=tmpb[2][:]))
                    else:
                        t2 = V.mark(V.e.tensor_tensor(out=acc[g][:], in0=acc[g][:], in1=tmpb[2][:], op=ALU.add))
                    acc_tok[g] = t2
                    tmp_tok[2] = t2
            if sp_:
                if qc == OWN // 512 - 1:
                    vs.busy = [tp]
                g = cnt["grp"] % 2
                cnt["grp"] += 1
                epilogue(j, h, qc, tk0, g, acc_tok[g])

        def epilogue_s0(tp):
            tk = {}
            V.wait(tp, ep_free["o"])
            tk["e1"] = V.mark(V.e.tensor_copy(out=o1[:], in_=psO1.ap))
            A.wait(tp, ep_free["o"])
            tk["e2"] = A.mark(A.e.activation(out=o2[:], in_=psO2.ap, func=AF.Copy))
            Obanks_free[0] = [tk["e1"], tk["e2"]]
            return tk

        def epilogue(j, h, qc, tk, g, tacc):
            def k1():
                PE.wait(tacc, psX.free)
                tk["r1"] = PE.mark(PE.e.matmul(psX.ap, lhsT=c["ones_f"][:], rhs=acc[g][:, 0:512], start=True, stop=True))

            def k2():
                V.wait(tk["r1"])
                tk["rc1"] = V.mark(V.e.reciprocal(out=sq[:], in_=psX.ap))
                psX.free = tk["rc1"]
                V.wait(tk["rc1"], tk["e1"])
                tk["t"] = V.mark(V.e.tensor_tensor(out=tt_[:], in0=o1[:], in1=sq[:], op=ALU.mult))

            def k3():
                PE.wait(psX.free)
                tk["r2"] = PE.mark(PE.e.matmul(psX.ap, lhsT=c["ones_f"][:], rhs=acc[g][:, 512:1024], start=True, stop=True))
                acc_free[g] = tk["r2"]

            def k4():
                V.wait(tk["r2"], tk["t"])
                tk["rc2"] = V.mark(V.e.reciprocal(out=sq[:], in_=psX.ap))
                psX.free = tk["rc2"]
                V.wait(tk["rc2"], tk["e2"])
                tk["u"] = V.mark(V.e.tensor_tensor(out=uu[:], in0=o2[:], in1=sq[:], op=ALU.mult))
                V.wait(tk["u"], tk["t"])
                tk["o"] = V.mark(V.e.scalar_tensor_tensor(out=oo[:], in0=uu[:], scalar=c["nlam"][:, 0:1],
                                                          in1=tt_[:], op0=ALU.mult, op1=ALU.add))
                V.wait(tk["o"])
                tk["sq"] = V.mark(V.e.tensor_tensor(out=sq[:], in0=oo[:], in1=oo[:], op=ALU.mult))

            def k5():
                PE.wait(tk["sq"], psX.free)
                tk["ss"] = PE.mark(PE.e.matmul(psX.ap, lhsT=c["ones_f"][:], rhs=sq[:], start=True, stop=True))

            def k6():
                A.wait(tk["ss"])
                tk["ln"] = A.mark(A.e.activation(out=rt[:], in_=psX.ap, func=AF.Ln, scale=1.0 / 128,
                                                 bias=c["eps_sub"][:]))
                psX.free = tk["ln"]
                A.wait(tk["ln"])
                tk["rs"] = A.mark(A.e.activation(out=rt[:], in_=rt[:], func=AF.Exp, scale=-0.5))

            def k7():
                s = ost[cnt["ost"] % 2]
                cnt["ost"] += 1
                V.wait(tk["rs"], s.busy)
                s.busy = []
                tk["on"] = V.mark(V.e.scalar_tensor_tensor(out=s.t[:], in0=oo[:], scalar=c["gsub"][:, 0:1],
                                                           in1=rt[:], op0=ALU.mult, op1=ALU.mult))
                ep_free["o"] = tk["on"]
                kx.store(s, d[f"MIXT{j}"][8 + h, :, qc * 512:(qc + 1) * 512], s.t[:], tk["on"])

            pend.extend([k1, k2, k3, k4, k5, k6, k7])

        load_kv(0)
        load_q(0, 0)

        def prefetch_for(n):
            hi, j, h, qc, kt = steps[n]
            if kt == 0:
                nq = cfg.jobs[j][1] // 512
                if qc + 1 < nq:
                    load_q(hi, qc + 1)
                elif hi + 1 < len(heads):
                    load_q(hi + 1, 0)
                if qc == 0 and hi + 1 < len(heads):
                    load_kv(hi + 1)

        prefetch_for(0)
        emit_qk(0)
        if N > 1:
            emit_qk(1)
        for n in range(N):
            hi, j, h, qc, kt = steps[n]
            if n > 0:
                prefetch_for(n)
            emit_exp(n)
            if n + 2 < N:
                emit_qk(n + 2)
            emit_av(n)
            if pend and kt >= 1 and kt % 2 == 0:
                pend.pop(0)()
            if n % 4 == 1 and wcast_pending():
                wcast_tick()
        while wcast_pending():
            wcast_tick()
        while pend:
            pend.pop(0)()
        kx.barrier()


def phase3(kx):
    nc, cfg, d = kx.nc, kx.cfg, kx.dram
    D, KC, DFF, MC, NFB = cfg.D, cfg.KC, cfg.DFF, cfg.MC, cfg.NFB
    PE, A, V = kx.PE, kx.ACT, kx.DVE
    c = kx.c
    NCB = D // 512
    mscale = float(cfg.MEMHD) ** -0.5
    es = contextlib.ExitStack()
    with es:
        NB3 = 6
        banks = [Bank(es.enter_context(nc.psum_tensor(f"p3s{i}", [128, 512], F32))[:]) for i in range(NB3)]
        pbt = [es.enter_context(nc.psum_tensor(f"p3b{i}", [128, 1024], BF16)) for i in range(2)]
        pb = [Bank(pbt[0][:, 0:512]), Bank(pbt[1][:, 0:512])]
        ncx = NormCtx(kx, es, pb, nx=0)
        h = Slot(nc, es, "p3h", [128, 4, D], F32)
        fa = Slot(nc, es, "p3fa", [128, 16, 512], BF16)
        fbs = Slot(nc, es, "p3fb", [128, 16, 512], BF16)
        AB = [fa, fbs]
        fr = {id(fa): None, id(fbs): None}
        pre = dict(tok=None)
        KmT = sbuf(kx, "p3KmT", [128, KC, 256], BF16, es)
        Vm = sbuf(kx, "p3Vm", [128, 2, D], BF16, es)
        PTm = [sbuf(kx, f"p3PTm{i}", [128, 2, 512], BF16, es) for i in range(2)]
        Rm = [sbuf(kx, f"p3Rm{i}", [128, 512], F32, es) for i in range(2)]
        actT = [sbuf(kx, f"p3act{i}", [128, 4, 512], BF16, es) for i in range(2)]
        sg = [sbuf(kx, f"p3sg{i}", [128, 512], F32, es) for i in range(2)]
        NW = 4
        wr = [Slot(nc, es, f"p3w{i}", [128, 8192], BF16) for i in range(NW)]
        gfin = Slot(nc, es, "p3gfin", [128, D], F32)
        ystat = sbuf(kx, "p3ystat", [128, 8], F32, es)
        yrstd = sbuf(kx, "p3yrstd", [128, 8], F32, es)
        tg = kx.load(gfin, gfin.t[:], d["g_final"])
        ysems = [kx.ysem, kx.ysem2]
        ys_tok = [None, None]
        st = dict(bi=0, wi=0, pi=0, ai=0, si=0, yi=0)
        htok = {}
        PTm_free = [None, None]
        Rm_free = [None, None]
        act_free = [None, None]
        sg_free = [None, None]

        def next_bank():
            b = banks[st["bi"] % NB3]
            st["bi"] += 1
            return b

        def lw(name, idx, kc_n, cols):
            s = wr[st["wi"] % NW]
            st["wi"] += 1
            view = s.t[:, 0:kc_n * cols].rearrange("p (k c) -> p k c", c=cols)
            t = kx.load(s, view, d[name][idx])
            return s, t, view

        def h_tiles(ntt, extra_toks=None):
            def mk(tt):
                def f():
                    toks = [htok.get((tt, cb)) for cb in range(NCB)]
                    if extra_toks:
                        toks = toks + list(extra_toks)

                    def rel(tk):
                        for cb in range(NCB):
                            htok[(tt, cb)] = list(tk)
                    return h.t[:, tt, :], toks, rel
                return f
            return [mk(tt) for tt in range(ntt)]

        def evac_copy(dst, bk, tp, extra=None):
            E = kx.alt()
            E.wait(tp, extra)
            if E is A:
                ins = E.e.activation(out=dst, in_=bk.ap if not isinstance(bk, tuple) else bk[0], func=AF.Copy)
            else:
                ins = E.e.tensor_copy(out=dst, in_=bk.ap if not isinstance(bk, tuple) else bk[0])
            te = E.mark(ins)
            return te

        def proj_tm_add(src, src_toks, nk, wname):
            last = None
            for cb in range(NCB):
                ws, wt, wv = lw(wname, cb, nk, 512)
                for tt in range(4):
                    bk = next_bank()
                    PE.wait(wt, src_toks, bk.free)
                    for kc in range(nk):
                        ins = PE.e.matmul(bk.ap, lhsT=src[:, kc, tt * 128:(tt + 1) * 128], rhs=wv[:, kc, :],
                                          start=(kc == 0), stop=(kc == nk - 1))
                    tp = PE.mark(ins)
                    last = tp
                    V.wait(tp, htok.get((tt, cb)))
                    reg = h.t[:, tt, cb * 512:(cb + 1) * 512]
                    ta = V.mark(V.e.tensor_tensor(out=reg, in0=bk.ap, in1=reg, op=ALU.add))
                    bk.free = ta
                    htok[(tt, cb)] = [ta]
                ws.busy = [last]
            return last

        for j, (T, OWN) in enumerate(cfg.jobs):
            kx.load(h, h.t[:, 0, :], d[f"mem{j}"][0:128, :])
            tm = kx.load(h, h.t[:, 1, :], d[f"mem{j}"][128:256, :], first=False)
            htok.clear()
            A_, B_ = AB
            fb = B_.t
            mtoks = make_xnT(kx, ncx, h_tiles(2, [tm]), c["g_mem"], fb, fr[id(B_)], ntt=2)
            fr[id(B_)] = None
            last = None
            kdone = []
            for m in range(KC):
                if m % 4 == 0:
                    ws, wt, wv = lw("Wb_mkv", m // 4, KC, 512)
                bk = next_bank()
                PE.wait(wt, mtoks, bk.free)
                for kc in range(KC):
                    ins = PE.e.matmul(bk.ap[:, 0:256], lhsT=wv[:, kc, (m % 4) * 128:(m % 4 + 1) * 128],
                                      rhs=fb[:, kc, 0:256], start=(kc == 0), stop=(kc == KC - 1))
                tp = PE.mark(ins)
                last = tp
                te = evac_copy(KmT[:, m, :], (bk.ap[:, 0:256],), tp, fr[id(A_)] if m == 0 else None)
                bk.free = te
                kdone.append(te)
                if m % 4 == 3 or m == KC - 1:
                    ws.busy = [last]
            for cb in range(NCB):
                ws, wt, wv = lw("Wb_mkv", NCB + cb, KC, 512)
                for tt in range(2):
                    bk = next_bank()
                    PE.wait(wt, mtoks, bk.free)
                    for kc in range(KC):
                        ins = PE.e.matmul(bk.ap, lhsT=fb[:, kc, tt * 128:(tt + 1) * 128], rhs=wv[:, kc, :],
                                          start=(kc == 0), stop=(kc == KC - 1))
                    tp = PE.mark(ins)
                    last = tp
                    te = evac_copy(Vm[:, tt, cb * 512:(cb + 1) * 512], bk, tp)
                    bk.free = te
                    kdone.append(te)
                ws.busy = [last]
            fr[id(B_)] = [last]
            mem_ready = kdone[-2:] + kdone[KC - 2:KC]
            for b in range(OWN // 512):
                t0 = b * 512
                A_, B_ = AB
                fa_t, fb = A_.t, B_.t
                if pre["tok"] is not None:
                    tmix = pre["tok"]
                    pre["tok"] = None
                else:
                    A_.busy = [fr[id(A_)]] if fr[id(A_)] else []
                    tmix = kx.load(A_, A_.t[:], d[f"MIXT{j}"][:, :, t0:t0 + 512].rearrange("c p q -> p c q"))
                h.busy = h.busy + [t for v_ in htok.values() if v_ for t in v_]
                for tt in range(4):
                    tx = kx.load(h, h.t[:, tt, :], d[f"x{j}"][t0 + tt * 128:t0 + (tt + 1) * 128, :], first=(tt == 0),
                                 eng=kx.ACT)
                htok.clear()
                for tt in range(4):
                    for cb in range(NCB):
                        htok[(tt, cb)] = [tx]
                last = proj_tm_add(fa_t, [tmix], 16, "Wb_out")
                fr[id(A_)] = [last]
                n1 = make_xnT(kx, ncx, h_tiles(4), c["g_xattn"], fb, fr[id(B_)])
                qdone = []
                for m in range(KC):
                    if m % 4 == 0:
                        ws, wt, wv = lw("Wb_mq", m // 4, KC, 512)
                    bk = next_bank()
                    PE.wait(wt, n1, bk.free)
                    for kc in range(KC):
                        ins = PE.e.matmul(bk.ap, lhsT=wv[:, kc, (m % 4) * 128:(m % 4 + 1) * 128], rhs=fb[:, kc, :],
                                          start=(kc == 0), stop=(kc == KC - 1))
                    tp = PE.mark(ins)
                    last = tp
                    te = evac_copy(fa_t[:, m, :], bk, tp, fr[id(A_)] if m == 0 else None)
                    bk.free = te
                    qdone.append(te)
                    if m % 4 == 3 or m == KC - 1:
                        ws.busy = [last]
                fr[id(B_)] = [last]
                qtoks = qdone[-2:]
                lastS = None
                for hm in range(4):
                    pi = st["pi"] % 2
                    st["pi"] += 1
                    pt = PTm[pi]
                    texp = []
                    for kc2 in range(2):
                        bk = next_bank()
                        PE.wait(qtoks, mem_ready, bk.free)
                        for dc in range(MC):
                            ins = PE.e.matmul(bk.ap, lhsT=KmT[:, hm * MC + dc, kc2 * 128:(kc2 + 1) * 128],
                                              rhs=fa_t[:, hm * MC + dc, :], start=(dc == 0), stop=(dc == MC - 1))
                        tp = PE.mark(ins)
                        lastS = tp
                        A.wait(tp, PTm_free[pi])
                        te = A.mark(A.e.activation(out=pt[:, kc2, :], in_=bk.ap, func=AF.Exp, scale=mscale))
                        bk.free = te
                        texp.append(te)
                    PTm_free[pi] = None
                    bks = next_bank()
                    PE.wait(texp, bks.free)
                    PE.e.matmul(bks.ap, lhsT=c["ones_bf"][:], rhs=pt[:, 0, :], start=True, stop=False)
                    tps = PE.mark(PE.e.matmul(bks.ap, lhsT=c["ones_bf"][:], rhs=pt[:, 1, :], start=False, stop=True))
                    V.wait(tps, Rm_free[pi])
                    tr = V.mark(V.e.reciprocal(out=Rm[pi][:], in_=bks.ap))
                    bks.free = tr
                    lastO = None
                    for dvc in range(MC):
                        ch = hm * MC + dvc
                        bk = next_bank()
                        PE.wait(bk.free)
                        PE.e.matmul(bk.ap, lhsT=Vm[:, 0, ch * 128:(ch + 1) * 128], rhs=pt[:, 0, :], start=True, stop=False)
                        tp = PE.mark(PE.e.matmul(bk.ap, lhsT=Vm[:, 1, ch * 128:(ch + 1) * 128], rhs=pt[:, 1, :],
                                                 start=False, stop=True))
                        lastO = tp
                        V.wait(tp, tr, fr[id(B_)])
                        to = V.mark(V.e.tensor_tensor(out=fb[:, ch, :], in0=bk.ap, in1=Rm[pi][:], op=ALU.mult))
                        bk.free = to
                    PTm_free[pi] = lastO
                    Rm_free[pi] = to
                fr[id(B_)] = None
                fr[id(A_)] = [lastS]
                om_toks = [to]
                last = proj_tm_add(fb, om_toks, KC, "Wb_mo")
                fr[id(B_)] = [last]
                n2 = make_xnT(kx, ncx, h_tiles(4), c["g_ffn"], fa_t, fr[id(A_)])
                if b + 1 < OWN // 512:
                    B_.busy = [fr[id(B_)]] if fr[id(B_)] else []
                    pre["tok"] = kx.load(B_, B_.t[:], d[f"MIXT{j}"][:, :, t0 + 512:t0 + 1024].rearrange("c p q -> p c q"))
                    fr[id(B_)] = None
                lastG = None
                for fblk in range(NFB):
                    wg, tg_, vg = lw("Wb_gu", fblk, KC, 512)
                    wu, tu_, vu = lw("Wb_gu", NFB + fblk, KC, 512)
                    wd, td_, vd = lw("Wb_down", fblk, 4, D)
                    ai = st["ai"] % 2
                    st["ai"] += 1
                    at = actT[ai]
                    tacts = []
                    for cc in range(4):
                        bg = next_bank()
                        PE.wait(tg_, n2, bg.free)
                        for kc in range(KC):
                            ins = PE.e.matmul(bg.ap, lhsT=vg[:, kc, cc * 128:(cc + 1) * 128], rhs=fa_t[:, kc, :],
                                              start=(kc == 0), stop=(kc == KC - 1))
                        tpg = PE.mark(ins)
                        bu = next_bank()
                        PE.wait(tu_, bu.free)
                        for kc in range(KC):
                            ins = PE.e.matmul(bu.ap, lhsT=vu[:, kc, cc * 128:(cc + 1) * 128], rhs=fa_t[:, kc, :],
                                              start=(kc == 0), stop=(kc == KC - 1))
                        tpu = PE.mark(ins)
                        lastG = tpu
                        si = st["si"] % 2
                        st["si"] += 1
                        A.wait(tpg, sg_free[si])
                        tsg = A.mark(A.e.activation(out=sg[si][:], in_=bg.ap, func=AF.Silu))
                        bg.free = tsg
                        V.wait(tsg, tpu, act_free[ai] if cc == 0 else None)
                        tact = V.mark(V.e.tensor_tensor(out=at[:, cc, :], in0=bu.ap, in1=sg[si][:], op=ALU.mult))
                        bu.free = tact
                        sg_free[si] = tact
                        tacts.append(tact)
                    wg.busy = [lastG]
                    wu.busy = [lastG]
                    lastD = None
                    for cb in range(NCB):
                        for tt in range(4):
                            bk = next_bank()
                            PE.wait(td_, tacts[-1], bk.free)
                            for cc in range(4):
                                ins = PE.e.matmul(bk.ap, lhsT=at[:, cc, tt * 128:(tt + 1) * 128],
                                                  rhs=vd[:, cc, cb * 512:(cb + 1) * 512], start=(cc == 0), stop=(cc == 3))
                            tp = PE.mark(ins)
                            lastD = tp
                            V.wait(tp, htok.get((tt, cb)))
                            reg = h.t[:, tt, cb * 512:(cb + 1) * 512]
                            ta = V.mark(V.e.tensor_tensor(out=reg, in0=bk.ap, in1=reg, op=ALU.add))
                            bk.free = ta
                            htok[(tt, cb)] = [ta]
                    wd.busy = [lastD]
                    act_free[ai] = lastD
                fr[id(A_)] = [lastG]
                if pre["tok"] is not None:
                    AB.reverse()
                ystage = ncx.hns[0][:].rearrange("p a d -> p (a d)").bitcast(F32)
                stoks = []
                rdtoks = []
                for tt in range(4):
                    col = st["yi"] % 8
                    st["yi"] += 1
                    toks = [htok.get((tt, cb)) for cb in range(NCB)]
                    A.wait(ncx.junk_tok)
                    ta, tc = rms_rstd(kx, h.t[:, tt, :], ystat[:, col:col + 1], yrstd[:, col:col + 1], ncx.junk[:],
                                      D, toks)
                    ncx.junk_tok = ta
                    ys = ystage[:, (tt % 2) * D:(tt % 2 + 1) * D]
                    V.wait(ta, tc, tg, ncx.hn_free[0], ys_tok[tt % 2])
                    ty = V.mark(V.e.scalar_tensor_tensor(out=ys, in0=h.t[:, tt, :],
                                                         scalar=yrstd[:, col:col + 1], in1=gfin.t[:],
                                                         op0=ALU.mult, op1=ALU.mult))
                    rdtoks += [ta, ty]
                    kx.POOL.wait(ty)
                    kx.POOL.e.dma_start(out=d[f"y{j}"][t0 + tt * 128:t0 + (tt + 1) * 128, :],
                                        in_=ys).then_inc(ysems[tt % 2].sem, 16)
                    ysems[tt % 2].cnt += 16
                    stk = (ysems[tt % 2].sem, ysems[tt % 2].cnt)
                    ys_tok[tt % 2] = stk
                    kx.store_toks.append(stk)
                    stoks.append(stk)
                ncx.hn_free[0] = [ncx.hn_free[0], stoks[-1], stoks[-2]]
                htok.clear()
                h.busy = rdtoks
        kx.barrier()


def rope_tables(pos):
    inv = (1.0 / (10000.0 ** (np.arange(0, 64, 2, dtype=np.float32) / np.float32(64)))).astype(np.float32)
    ang = pos.astype(np.float32)[:, None] * inv[None, :]
    ang = np.concatenate([ang, ang, ang, ang], axis=-1)
    return np.ascontiguousarray(np.cos(ang).T.astype(np.float32)), \
        np.ascontiguousarray(np.sin(ang).T.astype(np.float32))


def na_table(rpb, j, nt):
    R = 2 * nt
    q = np.arange(128)
    r = 2 * j + q // 64
    cq = q % 64
    rs = np.clip(r - 4, 0, R - 8)
    cst = np.clip(cq - 8, 0, 64 - 16)
    tab = np.full((128, 8, 640), NEG, dtype=np.float32)
    for s in range(5):
        kt = j - 2 + s
        if j == 0 and s == 0:
            kt = 3
        if j == nt - 1 and s == 4:
            kt = nt - 4
        if kt < 0 or kt >= nt:
            continue
        k = np.arange(128)
        kr = 2 * kt + k // 64
        kcn = k % 64
        valid = ((kr[None, :] >= rs[:, None]) & (kr[None, :] < rs[:, None] + 8) &
                 (kcn[None, :] >= cst[:, None]) & (kcn[None, :] < cst[:, None] + 16))
        dr = np.clip(kr[None, :] - r[:, None] + 7, 0, 14)
        dc = np.clip(kcn[None, :] - cq[:, None] + 15, 0, 30)
        g = rpb[:, dr, dc]
        g = np.transpose(g, (1, 0, 2))
        blk = tab[:, :, s * 128:(s + 1) * 128]
        blk[...] = np.where(valid[:, None, :], g, blk)
    return tab


def rot_matrix():
    Rm = np.zeros((128, 128), dtype=np.float32)
    for p in range(128):
        if (p % 64) < 32:
            Rm[p + 32, p] = -1.0
        else:
            Rm[p - 32, p] = 1.0
    return Rm.astype(ml_dtypes.bfloat16)


def prepare_core_inputs(cfg, core, nparts, xs, mems, shared):
    m = dict(shared)
    for j, (T, OWN) in enumerate(cfg.jobs):
        part, npart = nparts[j]
        x = xs[j]
        a = part * OWN
        own = np.arange(a, a + OWN)
        rest = np.concatenate([np.arange(0, a), np.arange(a + OWN, T)])
        perm = np.concatenate([own, rest])
        m[f"x{j}"] = np.ascontiguousarray(x[perm])
        cosT, sinT = rope_tables(perm)
        m[f"cos{j}"] = cosT
        m[f"sin{j}"] = sinT
        nt = T // 128
        ta, tb = a // 128, (a + OWN) // 128
        xh = np.zeros((512, cfg.D), dtype=np.float32)

        def tile(k):
            return x[k * 128:(k + 1) * 128]
        if ta == 0:
            xh[0:128] = tile(3)
        else:
            xh[0:128] = tile(ta - 2)
            xh[128:256] = tile(ta - 1)
        if tb == nt:
            xh[384:512] = tile(nt - 4)
        else:
            xh[256:384] = tile(tb)
            xh[384:512] = tile(tb + 1)
        m[f"xh{j}"] = xh
        m[f"mem{j}"] = np.ascontiguousarray(mems[j])
        rpb = shared["_rpb"]
        m[f"nab{j}"] = np.stack([na_table(rpb, g, nt) for g in (ta, ta + 1, tb - 2, tb - 1)])
    del m["_rpb"]
    return m


def shared_inputs(cfg, inp):
    KC = cfg.KC
    sh = {}
    sh["w_in"] = np.ascontiguousarray(inp["w_in"][0])
    sh["w_out"] = np.ascontiguousarray(inp["w_out"][0])
    sh["w_mq"] = np.ascontiguousarray(inp["w_mq"][0])
    sh["w_mkv"] = np.ascontiguousarray(inp["w_mkv"][0])
    sh["w_mo"] = np.ascontiguousarray(inp["w_mo"][0])
    sh["w_gu"] = np.ascontiguousarray(inp["w_gate_up"][0])
    sh["w_down"] = np.ascontiguousarray(inp["w_down"][0])
    for g, src in (("g_mix", "g_mix"), ("g_xattn", "g_xattn"), ("g_mem", "g_mem"), ("g_ffn", "g_ffn")):
        sh[g] = np.ascontiguousarray(np.asarray(inp[src][0], dtype=np.float32).reshape(KC, 128).T)
    sh["g_final"] = np.ascontiguousarray(np.broadcast_to(np.asarray(inp["g_final"], dtype=np.float32)[None, :], (128, cfg.D)))
    sh["g_subln"] = np.ascontiguousarray(np.asarray(inp["g_subln"][0], dtype=np.float32).reshape(128, 1))
    lv = np.stack([inp["lam_q1"][0], inp["lam_k1"][0], inp["lam_q2"][0], inp["lam_k2"][0]]).astype(np.float32)
    sh["lamv"] = np.ascontiguousarray(np.broadcast_to(lv[None], (128, 4, 64)))
    sh["ident"] = np.eye(128, dtype=np.float32).astype(ml_dtypes.bfloat16)
    sh["rotm"] = rot_matrix()
    rpb = np.asarray(inp["rpb"][0], dtype=np.float32)
    sh["_rpb"] = rpb
    sh["nabi"] = na_table(rpb, 8, 32)
    return sh


_PROGRAM_CACHE = {}


def kernel(**inputs):
    cfg = Cfg()
    inp = {k: np.asarray(v) for k, v in inputs.items()}
    sh = shared_inputs(cfg, inp)
    in_maps = []
    for c in range(8):
        xs = [inp["x_prompt"][c // 4], inp["x_sample"][c // 2]]
        mems = [inp["mem_prompt"][c // 4], inp["mem_sample"][c // 2]]
        in_maps.append(prepare_core_inputs(cfg, c, [(c % 4, 4), (c % 2, 2)], xs, mems, sh))
    nc = build_program(cfg)
    res = run_bass_kernel_spmd(nc, in_maps, core_ids=list(range(8)))
    yp = np.zeros((2, 16384, cfg.D), dtype=np.float32)
    ysm = np.zeros((4, 4096, cfg.D), dtype=np.float32)
    for c in range(8):
        r = res.results[c]
        yp[c // 4, (c % 4) * 4096:(c % 4 + 1) * 4096] = r["y0"]
        ysm[c // 2, (c % 2) * 2048:(c % 2 + 1) * 2048] = r["y1"]
    return (yp, ysm)
```
